# Optimizing a Trainium2 kernel written in Bass

```python
import math
import jax
import jax.numpy as jnp
from jax import lax
import numpy as np

D_MODEL = 1024
BATCH = 8
SEQ = 4096
DEPTH = 2

HEAD_DIM = 64
RET_HEADS = 4
MOBA_HEADS = 8
GDN_HEADS = 4
RET_W = RET_HEADS * HEAD_DIM
MOBA_W = MOBA_HEADS * HEAD_DIM
GDN_W = GDN_HEADS * HEAD_DIM
D_MIX = RET_W + MOBA_W + GDN_W
IN_COLS = 4 * RET_W + 3 * MOBA_W + 4 * GDN_W + 2 * GDN_HEADS
RET_CHUNK = 64
ROPE_BASE = 10000.0
MOBA_BLOCK = 256
MOBA_TOPK = 3
MOBA_Q_CHUNK = 64
REL_BUCKETS = 32
REL_MAX_DIST = 128
GDN_CHUNK = 64
GDN_CONV = 4
MEM_LEN = 256
CROSS_HEADS = 4
CROSS_HEAD_DIM = 128
CROSS_W = CROSS_HEADS * CROSS_HEAD_DIM
D_FF = 2816
FFN_CONV = 3
EPS = 1e-6

kernel_name = 'hybrid_retention_moba_gdn_block'


def rms_norm(x, w):
    xf = x.astype(jnp.float32)
    y = xf * lax.rsqrt(jnp.mean(xf * xf, axis=-1, keepdims=True) + EPS)
    return (y * w.astype(jnp.float32)).astype(x.dtype)


def l2_norm(x):
    xf = x.astype(jnp.float32)
    return xf * lax.rsqrt(jnp.sum(xf * xf, axis=-1, keepdims=True) + EPS)


def causal_dwconv(x, w):
    k, c = w.shape
    return lax.conv_general_dilated(
        x, w[:, None, :].astype(x.dtype), window_strides=(1,), padding=[(k - 1, 0)],
        dimension_numbers=('NWC', 'WIO', 'NWC'), feature_group_count=c)


def to_heads(t, n_heads):
    b, s, _ = t.shape
    return t.reshape(b, s, n_heads, -1).transpose(0, 2, 1, 3)


def from_heads(t):
    b, h, s, d = t.shape
    return t.transpose(0, 2, 1, 3).reshape(b, s, h * d)


def rotary(x, pos):
    half = x.shape[-1] // 2
    inv_freq = ROPE_BASE ** (-jnp.arange(half, dtype=jnp.float32) / half)
    ang = pos[:, None] * inv_freq[None, :]
    cos, sin = jnp.cos(ang), jnp.sin(ang)
    xf = x.astype(jnp.float32)
    x1, x2 = xf[..., :half], xf[..., half:]
    return jnp.concatenate([x1 * cos - x2 * sin, x2 * cos + x1 * sin], axis=-1).astype(x.dtype)


def t5_bucket(rel):
    n = jnp.maximum(rel, 0)
    exact = REL_BUCKETS // 2
    nf = jnp.maximum(n, exact).astype(jnp.float32)
    large = exact + (jnp.log(nf / exact) / math.log(REL_MAX_DIST / exact)
                     * (REL_BUCKETS - exact)).astype(jnp.int32)
    return jnp.where(n < exact, n, jnp.minimum(large, REL_BUCKETS - 1))


def retention(q, k, v):
    b, h, s, dh = q.shape
    c = RET_CHUNK
    n = s // c
    log_gamma = jnp.log1p(-jnp.exp2(-5.0 - jnp.arange(h, dtype=jnp.float32)))
    idx = jnp.arange(c, dtype=jnp.float32)
    diff = idx[:, None] - idx[None, :]
    d_intra = jnp.where(diff >= 0, jnp.exp(log_gamma[:, None, None] * jnp.maximum(diff, 0.0)), 0.0)
    xi = jnp.exp(log_gamma[:, None] * (idx + 1.0))
    zeta = jnp.exp(log_gamma[:, None] * (c - 1.0 - idx))
    g_chunk = jnp.exp(log_gamma * c)
    qc = q.astype(jnp.float32).reshape(b, h, n, c, dh)
    kc = (k.astype(jnp.float32) * dh ** -0.5).reshape(b, h, n, c, dh)
    vc = v.astype(jnp.float32).reshape(b, h, n, c, dh)
    scores = jnp.einsum('bhncd,bhnjd->bhncj', qc, kc) * d_intra[None, :, None]
    inner = jnp.einsum('bhncj,bhnje->bhnce', scores, vc)
    kv_inc = jnp.einsum('bhncd,bhnce->bhnde', kc * zeta[None, :, None, :, None], vc)

    def step(state, inc):
        return g_chunk[None, :, None, None] * state + inc, state

    _, prev = lax.scan(step, jnp.zeros((b, h, dh, dh), jnp.float32), jnp.moveaxis(kv_inc, 2, 0))
    prev = jnp.moveaxis(prev, 0, 2)
    cross = jnp.einsum('bhncd,bhnde->bhnce', qc, prev) * xi[None, :, None, :, None]
    return (inner + cross).reshape(b, h, s, dh)


def moba_attention(q, k, v, rel_bias):
    b, h, s, dh = q.shape
    nb = -(-s // MOBA_BLOCK)
    s_pad = nb * MOBA_BLOCK
    pad = ((0, 0), (0, 0), (0, s_pad - s), (0, 0))
    q, k, v = jnp.pad(q, pad), jnp.pad(k, pad), jnp.pad(v, pad)
    kb = k.reshape(b, h, nb, MOBA_BLOCK, dh)
    vb = v.reshape(b, h, nb, MOBA_BLOCK, dh)
    k_mean = jnp.mean(kb.astype(jnp.float32), axis=3)
    topk = min(MOBA_TOPK, nb)
    scale = dh ** -0.5
    b_idx = jnp.arange(b)[:, None, None, None]
    h_idx = jnp.arange(h)[None, :, None, None]
    key_off = jnp.arange(MOBA_BLOCK, dtype=jnp.int32)
    blk_ids = jnp.arange(nb, dtype=jnp.int32)

    def query_chunk(ci):
        q0 = ci * MOBA_Q_CHUNK
        qc = lax.dynamic_slice_in_dim(q, q0, MOBA_Q_CHUNK, axis=2).astype(jnp.float32)
        q_pos = q0 + jnp.arange(MOBA_Q_CHUNK, dtype=jnp.int32)
        blk = q0 // MOBA_BLOCK
        gate = jnp.einsum('bhqd,bhnd->bhqn', qc, k_mean)
        gate = jnp.where(blk_ids < blk, gate, -jnp.inf)
        _, sel = lax.top_k(gate, topk)
        valid = sel < blk
        k_sel = kb[b_idx, h_idx, sel].astype(jnp.float32)
        v_sel = vb[b_idx, h_idx, sel].astype(jnp.float32)
        k_own = lax.dynamic_index_in_dim(kb, blk, axis=2, keepdims=False).astype(jnp.float32)
        v_own = lax.dynamic_index_in_dim(vb, blk, axis=2, keepdims=False).astype(jnp.float32)
        sel_pos = sel[..., None] * MOBA_BLOCK + key_off
        own_pos = blk * MOBA_BLOCK + key_off
        bias_sel = rel_bias[h_idx[..., None], t5_bucket(q_pos[None, None, :, None, None] - sel_pos)]
        bias_own = rel_bias[:, t5_bucket(q_pos[:, None] - own_pos[None, :])]
        s_sel = jnp.einsum('bhqd,bhqtkd->bhqtk', qc, k_sel) * scale + bias_sel
        s_sel = jnp.where(valid[..., None], s_sel, -jnp.inf)
        s_own = jnp.einsum('bhqd,bhkd->bhqk', qc, k_own) * scale + bias_own[None]
        s_own = jnp.where(own_pos[None, :] <= q_pos[:, None], s_own, -jnp.inf)
        logits = jnp.concatenate([s_sel.reshape(b, h, MOBA_Q_CHUNK, topk * MOBA_BLOCK), s_own], axis=-1)
        p = jax.nn.softmax(logits, axis=-1)
        p_sel = p[..., :topk * MOBA_BLOCK].reshape(b, h, MOBA_Q_CHUNK, topk, MOBA_BLOCK)
        p_own = p[..., topk * MOBA_BLOCK:]
        return (jnp.einsum('bhqtk,bhqtkd->bhqd', p_sel, v_sel)
                + jnp.einsum('bhqk,bhkd->bhqd', p_own, v_own))

    outs = lax.map(query_chunk, jnp.arange(s // MOBA_Q_CHUNK, dtype=jnp.int32))
    return jnp.moveaxis(outs, 0, 2).reshape(b, h, s, dh).astype(q.dtype)


def gated_delta_rule(q, k, v, log_decay, beta):
    b, h, s, dk = q.shape
    dv = v.shape[-1]
    c = GDN_CHUNK
    n = s // c
    qc = (q.astype(jnp.float32) * dk ** -0.5).reshape(b, h, n, c, dk)
    kc = k.astype(jnp.float32).reshape(b, h, n, c, dk)
    vc = v.astype(jnp.float32).reshape(b, h, n, c, dv)
    bc = beta.astype(jnp.float32).reshape(b, h, n, c)
    gcum = jnp.cumsum(log_decay.astype(jnp.float32).reshape(b, h, n, c), axis=-1)
    incl = jnp.tril(jnp.ones((c, c), dtype=bool))
    strict = jnp.tril(jnp.ones((c, c), dtype=bool), -1)
    decay = jnp.exp(jnp.where(incl, gcum[..., :, None] - gcum[..., None, :], -jnp.inf))
    kk = jnp.einsum('bhncd,bhnjd->bhncj', kc, kc)
    a = jnp.where(strict, bc[..., :, None] * kk * decay, 0.0) + jnp.eye(c, dtype=jnp.float32)
    rhs = jnp.concatenate([vc * bc[..., None], kc * (bc * jnp.exp(gcum))[..., None]], axis=-1)
    sol = lax.linalg.triangular_solve(a, rhs, left_side=True, lower=True, unit_diagonal=True)
    u, w = sol[..., :dv], sol[..., dv:]
    qk = jnp.einsum('bhncd,bhnjd->bhncj', qc, kc) * decay
    q_dec = qc * jnp.exp(gcum)[..., None]
    k_dec = kc * jnp.exp(gcum[..., -1:] - gcum)[..., None]
    g_last = jnp.exp(gcum[..., -1])

    def step(state, xs):
        u_n, w_n, qk_n, q_n, k_n, gl = xs
        v_new = u_n - jnp.einsum('bhcd,bhde->bhce', w_n, state)
        o = jnp.einsum('bhcd,bhde->bhce', q_n, state) + jnp.einsum('bhcj,bhje->bhce', qk_n, v_new)
        state = state * gl[..., None, None] + jnp.einsum('bhcd,bhce->bhde', k_n, v_new)
        return state, o

    xs = tuple(jnp.moveaxis(t, 2, 0) for t in (u, w, qk, q_dec, k_dec, g_last))
    _, o = lax.scan(step, jnp.zeros((b, h, dk, dv), jnp.float32), xs)
    return jnp.moveaxis(o, 0, 2).reshape(b, h, s, dv)


def hybrid_mixer(h, w_in, ret_norm, moba_q_norm, moba_k_norm, gdn_conv, gdn_a_log, gdn_dt_bias,
                 gdn_norm, w_out, rel_bias):
    s = h.shape[1]
    widths = [RET_W] * 4 + [MOBA_W] * 3 + [3 * GDN_W, GDN_W, GDN_HEADS, GDN_HEADS]
    cuts = [int(cc) for cc in np.cumsum(widths)[:-1]]
    rq, rk, rv, rg, mq, mk, mv, g_qkv, g_z, g_b, g_a = jnp.split(h @ w_in, cuts, axis=-1)

    pos = jnp.arange(s, dtype=jnp.float32)
    ret = retention(rotary(to_heads(rq, RET_HEADS), pos), rotary(to_heads(rk, RET_HEADS), pos),
                    to_heads(rv, RET_HEADS))
    y_ret = from_heads(rms_norm(ret, ret_norm)).astype(h.dtype) * jax.nn.silu(rg)

    y_moba = from_heads(moba_attention(rms_norm(to_heads(mq, MOBA_HEADS), moba_q_norm),
                                       rms_norm(to_heads(mk, MOBA_HEADS), moba_k_norm),
                                       to_heads(mv, MOBA_HEADS), rel_bias))

    gq, gk, gv = jnp.split(jax.nn.silu(causal_dwconv(g_qkv, gdn_conv)), 3, axis=-1)
    beta = jax.nn.sigmoid(g_b.astype(jnp.float32)).transpose(0, 2, 1)
    log_decay = (-jnp.exp(gdn_a_log.astype(jnp.float32))
                 * jax.nn.softplus(g_a.astype(jnp.float32) + gdn_dt_bias.astype(jnp.float32))).transpose(0, 2, 1)
    o = gated_delta_rule(l2_norm(to_heads(gq, GDN_HEADS)), l2_norm(to_heads(gk, GDN_HEADS)),
                         to_heads(gv, GDN_HEADS), log_decay, beta)
    y_gdn = from_heads(rms_norm(o, gdn_norm)).astype(h.dtype) * jax.nn.silu(g_z)

    return jnp.concatenate([y_ret, y_moba, y_gdn], axis=-1) @ w_out


def memory_cross_attention(h, mem_h, wq, wkv, q_norm, k_norm, wo):
    b, s, _ = h.shape
    m = mem_h.shape[1]
    q = rms_norm((h @ wq).reshape(b, s, CROSS_HEADS, CROSS_HEAD_DIM), q_norm)
    k, v = jnp.split(mem_h @ wkv, 2, axis=-1)
    k = rms_norm(k.reshape(b, m, CROSS_HEADS, CROSS_HEAD_DIM), k_norm)
    v = v.reshape(b, m, CROSS_HEADS, CROSS_HEAD_DIM)
    logits = jnp.einsum('bshd,bmhd->bhsm', q.astype(jnp.float32), k.astype(jnp.float32)) * CROSS_HEAD_DIM ** -0.5
    p = jax.nn.softmax(logits, axis=-1)
    o = jnp.einsum('bhsm,bmhd->bshd', p, v.astype(jnp.float32)).reshape(b, s, CROSS_W).astype(h.dtype)
    return o @ wo


def conv_ffn(h, w_up, conv_w, conv_b, w_down):
    u = causal_dwconv(h @ w_up, conv_w) + conv_b
    gate, val = jnp.split(u, 2, axis=-1)
    return (jax.nn.silu(gate) * val) @ w_down


def setup_inputs(seed: int = 0) -> dict:
    key = jax.random.key(seed)
    keys = jax.random.split(key, 32)
    L = DEPTH

    def normal(i, shape, scale):
        return jax.random.normal(keys[i], shape, jnp.float32) * scale

    def gain(i, dim):
        return 1.0 + normal(i, (L, dim), 0.02)

    dt = jnp.exp(jax.random.uniform(keys[8], (L, GDN_HEADS), jnp.float32, math.log(1e-3), math.log(1e-1)))
    return {
        'x': normal(0, (BATCH, SEQ, D_MODEL), 1.0),
        'mem': normal(1, (BATCH, MEM_LEN, D_MODEL), 1.0),
        'norm_mix': gain(2, D_MODEL),
        'w_in': normal(3, (L, D_MODEL, IN_COLS), D_MODEL ** -0.5),
        'ret_norm': gain(4, HEAD_DIM),
        'moba_q_norm': gain(5, HEAD_DIM),
        'moba_k_norm': gain(6, HEAD_DIM),
        'gdn_conv': normal(7, (L, GDN_CONV, 3 * GDN_W), GDN_CONV ** -0.5),
        'gdn_a_log': jnp.log(jax.random.uniform(keys[9], (L, GDN_HEADS), jnp.float32, 1.0, 16.0)),
        'gdn_dt_bias': dt + jnp.log(-jnp.expm1(-dt)),
        'gdn_norm': gain(10, HEAD_DIM),
        'w_out': normal(11, (L, D_MIX, D_MODEL), D_MIX ** -0.5),
        'norm_cross': gain(12, D_MODEL),
        'norm_mem': gain(13, D_MODEL),
        'cross_wq': normal(14, (L, D_MODEL, CROSS_W), D_MODEL ** -0.5),
        'cross_wkv': normal(15, (L, D_MODEL, 2 * CROSS_W), D_MODEL ** -0.5),
        'cross_q_norm': gain(16, CROSS_HEAD_DIM),
        'cross_k_norm': gain(17, CROSS_HEAD_DIM),
        'cross_wo': normal(18, (L, CROSS_W, D_MODEL), CROSS_W ** -0.5),
        'norm_ffn': gain(19, D_MODEL),
        'ffn_up': normal(20, (L, D_MODEL, 2 * D_FF), D_MODEL ** -0.5),
        'ffn_conv': normal(21, (L, FFN_CONV, 2 * D_FF), FFN_CONV ** -0.5),
        'ffn_conv_b': normal(22, (L, 2 * D_FF), 0.01),
        'ffn_down': normal(23, (L, D_FF, D_MODEL), D_FF ** -0.5),
        'rel_bias': normal(24, (MOBA_HEADS, REL_BUCKETS), 0.2),
    }


def reference(x, mem, norm_mix, w_in, ret_norm, moba_q_norm, moba_k_norm, gdn_conv, gdn_a_log,
              gdn_dt_bias, gdn_norm, w_out, norm_cross, norm_mem, cross_wq, cross_wkv, cross_q_norm,
              cross_k_norm, cross_wo, norm_ffn, ffn_up, ffn_conv, ffn_conv_b, ffn_down, rel_bias):
    for l in range(DEPTH):
        h = rms_norm(x, norm_mix[l])
        x = x + hybrid_mixer(h, w_in[l], ret_norm[l], moba_q_norm[l], moba_k_norm[l], gdn_conv[l],
                             gdn_a_log[l], gdn_dt_bias[l], gdn_norm[l], w_out[l], rel_bias)
        h = rms_norm(x, norm_cross[l])
        x = x + memory_cross_attention(h, rms_norm(mem, norm_mem[l]), cross_wq[l], cross_wkv[l],
                                       cross_q_norm[l], cross_k_norm[l], cross_wo[l])
        h = rms_norm(x, norm_ffn[l])
        x = x + conv_ffn(h, ffn_up[l], ffn_conv[l], ffn_conv_b[l], ffn_down[l])
    return x
```

```python
import contextlib
import math
import numpy as np
import concourse.bass as bass
import concourse.mybir as mybir
from concourse.bass_utils import run_bass_kernel_spmd

F32 = mybir.dt.float32
BF16 = mybir.dt.bfloat16
AF = mybir.ActivationFunctionType
ALU = mybir.AluOpType
AX = mybir.AxisListType

S_LEN = 4096
D = 1024
NT = S_LEN // 128
NS = S_LEN // 512
MEM = 256
DFF = 2816
NFC = DFF // 128
IN_COLS = 3592
EPS = 1e-6
NEG = -30000.0
NRING = 32
EPOCH = 30000


class Sched:
    ENG = ("pe", "act", "dve", "pool", "sp")

    def __init__(self, nc, stack):
        self.nc = nc
        self.stack = stack
        self.h = {"pe": nc.tensor, "act": nc.scalar, "dve": nc.vector, "pool": nc.gpsimd, "sp": nc.sync}
        self.prog = {e: [] for e in self.ENG}
        self.cnt = {e: 0 for e in self.ENG}
        self.epoch = {e: 0 for e in self.ENG}
        self.sems = {}
        self.waited = {e: {} for e in self.ENG}
        self.lastw = {}
        self.readers = {}
        self.rq = {"sp": 0, "pool": 1, "act": 2}
        self.ring = [stack.enter_context(nc.semaphore(f"dq{i}")) for i in range(3 * NRING)]
        self.ring_cnt = [0] * (3 * NRING)
        self.dma_next = [0, 0, 0]
        self.nsem = 0
        self.ninst = 0
        for e in self.ENG:
            self._new_sem(e)

    def _new_sem(self, e):
        self.nsem += 1
        s = self.stack.enter_context(self.nc.semaphore(f"s_{e}_{self.nsem}"))
        self.sems[(e, self.epoch[e])] = s

    def _wait(self, eng, semkey, v):
        if self.waited[eng].get(semkey, 0) >= v:
            return
        self.waited[eng][semkey] = v
        sh = self.ring[semkey[1]] if semkey[0] == "d" else self.sems[(semkey[1], semkey[2])]
        self.prog[eng].append(("wait_ge", (sh, v), {}, None))

    def _deps(self, eng, reads, writes):
        toks = []
        for k in reads:
            t = self.lastw.get(k)
            if t is not None:
                toks.append(t)
        for k in writes:
            t = self.lastw.get(k)
            if t is not None:
                toks.append(t)
            toks.extend(self.readers.get(k, ()))
        for (sk, v) in toks:
            if sk[0] == "e" and sk[1] == eng and eng == "pe":
                continue
            self._wait(eng, sk, v)

    def _commit(self, tok, reads, writes):
        for k in reads:
            lst = self.readers.setdefault(k, [])
            lst[:] = [t for t in lst if t[0] != tok[0]]
            lst.append(tok)
        for k in writes:
            self.lastw[k] = tok
            self.readers[k] = []

    def op(self, eng, name, *args, R=(), W=(), inc=True, **kw):
        self._deps(eng, R, W)
        self.ninst += 1
        if inc:
            if self.cnt[eng] >= EPOCH:
                self.epoch[eng] += 1
                self.cnt[eng] = 0
                self._new_sem(eng)
            self.cnt[eng] += 1
            sh = self.sems[(eng, self.epoch[eng])]
            self.prog[eng].append((name, args, kw, sh))
            tok = (("e", eng, self.epoch[eng]), self.cnt[eng])
        else:
            self.prog[eng].append((name, args, kw, None))
            if self.cnt[eng] >= EPOCH:
                tok = (("e", eng, self.epoch[eng] + 1), 1)
            else:
                tok = (("e", eng, self.epoch[eng]), self.cnt[eng] + 1)
        self._commit(tok, R, W)
        return tok

    def dma(self, q, out, in_, R=(), W=(), **kw):
        self._deps(q, R, W)
        self.ninst += 1
        qi = self.rq[q]
        slot = qi * NRING + self.dma_next[qi] % NRING
        self.dma_next[qi] += 1
        if self.ring_cnt[slot] > 0:
            self._wait(q, ("d", slot), 16 * self.ring_cnt[slot])
        self.ring_cnt[slot] += 1
        v = 16 * self.ring_cnt[slot]
        kw = dict(kw)
        kw["out"] = out
        kw["in_"] = in_
        self.prog[q].append(("dma_start", (), kw, (self.ring[slot], 16)))
        tok = (("d", slot), v)
        self._commit(tok, R, W)
        return tok

    def barrier(self):
        for f in self.ENG:
            for e in self.ENG:
                if e != f and self.cnt[e] > 0:
                    self._wait(f, ("e", e, self.epoch[e]), self.cnt[e])
            for s in range(3 * NRING):
                if self.ring_cnt[s] > 0:
                    self._wait(f, ("d", s), 16 * self.ring_cnt[s])
        self.lastw.clear()
        self.readers.clear()

    @staticmethod
    def _emit(h, lst):
        for (name, args, kw, inc) in lst:
            ins = getattr(h, name)(*args, **kw)
            if inc is not None:
                if isinstance(inc, tuple):
                    ins.then_inc(inc[0], inc[1])
                else:
                    ins.then_inc(inc, 1)

    def finish(self):
        self.barrier()
        with self.nc.Block() as block:
            @block.tensor
            def _(e):
                self._emit(self.h["pe"], self.prog["pe"])

            @block.scalar
            def _(e):
                self._emit(self.h["act"], self.prog["act"])

            @block.vector
            def _(e):
                self._emit(self.h["dve"], self.prog["dve"])

            @block.gpsimd
            def _(e):
                self._emit(self.h["pool"], self.prog["pool"])

            @block.sync
            def _(e):
                self._emit(self.h["sp"], self.prog["sp"])


def _t5_bucket(n):
    n = np.maximum(n, 0)
    exact = 16
    nf = np.maximum(n, exact).astype(np.float32)
    large = exact + (np.log(nf / np.float32(exact)) / np.float32(math.log(128 / exact)) * np.float32(16)).astype(np.int32)
    return np.where(n < exact, n, np.minimum(large, 31))


def make_consts():
    c = {}
    c["ident"] = np.eye(128, dtype=np.float32)
    half = 32
    inv_freq = (10000.0 ** (-np.arange(half, dtype=np.float32) / half)).astype(np.float32)
    pos = np.arange(S_LEN, dtype=np.float32)
    ang = pos[None, :] * inv_freq[:, None]
    cos = np.cos(ang).astype(np.float32)
    sin = np.sin(ang).astype(np.float32)
    cos64 = np.concatenate([cos, cos], 0)
    sin64 = np.concatenate([-sin, sin], 0)
    c["rope"] = np.stack([np.concatenate([cos64, cos64], 0), np.concatenate([sin64, sin64], 0)], 0).astype(np.float32)
    lg = np.log1p(-np.exp2(-5.0 - np.arange(4, dtype=np.float64)))
    kl = np.arange(128)[:, None]
    ql = np.arange(512)[None, :]
    dec = np.zeros((4, 5, 128, 512), np.float32)
    for h in range(4):
        dec[h, 0] = np.exp(lg[h] * (ql - kl + 128))
        for jj in range(4):
            e = ql - kl - 128 * jj
            dec[h, 1 + jj] = np.where(e >= 0, np.exp(lg[h] * np.maximum(e, 0)), 0.0)
    c["ret_decay"] = dec
    c["ret_lg"] = lg
    oh = np.zeros((16, S_LEN), np.float32)
    for n in range(16):
        oh[n, n * 256:(n + 1) * 256] = 1.0
    c["blk_oh"] = oh
    t5 = np.zeros((33, 383), np.float32)
    for i in range(383):
        dl = i - 127
        if dl >= 0:
            t5[int(_t5_bucket(np.array(dl))), i] += 1.0
            t5[31, i] -= 1.0
        else:
            t5[32, i] = NEG
    c["t5oh"] = t5
    fm = np.zeros((2, 8, 2, 4, 16), np.float32)
    for s_ in range(8):
        for i4 in range(4):
            blk = (4 * s_ + i4) // 2
            fm[0, s_, :, i4, blk:] = -1e30
            fm[1, s_, :, i4, blk] = 1.0
    c["moba_fm"] = fm.reshape(2, 1024)
    i = np.arange(128)[:, None]
    j = np.arange(128)[None, :]
    c["tri"] = np.stack([(i <= j), (i > j), (i >= j)], 0).astype(np.float32)
    return c


CONST_SHAPES = {"ident": [128, 128], "rope": [2, 128, S_LEN], "ret_decay": [4, 5, 128, 512],
                "blk_oh": [16, S_LEN], "t5oh": [33, 383], "tri": [3, 128, 128], "moba_fm": [2, 1024]}

W_SHAPES = {
    "x": [S_LEN, D], "mem": [MEM, D], "norm_mix": [2, D], "w_in": [2, D, IN_COLS], "ret_norm": [2, 64],
    "moba_q_norm": [2, 64], "moba_k_norm": [2, 64], "gdn_conv": [2, 4, 768], "gdn_a_log": [2, 4],
    "gdn_dt_bias": [2, 4], "gdn_norm": [2, 64], "w_out": [2, D, D], "norm_cross": [2, D], "norm_mem": [2, D],
    "cross_wq": [2, D, 512], "cross_wkv": [2, D, D], "cross_q_norm": [2, 128], "cross_k_norm": [2, 128],
    "cross_wo": [2, 512, D], "norm_ffn": [2, D], "ffn_up": [2, D, 2 * DFF], "ffn_conv": [2, 3, 2 * DFF],
    "ffn_conv_b": [2, 2 * DFF], "ffn_down": [2, DFF, D], "rel_bias": [8, 32],
}


def dap(t, off, pat):
    return bass.AP(t.tensor if hasattr(t, "tensor") else t, off, pat)


class Builder:
    def __init__(self, debug=False, stages=None, nlayers=2):
        self.debug = debug
        self.stages = stages
        self.nlayers = nlayers
        self.nc = bass.Bass("TRN2", target_bir_lowering=False)
        nc = self.nc
        self.I = {k: nc.dram_tensor(k, s, F32, kind="ExternalInput").ap() for k, s in W_SHAPES.items()}
        self.C = {k: nc.dram_tensor("c_" + k, s, F32, kind="ExternalInput").ap() for k, s in CONST_SHAPES.items()}
        self.y = nc.dram_tensor("y", [S_LEN, D], F32, kind="ExternalOutput").ap()
        self.dbg = {}
        self.lg = make_consts()["ret_lg"]
        self.mix_parts = None
        self.gdn_lb = 7

    def scratch(self, name, shape, dt):
        kind = "ExternalOutput" if (self.debug and name in self.debug) else "Internal"
        t = self.nc.dram_tensor(name, shape, dt, kind=kind).ap()
        return t

    def build(self):
        nc = self.nc
        with contextlib.ExitStack() as st:
            self.S = S = Sched(nc, st)
            self.ps = [st.enter_context(nc.psum_tensor(f"ps{i}", [128, 512], F32)) for i in range(8)]
            self.psb = [self.ps[6 + i][:].bitcast(BF16) for i in range(2)]
            self.psi = 0
            self.ident_f = st.enter_context(nc.sbuf_tensor("ident_f", [128, 128], F32))
            self.ident_b = st.enter_context(nc.sbuf_tensor("ident_b", [128, 128], BF16))
            self.ones_f = st.enter_context(nc.sbuf_tensor("ones_f", [128, 128], F32))
            self.ones_b = st.enter_context(nc.sbuf_tensor("ones_b", [128, 128], BF16))
            self.blk64 = st.enter_context(nc.sbuf_tensor("blk64", [128, 128], F32))
            self.epsc = st.enter_context(nc.sbuf_tensor("epsc", [128, 1], F32))
            self.nhalf = st.enter_context(nc.sbuf_tensor("nhalf", [128, 512], F32))
            self.none_ = st.enter_context(nc.sbuf_tensor("none_", [128, 512], F32))
            S.dma("sp", self.ident_f[:], self.C["ident"], W=["ident_f"])
            S.op("dve", "tensor_copy", self.ident_b[:], self.ident_f[:], R=["ident_f"], W=["ident_b"])
            S.op("dve", "memset", self.ones_f[:], 1.0, W=["ones_f"])
            S.op("dve", "memset", self.ones_b[:], 1.0, W=["ones_b"])
            S.op("dve", "memset", self.blk64[:], 0.0, W=["blk64"])
            S.op("dve", "memset", self.blk64[0:64, 0:64], 1.0, R=["blk64"], W=["blk64"])
            S.op("dve", "memset", self.blk64[64:128, 64:128], 1.0, R=["blk64"], W=["blk64"])
            S.op("dve", "memset", self.epsc[:], EPS, W=["epsc"])
            S.op("dve", "memset", self.nhalf[:], -0.5, W=["nhalf"])
            S.op("dve", "memset", self.none_[:], -1.0, W=["none_"])
            self.CK = ["ident_f", "ident_b", "ones_f", "ones_b", "blk64", "epsc"]
            S.barrier()
            self.prep_t5(st)

            xs = [self.I["x"]]
            n_sub = 0
            for l in range(self.nlayers):
                for kind in ("mix", "cross", "ffn"):
                    if self.stages is not None and (l, kind) not in self.stages:
                        continue
                    n_sub += 1
            k = 0
            for l in range(self.nlayers):
                for kind in ("mix", "cross", "ffn"):
                    if self.stages is not None and (l, kind) not in self.stages:
                        continue
                    k += 1
                    xout = self.y if k == n_sub else self.scratch(f"xres{k}", [S_LEN, D], F32)
                    getattr(self, "sub_" + kind)(l, xs[-1], xout)
                    xs.append(xout)
                    S.barrier()
                    self._reset_consts()
            S.finish()
        return nc

    def _reset_consts(self):
        pass

    def nps(self):
        i = self.psi % 6
        self.psi += 1
        return i

    def norm_T(self, st, X, ntok, gain_row, hT, hkey, tag):
        nc, S = self.nc, self.S
        gT = st.enter_context(nc.sbuf_tensor(f"gT_{tag}", [128, 8], F32))
        S.dma("sp", gT[:], dap(gain_row, gain_row.offset, [[1, 128], [128, 8]]), W=[f"gT_{tag}"],
              allow_slow_non_contiguous=True)
        NB = 3
        xb = [st.enter_context(nc.sbuf_tensor(f"nx{i}_{tag}", [128, D], F32)) for i in range(NB)]
        sq = st.enter_context(nc.sbuf_tensor(f"nsq_{tag}", [128, D], BF16))
        xsb = [st.enter_context(nc.sbuf_tensor(f"nxs{i}_{tag}", [128, D], BF16)) for i in range(NB)]
        ss = [st.enter_context(nc.sbuf_tensor(f"nss{i}_{tag}", [128, 2], F32)) for i in range(NB)]
        nt = ntok // 128

        def evac(t):
            pb = self.psb[t % 2]
            S.op("dve", "tensor_tensor", hT[:, :, t * 128:(t + 1) * 128], pb[:].rearrange("p (c t) -> p c t", c=8),
                 gT[:].unsqueeze(2).to_broadcast([128, 8, 128]), ALU.mult, R=[("ps", 6 + t % 2), f"gT_{tag}"], W=[(hkey, t)])

        for t in range(nt):
            b = t % NB
            kx, ks, kxs = f"nx{b}_{tag}", f"nss{b}_{tag}", f"nxs{b}_{tag}"
            S.dma("sp", xb[b][:], X[t * 128:(t + 1) * 128, :], W=[kx])
            S.op("pool", "memset", ss[b][:], 0.0, W=[ks])
            S.op("act", "activation", sq[:], xb[b][:], AF.Square, accum_out=ss[b][:, 0:1], R=[kx, ks], W=[ks, f"nsq_{tag}"])
            S.op("act", "activation", ss[b][:, 1:2], ss[b][:, 0:1], AF.Sqrt, bias=self.epsc[:], scale=1.0 / D, R=[ks], W=[ks])
            S.op("dve", "reciprocal", ss[b][:, 1:2], ss[b][:, 1:2], R=[ks], W=[ks])
            S.op("dve", "tensor_scalar", xsb[b][:], xb[b][:], ss[b][:, 1:2], None, ALU.mult, R=[kx, ks], W=[kxs])
            pb = self.psb[t % 2]
            for c in range(8):
                S.op("pe", "transpose", pb[:, c * 128:(c + 1) * 128], xsb[b][:, c * 128:(c + 1) * 128], self.ident_b[:],
                     R=[kxs], W=[("ps", 6 + t % 2)], inc=(c == 7))
            if t >= 1:
                evac(t - 1)
        evac(nt - 1)

    def load_w(self, wt, key, Wl, col0, ncols, nk=8, row0=0):
        ncol_total = Wl.shape[-1]
        src = dap(Wl, Wl.offset + row0 * ncol_total + col0, [[ncol_total, 128], [128 * ncol_total, nk], [1, ncols]])
        return self.S.dma("pool", wt[:, 0:nk, 0:ncols], src, W=[key])

    def out_proj_residual(self, st, lhs_fn, nk, Wl, X, Xout, tag, lhs_keys, src=None):
        nc, S = self.nc, self.S
        wt = st.enter_context(nc.sbuf_tensor(f"wo_{tag}", [128, nk, D], BF16))
        step = 8
        for k0 in range(0, nk, step):
            kn = min(step, nk - k0)
            src_w = dap(Wl, Wl.offset + k0 * 128 * D, [[D, 128], [128 * D, kn], [1, D]])
            S.dma("pool", wt[:, k0:k0 + kn, :], src_w, W=[f"wo{k0}"])
        wkeys = [f"wo{k0}" for k0 in range(0, nk, step)]
        NX = 4
        xb = [st.enter_context(nc.sbuf_tensor(f"ox{i}_{tag}", [128, D], F32)) for i in range(NX)]
        lb = None
        if src is not None:
            lb = [st.enter_context(nc.sbuf_tensor(f"ol{i}_{tag}", [128, nk, 128], BF16)) for i in range(3)]
        for t in range(NT):
            b = t % NX
            S.dma("sp", xb[b][:], X[t * 128:(t + 1) * 128, :], W=[f"ox{b}"])
            if src is not None:
                l3 = t % 3
                S.dma("sp", lb[l3][:], dap(src, src.offset + t * 128, [[S_LEN, 128], [128 * S_LEN, nk], [1, 128]]), W=[f"ol{l3}"])
            for hf in range(2):
                pi = (2 * t + hf) % 6
                for k in range(nk):
                    if src is not None:
                        lhs, lk = lb[t % 3][:, k, :], [f"ol{t % 3}"]
                    else:
                        lhs, lk = lhs_fn(k, t), lhs_keys(k, t)
                    S.op("pe", "matmul", self.ps[pi][:, :], lhs, wt[:, k, hf * 512:(hf + 1) * 512],
                         start=(k == 0), stop=(k == nk - 1), R=wkeys + lk, W=[("ps", pi)], inc=(k == nk - 1))
                S.op("dve", "tensor_tensor", xb[b][:, hf * 512:(hf + 1) * 512], self.ps[pi][:, :],
                     xb[b][:, hf * 512:(hf + 1) * 512], ALU.add, R=[("ps", pi), f"ox{b}"], W=[f"ox{b}"])
            S.dma("pool", Xout[t * 128:(t + 1) * 128, :], xb[b][:], R=[f"ox{b}"], W=[("xout", tag, t)])

    def col_vec(self, st, name, src_row, n=128, scale=None):
        t = st.enter_context(self.nc.sbuf_tensor(name, [128, 1], F32))
        self.S.dma("sp", t[0:n, :], dap(src_row, src_row.offset, [[1, n], [1, 1]]), W=[name])
        if scale is not None:
            self.S.op("dve", "tensor_scalar", t[0:n, :], t[0:n, :], float(scale), None, ALU.mult, R=[name], W=[name])
        return t

    def sub_cross(self, l, X, Xout):
        nc, S = self.nc, self.S
        tag = f"c{l}"
        with contextlib.ExitStack() as st:
            hT = st.enter_context(nc.sbuf_tensor(f"hT_{tag}", [128, 8, S_LEN], BF16))
            oT = st.enter_context(nc.sbuf_tensor(f"oT_{tag}", [128, 4, S_LEN], BF16))
            with contextlib.ExitStack() as st2:
                self.norm_T(st2, X, S_LEN, self.I["norm_cross"][l], hT, "hT", tag)
            S.barrier()
            with contextlib.ExitStack() as st2:
                sb = lambda n, s, d: st2.enter_context(nc.sbuf_tensor(f"{n}_{tag}", s, d))
                memT = sb("memT", [128, 8, MEM], BF16)
                with contextlib.ExitStack() as st3:
                    self.norm_T(st3, self.I["mem"], MEM, self.I["norm_mem"][l], memT, "memT", tag + "m")
                S.barrier()
                kT = sb("kT", [128, 4, MEM], BF16)
                vtm = sb("vtm", [128, 2, 512], BF16)
                gq = self.col_vec(st2, f"gq_{tag}", self.I["cross_q_norm"][l], scale=128.0 ** -0.5)
                gk = self.col_vec(st2, f"gk_{tag}", self.I["cross_k_norm"][l])
                wk = sb("wk", [128, 8, 512], BF16)
                wv = sb("wv", [128, 8, 512], BF16)
                wq = sb("wq", [128, 8, 512], BF16)
                self.load_w(wk, "wk", self.I["cross_wkv"][l], 0, 512)
                self.load_w(wv, "wv", self.I["cross_wkv"][l], 512, 512)
                self.load_w(wq, "wq", self.I["cross_wq"][l], 0, 512)
                sq = [sb(f"sq{i}", [128, 512], F32) for i in range(2)]
                rs = [sb(f"rs{i}", [128, 512], F32) for i in range(2)]
                qn = [sb(f"qn{i}", [128, 512], BF16) for i in range(2)]
                pT = [sb(f"pT{i}", [128, 512], BF16) for i in range(4)]
                rec = [sb(f"rec{i}", [128, 512], F32) for i in range(2)]
                mk = [("memT", 0), ("memT", 1)]
                for h in range(4):
                    pi = h % 2
                    for k in range(8):
                        S.op("pe", "matmul", self.ps[pi][:, 0:MEM], wk[:, k, h * 128:(h + 1) * 128], memT[:, k, :],
                             start=(k == 0), stop=(k == 7), R=["wk"] + mk, W=[("ps", pi)], inc=(k == 7))
                    S.op("act", "activation", sq[pi][:, 0:MEM], self.ps[pi][:, 0:MEM], AF.Square, R=[("ps", pi)], W=[f"sq{pi}"])
                    pj = 2 + pi
                    S.op("pe", "matmul", self.ps[pj][:, 0:MEM], self.ones_f[:], sq[pi][:, 0:MEM], start=True, stop=True, R=[f"sq{pi}"], W=[("ps", pj)])
                    S.op("act", "activation", rs[pi][:, 0:MEM], self.ps[pj][:, 0:MEM], AF.Sqrt, bias=self.epsc[:], scale=1.0 / 128, R=[("ps", pj)], W=[f"rs{pi}"])
                    S.op("dve", "reciprocal", rs[pi][:, 0:MEM], rs[pi][:, 0:MEM], R=[f"rs{pi}"], W=[f"rs{pi}"])
                    S.op("dve", "scalar_tensor_tensor", kT[:, h, :], self.ps[pi][:, 0:MEM], gk[:, 0:1], rs[pi][:, 0:MEM], ALU.mult, ALU.mult,
                         R=[("ps", pi), f"rs{pi}", f"gk_{tag}"], W=["kT"])
                for mt in range(2):
                    pi = 4 + mt
                    for k in range(8):
                        S.op("pe", "matmul", self.ps[pi][:, :], memT[:, k, mt * 128:(mt + 1) * 128], wv[:, k, :],
                             start=(k == 0), stop=(k == 7), R=["wv"] + mk, W=[("ps", pi)], inc=(k == 7))
                    S.op("act", "copy", vtm[:, mt, :], self.ps[pi][:, :], R=[("ps", pi)], W=["vtm"])
                its = [(h, s) for h in range(4) for s in range(NS)]

                def ca(i):
                    h, s = its[i]
                    b2 = i % 2
                    pq = b2
                    for k in range(8):
                        S.op("pe", "matmul", self.ps[pq][:, :], wq[:, k, h * 128:(h + 1) * 128], hT[:, k, s * 512:(s + 1) * 512],
                             start=(k == 0), stop=(k == 7), R=["wq"] + [("hT", 4 * s + j) for j in range(4)], W=[("ps", pq)], inc=(k == 7))
                    S.op("act", "activation", sq[b2][:], self.ps[pq][:, :], AF.Square, R=[("ps", pq)], W=[f"sq{b2}"])

                def cb_(i):
                    h, s = its[i]
                    b2 = i % 2
                    pq = b2
                    S.op("pe", "matmul", self.ps[2][:, :], self.ones_f[:], sq[b2][:], start=True, stop=True, R=[f"sq{b2}"], W=[("ps", 2)])
                    S.op("act", "activation", rs[b2][:], self.ps[2][:, :], AF.Sqrt, bias=self.epsc[:], scale=1.0 / 128, R=[("ps", 2)], W=[f"rs{b2}"])
                    S.op("dve", "reciprocal", rs[b2][:], rs[b2][:], R=[f"rs{b2}"], W=[f"rs{b2}"])
                    S.op("dve", "scalar_tensor_tensor", qn[b2][:], self.ps[pq][:, :], gq[:, 0:1], rs[b2][:], ALU.mult, ALU.mult,
                         R=[("ps", pq), f"rs{b2}", f"gq_{tag}"], W=[f"qn{b2}"])

                def cc(i):
                    h, s = its[i]
                    b2 = i % 2
                    po, pz = 4, 5
                    for mt in range(2):
                        S.op("pe", "matmul", self.ps[3][:, :], kT[:, h, mt * 128:(mt + 1) * 128], qn[b2][:], start=True, stop=True,
                             R=["kT", f"qn{b2}"], W=[("ps", 3)])
                        pt = pT[(2 * i + mt) % 4]
                        ptk = f"pT{(2 * i + mt) % 4}"
                        S.op("act", "activation", pt[:], self.ps[3][:, :], AF.Exp, R=[("ps", 3)], W=[ptk])
                    for mt in range(2):
                        pt = pT[(2 * i + mt) % 4]
                        ptk = f"pT{(2 * i + mt) % 4}"
                        S.op("pe", "matmul", self.ps[po][:, :], vtm[:, mt, h * 128:(h + 1) * 128], pt[:], start=(mt == 0), stop=(mt == 1),
                             R=["vtm", ptk], W=[("ps", po)], inc=(mt == 1))
                    for mt in range(2):
                        pt = pT[(2 * i + mt) % 4]
                        ptk = f"pT{(2 * i + mt) % 4}"
                        S.op("pe", "matmul", self.ps[pz][:, :], self.ones_b[:], pt[:], start=(mt == 0), stop=(mt == 1),
                             R=[ptk], W=[("ps", pz)], inc=(mt == 1))
                    S.op("dve", "reciprocal", rec[b2][:], self.ps[pz][:, :], R=[("ps", pz)], W=[f"rec{b2}"])
                    S.op("dve", "tensor_tensor", oT[:, h, s * 512:(s + 1) * 512], self.ps[po][:, :], rec[b2][:], ALU.mult,
                         R=[("ps", po), f"rec{b2}"], W=[("oT", h, s)])

                n_it = len(its)
                for i in range(n_it + 2):
                    if i < n_it:
                        ca(i)
                    if 0 <= i - 1 < n_it:
                        cb_(i - 1)
                    if 0 <= i - 2 < n_it:
                        cc(i - 2)
            S.barrier()
            with contextlib.ExitStack() as st2:
                self.out_proj_residual(st2, lambda k, t: oT[:, k, t * 128:(t + 1) * 128], 4, self.I["cross_wo"][l], X, Xout, tag,
                                       lambda k, t: [("oT", k, t // 4)])

    def fm_proj(self, wt, wkey, hT, s, pi, col0=0):
        S = self.S
        for k in range(8):
            S.op("pe", "matmul", self.ps[pi][:, :], wt[:, k, col0:col0 + 128], hT[:, k, s * 512:(s + 1) * 512],
                 start=(k == 0), stop=(k == 7), R=[wkey] + [("hT", 4 * s + i) for i in range(4)], W=[("ps", pi)], inc=(k == 7))

    def prep_t5(self, st):
        nc, S = self.nc, self.S
        self.t5R = self.scratch("t5R", [8, 128, 383], F32)
        self.bias_t = st.enter_context(nc.sbuf_tensor("bias_t", [128, 8, 2, 128], BF16))
        self.negt = st.enter_context(nc.sbuf_tensor("negt", [128, 128], BF16))
        S.op("dve", "memset", self.negt[:], NEG, W=["negt"])
        with contextlib.ExitStack() as st2:
            relbT = st2.enter_context(nc.sbuf_tensor("relbT", [32, 8], F32))
            rb = self.I["rel_bias"]
            S.dma("sp", relbT[:], dap(rb, rb.offset, [[1, 32], [32, 8]]), W=["relbT"], allow_slow_non_contiguous=True)
            t5 = st2.enter_context(nc.sbuf_tensor("t5oh", [33, 383], F32))
            S.dma("sp", t5[:], self.C["t5oh"], W=["t5oh"])
            L = st2.enter_context(nc.sbuf_tensor("t5L", [33, 128], F32))
            Rs = st2.enter_context(nc.sbuf_tensor("t5Rs", [128, 383], F32))
            for h in range(8):
                S.op("dve", "tensor_copy", L[0:32, :], relbT[0:32, h:h + 1].to_broadcast([32, 128]), R=["relbT"], W=["t5L"])
                S.op("dve", "memset", L[32:33, :], 1.0, R=["t5L"], W=["t5L"])
                S.op("pe", "matmul", self.ps[4][:, 0:383], L[:], t5[:], start=True, stop=True, R=["t5L", "t5oh"], W=[("ps", 4)])
                S.op("act", "copy", Rs[:], self.ps[4][:, 0:383], R=[("ps", 4)], W=["t5Rs"])
                S.dma("sp", self.t5R[h], Rs[:], R=["t5Rs"], W=[("t5R", h)])
                base = self.t5R.offset + h * 128 * 383
                S.dma("pool", self.bias_t[:, h, 0, :], dap(self.t5R, base + 127, [[382, 128], [1, 128]]), R=[("t5R", h)], W=["bias_t"])
                S.dma("pool", self.bias_t[:, h, 1, :], dap(self.t5R, base + 255, [[382, 128], [1, 128]]), R=[("t5R", h)], W=["bias_t"])
        S.barrier()

    def sub_mix(self, l, X, Xout):
        nc, S = self.nc, self.S
        tag = f"m{l}"
        I = self.I
        Win = I["w_in"][l]
        sc = lambda n, shp, dt: self.scratch(f"{n}{l}", shp, dt)
        qTr = sc("qTr", [256, S_LEN], BF16); kTr = sc("kTr", [256, S_LEN], BF16); rgT = sc("rgT", [256, S_LEN], F32)
        rvd = sc("rvd", [S_LEN, 256], BF16); mqT = sc("mqT", [512, S_LEN], BF16); mkT = sc("mkT", [512, S_LEN], BF16)
        mvd = sc("mvd", [S_LEN, 512], BF16); mmask = sc("mmask", [8, 16, S_LEN], BF16)
        gqT = sc("gqT", [256, S_LEN], BF16); gkT = sc("gkT", [256, S_LEN], BF16); gvT = sc("gvT", [256, S_LEN], BF16)
        gzd = sc("gzd", [S_LEN, 256], F32); yT = sc("yT", [D, S_LEN], BF16)
        with contextlib.ExitStack() as st:
            gba = st.enter_context(nc.sbuf_tensor(f"gba_{tag}", [128, NT, 8], F32))
            with contextlib.ExitStack() as sp_:
                hT = sp_.enter_context(nc.sbuf_tensor(f"hT_{tag}", [128, 8, S_LEN], BF16))
                with contextlib.ExitStack() as st2:
                    self.norm_T(st2, X, S_LEN, I["norm_mix"][l], hT, "hT", tag)
                S.barrier()
                cur = [sp_]
                sb = lambda n, shp, dt: cur[0].enter_context(nc.sbuf_tensor(f"{n}_{tag}", shp, dt))
                hkeys = lambda s: [("hT", 4 * s + i) for i in range(4)]
                wtm = sb("wtm", [128, 8, 512], BF16)
                stg_b = [sb(f"stgb{i}", [128, 512], BF16) for i in range(2)]
                stg_f = [sb(f"stgf{i}", [128, 256], F32) for i in range(2)]
                for (nm, col0, ncols) in (("rv", 512, 256), ("mv", 2048, 512), ("gz", 3328, 256), ("gba", 3584, 8)):
                    self.load_w(wtm, "wtm", Win, col0, ncols)
                    for t in range(NT):
                        pi = 2 + t % 2
                        for k in range(8):
                            S.op("pe", "matmul", self.ps[pi][:, 0:ncols], hT[:, k, t * 128:(t + 1) * 128], wtm[:, k, 0:ncols],
                                 start=(k == 0), stop=(k == 7), R=["wtm", ("hT", t)], W=[("ps", pi)], inc=(k == 7))
                        b = t % 2
                        if nm == "rv":
                            S.op("act", "copy", stg_b[b][:, 0:256], self.ps[pi][:, 0:256], R=[("ps", pi)], W=[f"stgb{b}"])
                            S.dma("sp", rvd[t * 128:(t + 1) * 128, :], stg_b[b][:, 0:256], R=[f"stgb{b}"], W=[("rvd", t)])
                        elif nm == "mv":
                            S.op("act", "copy", stg_b[b][:, :], self.ps[pi][:, :], R=[("ps", pi)], W=[f"stgb{b}"])
                            S.dma("sp", mvd[t * 128:(t + 1) * 128, :], stg_b[b][:, :], R=[f"stgb{b}"], W=[("mvd", t)])
                        elif nm == "gz":
                            S.op("act", "activation", stg_f[b][:, :], self.ps[pi][:, 0:256], AF.Silu, R=[("ps", pi)], W=[f"stgf{b}"])
                            S.dma("sp", gzd[t * 128:(t + 1) * 128, :], stg_f[b][:, :], R=[f"stgf{b}"], W=[("gzd", t)])
                        else:
                            S.op("act", "copy", gba[:, t, :], self.ps[pi][:, 0:8], R=[("ps", pi)], W=[("gba", t)])
                wa = [sb(f"wa{i}", [128, 8, 128], BF16) for i in range(2)]
                wb = [sb(f"wb{i}", [128, 8, 128], BF16) for i in range(2)]
                ob = [sb(f"ob{i}", [128, 512], BF16) for i in range(3)]
                sq = [sb(f"sq{i}", [128, 512], F32) for i in range(2)]
                rs = [sb(f"rs{i}", [128, 512], F32) for i in range(2)]
                sub_r = contextlib.ExitStack()
                cur[0] = sub_r
                rope = [sb(f"rope{i}", [128, 2, 512], F32) for i in range(3)]
                t1 = [sb(f"t1{i}", [128, 512], F32) for i in range(2)]
                t2 = [sb(f"t2{i}", [128, 512], F32) for i in range(2)]
                of = [sb(f"of{i}", [128, 512], F32) for i in range(2)]
                nload = [0, 0]

                def ldw(col0):
                    b = nload[0] % 2
                    nload[0] += 1
                    self.load_w(wa[b], f"wa{b}", Win, col0, 128)
                    return wa[b], f"wa{b}"

                def ldw_perm(col0):
                    b = nload[1] % 2
                    nload[1] += 1
                    for (d0, s0) in ((0, 32), (32, 0), (64, 96), (96, 64)):
                        src = dap(Win, Win.offset + col0 + s0, [[IN_COLS, 128], [128 * IN_COLS, 8], [1, 32]])
                        S.dma("pool", wb[b][:, :, d0:d0 + 32], src, R=[f"wb{b}"], W=[f"wb{b}"])
                    return wb[b], f"wb{b}"

                def run_pipe(n, stage_a, stage_b, la=1):
                    for i in range(n + la):
                        if i < n:
                            stage_a(i)
                        if i - la >= 0:
                            stage_b(i - la)

                rp = self.C["rope"]
                its = []
                for (col_base, scale, dst) in ((0, 1.0, qTr), (256, 0.125, kTr)):
                    for ch in range(2):
                        for s in range(NS):
                            its.append((col_base, scale, dst, ch, s))
                wcur = {}

                def ra(i):
                    col_base, scale, dst, ch, s = its[i]
                    if s == 0:
                        wcur["B"] = ldw_perm(col_base + ch * 128)
                        wcur["A"] = ldw(col_base + ch * 128)
                    b3 = i % 3
                    S.dma("sp", rope[b3][:], dap(rp, rp.offset + s * 512, [[S_LEN, 128], [128 * S_LEN, 2], [1, 512]]), W=[f"rope{b3}"])
                    pa = (2 * i) % 4
                    self.fm_proj(wcur["A"][0], wcur["A"][1], hT, s, pa)
                    self.fm_proj(wcur["B"][0], wcur["B"][1], hT, s, pa + 1)

                def rb(i):
                    col_base, scale, dst, ch, s = its[i]
                    b3, b2 = i % 3, i % 2
                    pa = (2 * i) % 4
                    S.op("dve", "scalar_tensor_tensor", t1[b2][:], self.ps[pa][:, :], float(scale), rope[b3][:, 0, :], ALU.mult, ALU.mult,
                         R=[("ps", pa), f"rope{b3}"], W=[f"t1{b2}"])
                    S.op("dve", "scalar_tensor_tensor", t2[b2][:], self.ps[pa + 1][:, :], float(scale), rope[b3][:, 1, :], ALU.mult, ALU.mult,
                         R=[("ps", pa + 1), f"rope{b3}"], W=[f"t2{b2}"])
                    S.op("pool", "tensor_tensor", ob[b3][:], t1[b2][:], t2[b2][:], ALU.add, R=[f"t1{b2}", f"t2{b2}"], W=[f"ob{b3}"])
                    S.dma("pool", dst[ch * 128:(ch + 1) * 128, s * 512:(s + 1) * 512], ob[b3][:], R=[f"ob{b3}"], W=[("fmout", i)])
                run_pipe(len(its), ra, rb)

                its_g = [(ch, s) for ch in range(2) for s in range(NS)]

                def ga(i):
                    ch, s = its_g[i]
                    if s == 0:
                        wcur["A"] = ldw(768 + ch * 128)
                    self.fm_proj(wcur["A"][0], wcur["A"][1], hT, s, i % 4)

                def gb(i):
                    ch, s = its_g[i]
                    b2 = i % 2
                    S.op("act", "activation", of[b2][:], self.ps[i % 4][:, :], AF.Silu, R=[("ps", i % 4)], W=[f"of{b2}"])
                    S.dma("pool", rgT[ch * 128:(ch + 1) * 128, s * 512:(s + 1) * 512], of[b2][:], R=[f"of{b2}"], W=[("rgT", ch, s)])
                run_pipe(len(its_g), ga, gb)

                S.barrier()
                sub_r.close()
                sub_m = contextlib.ExitStack()
                cur[0] = sub_m
                gk2 = sb("gk2", [128, 1], F32); gq2 = sb("gq2", [128, 1], F32)
                for hh in range(2):
                    S.dma("sp", gk2[hh * 64:(hh + 1) * 64, :], dap(I["moba_k_norm"][l], I["moba_k_norm"][l].offset, [[1, 64], [1, 1]]), R=["gk2"], W=["gk2"])
                    S.dma("sp", gq2[hh * 64:(hh + 1) * 64, :], dap(I["moba_q_norm"][l], I["moba_q_norm"][l].offset, [[1, 64], [1, 1]]), R=["gq2"], W=["gq2"])
                S.op("dve", "tensor_scalar", gq2[:], gq2[:], 0.125, None, ALU.mult, R=["gq2"], W=["gq2"])
                kms = sb("kms", [128, 4, 16], F32)
                n32 = [sb(f"n32{i}", [128, 512], F32) for i in range(2)]
                gate = sb("gate", [128, 64, 16], F32)
                m8 = sb("m8", [128, 64, 8], F32)
                thr = sb("thr", [128, 64], F32)
                alw = sb("alw", [128, 64, 16], F32)
                mTs = sb("mTs", [16, 2, S_LEN], BF16)
                fmk = sb("fmk", [128, 2, 1024], F32)
                S.dma("sp", fmk[:], dap(self.C["moba_fm"], self.C["moba_fm"].offset, [[0, 128], [1024, 2], [1, 1024]]), W=["fmk"])
                its_m = []
                for (isq, col_base, gcol, gkey, dst) in ((0, 1536, gk2, "gk2", mkT), (1, 1024, gq2, "gq2", mqT)):
                    for ch in range(4):
                        for s in range(NS):
                            its_m.append((isq, col_base, gcol, gkey, dst, ch, s))

                def ma(i):
                    isq, col_base, gcol, gkey, dst, ch, s = its_m[i]
                    if s == 0:
                        wcur["A"] = ldw(col_base + ch * 128)
                    pa = i % 3
                    self.fm_proj(wcur["A"][0], wcur["A"][1], hT, s, pa)
                    S.op("act", "activation", sq[i % 2][:], self.ps[pa][:, :], AF.Square, R=[("ps", pa)], W=[f"sq{i % 2}"])

                def mb(i):
                    isq, col_base, gcol, gkey, dst, ch, s = its_m[i]
                    pa = i % 3
                    b2, b3 = i % 2, i % 3
                    S.op("pe", "matmul", self.ps[4][:, :], self.blk64[:], sq[b2][:], start=True, stop=True, R=[f"sq{b2}"], W=[("ps", 4)])
                    S.op("act", "activation", rs[b2][:], self.ps[4][:, :], AF.Sqrt, bias=self.epsc[:], scale=1.0 / 64, R=[("ps", 4)], W=[f"rs{b2}"])
                    S.op("dve", "reciprocal", rs[b2][:], rs[b2][:], R=[f"rs{b2}"], W=[f"rs{b2}"])
                    S.op("dve", "scalar_tensor_tensor", n32[b2][:], self.ps[pa][:, :], gcol[:, 0:1], rs[b2][:], ALU.mult, ALU.mult,
                         R=[("ps", pa), f"rs{b2}", gkey], W=[f"n32{b2}"])
                    S.op("act", "copy", ob[b3][:], n32[b2][:], R=[f"n32{b2}"], W=[f"ob{b3}"])
                    S.dma("pool", dst[ch * 128:(ch + 1) * 128, s * 512:(s + 1) * 512], ob[b3][:], R=[f"ob{b3}"], W=[("mT", isq, ch, s)])
                    if isq == 0:
                        S.op("dve", "tensor_reduce", kms[:, ch, 2 * s:2 * s + 2], n32[b2][:].rearrange("p (a b) -> p a b", a=2), AX.X, ALU.add,
                             R=[f"n32{b2}"], W=["kms"])
                        return
                    gbank = 3 if s < 4 else 5
                    for hh in range(2):
                        for i4 in range(4):
                            g = hh * 4 + i4
                            c0 = (s % 4) * 128 + g * 16
                            S.op("pe", "matmul", self.ps[gbank][:, c0:c0 + 16], n32[b2][hh * 64:(hh + 1) * 64, i4 * 128:(i4 + 1) * 128],
                                 kms[hh * 64:(hh + 1) * 64, ch, :], start=True, stop=True, R=[f"n32{b2}", "kms"], W=[("ps", gbank)], inc=(g == 7))
                    if s != NS - 1:
                        return
                    gfl = gate[:].rearrange("p a b -> p (a b)")
                    afl = alw[:].rearrange("p a b -> p (a b)")
                    S.op("act", "copy", gfl[:, 0:512], self.ps[3][:, :], R=[("ps", 3)], W=["gate"])
                    S.op("act", "copy", gfl[:, 512:1024], self.ps[5][:, :], R=[("ps", 5), "gate"], W=["gate"])
                    S.op("dve", "tensor_tensor", gfl, gfl, fmk[:, 0, :], ALU.add, R=["gate", "fmk"], W=["gate"])
                    for g in range(64):
                        S.op("dve", "max", m8[:, g, :], gate[:, g, :], R=["gate"], W=[("m8", g)])
                    S.op("dve", "tensor_scalar", thr[:], m8[:, :, 2], -1e29, None, ALU.max, R=[("m8", g) for g in range(64)], W=["thr"])
                    S.op("dve", "tensor_tensor", alw[:], gate[:], thr[:].unsqueeze(2).to_broadcast([128, 64, 16]), ALU.is_ge,
                         R=["gate", "thr"], W=["alw"])
                    S.op("dve", "tensor_tensor", afl, afl, fmk[:, 1, :], ALU.max, R=["alw", "fmk"], W=["alw"])
                    S.op("dve", "tensor_scalar", afl, afl, 1.0, -NEG, ALU.subtract, ALU.mult, R=["alw"], W=["alw"])
                    for s2 in range(NS):
                        for hh in range(2):
                            tb = 3 if (2 * s2 + hh) % 2 == 0 else 5
                            for i4 in range(4):
                                S.op("pe", "transpose", self.ps[tb][0:16, i4 * 128:(i4 + 1) * 128], alw[:, s2 * 8 + hh * 4 + i4, :], self.ident_f[:],
                                     R=["alw"], W=[("ps", tb)], inc=(i4 == 3))
                            S.op("act", "copy", mTs[:, hh, s2 * 512:(s2 + 1) * 512], self.ps[tb][0:16, :], R=[("ps", tb)], W=[("mTs", hh, s2)])
                    for hh in range(2):
                        S.dma("pool", mmask[2 * ch + hh, :, :], mTs[:, hh, :], R=[("mTs", hh, s2) for s2 in range(NS)], W=[("mmask", ch, hh)])
                run_pipe(len(its_m), ma, mb)
                S.barrier()
                sub_m.close()
                sub_g = contextlib.ExitStack()
                cur[0] = sub_g

                cwg = sb("cwg", [128, 4, 6], F32)
                gc = I["gdn_conv"][l]
                for kk in range(4):
                    S.dma("sp", cwg[:, kk, :], dap(gc, gc.offset + kk * 768, [[1, 128], [128, 6]]), R=["cwg"], W=["cwg"], allow_slow_non_contiguous=True)
                gpre = [sb(f"gpre{i}", [128, 3 + S_LEN], F32) for i in range(2)]
                gu = [sb(f"gu{i}", [128, S_LEN], F32) for i in range(2)]
                gvb = sb("gvb", [128, S_LEN], BF16)
                for i in range(2):
                    S.op("pool", "memset", gpre[i][:, 0:3], 0.0, W=[f"gpre{i}"])

                def g_proj(ch):
                    cb = ch % 2
                    wA, wAk = ldw(2560 + ch * 128)
                    for s in range(NS):
                        pa = s % 4
                        self.fm_proj(wA, wAk, hT, s, pa)
                        S.op("act", "copy", gpre[cb][:, 3 + s * 512:3 + (s + 1) * 512], self.ps[pa][:, :], R=[("ps", pa)], W=[f"gpre{cb}"])

                def g_post(ch):
                    cb = ch % 2
                    S.op("act", "activation", gu[cb][:], gpre[cb][:, 3:3 + S_LEN], AF.Copy, scale=cwg[:, 3, ch:ch + 1], R=[f"gpre{cb}", "cwg"], W=[f"gu{cb}"])
                    for kk in range(3):
                        S.op("dve", "scalar_tensor_tensor", gu[cb][:], gpre[cb][:, kk:kk + S_LEN], cwg[:, kk, ch:ch + 1], gu[cb][:], ALU.mult, ALU.add,
                             R=[f"gpre{cb}", "cwg", f"gu{cb}"], W=[f"gu{cb}"])
                    S.op("act", "activation", gu[cb][:], gu[cb][:], AF.Silu, R=[f"gu{cb}"], W=[f"gu{cb}"])
                    if ch >= 4:
                        S.op("act", "copy", gvb[:], gu[cb][:], R=[f"gu{cb}"], W=["gvb"])
                        S.dma("pool", gvT[(ch - 4) * 128:(ch - 3) * 128, :], gvb[:], R=["gvb"], W=[("gvT", ch)])
                        return
                    dst = gqT if ch < 2 else gkT
                    qs = 0.125 if ch < 2 else 1.0

                    def la_(s):
                        S.op("act", "activation", sq[s % 2][:], gu[cb][:, s * 512:(s + 1) * 512], AF.Square, R=[f"gu{cb}"], W=[f"sq{s % 2}"])

                    def lb_(s):
                        b2, b3 = s % 2, s % 3
                        S.op("pe", "matmul", self.ps[4 + b2][:, :], self.blk64[:], sq[b2][:], start=True, stop=True, R=[f"sq{b2}"], W=[("ps", 4 + b2)])
                        S.op("act", "activation", rs[b2][:], self.ps[4 + b2][:, :], AF.Sqrt, bias=self.epsc[:], scale=1.0, R=[("ps", 4 + b2)], W=[f"rs{b2}"])
                        S.op("dve", "reciprocal", rs[b2][:], rs[b2][:], R=[f"rs{b2}"], W=[f"rs{b2}"])
                        S.op("dve", "scalar_tensor_tensor", ob[b3][:], gu[cb][:, s * 512:(s + 1) * 512], float(qs), rs[b2][:], ALU.mult, ALU.mult,
                             R=[f"gu{cb}", f"rs{b2}"], W=[f"ob{b3}"])
                        S.dma("pool", dst[(ch % 2) * 128:(ch % 2 + 1) * 128, s * 512:(s + 1) * 512], ob[b3][:], R=[f"ob{b3}"], W=[("gT", ch, s)])
                    run_pipe(NS, la_, lb_)

                for ch in range(7):
                    if ch < 6:
                        g_proj(ch)
                    if ch >= 1:
                        g_post(ch - 1)
                S.barrier()
                sub_g.close()
                cur[0] = sp_
            S.barrier()
            if self.mix_parts is None or "ret" in self.mix_parts:
                self.mix_ret(l, tag, qTr, kTr, rgT, rvd, yT)
                S.barrier()
            if self.mix_parts is None or "moba" in self.mix_parts:
                self.mix_moba(l, tag, mqT, mkT, mvd, mmask, yT)
                S.barrier()
            if self.mix_parts is None or "gdn" in self.mix_parts:
                self.mix_gdn(l, tag, gba, gqT, gkT, gvT, gzd, yT)
                S.barrier()
        with contextlib.ExitStack() as st:
            self.out_proj_residual(st, None, 8, I["w_out"][l], X, Xout, tag, None, src=yT)

    def mix_ret(self, l, tag, qTr, kTr, rgT, rvd, yT):
        nc, S = self.nc, self.S
        with contextlib.ExitStack() as st:
            sb = lambda n, shp, dt: st.enter_context(nc.sbuf_tensor(f"r{n}_{tag}", shp, dt))
            kT = [sb(f"kT{i}", [64, S_LEN], BF16) for i in range(2)]
            vh = [sb(f"vh{i}", [128, NT, 64], BF16) for i in range(2)]
            G = [sb(f"G{i}", [128, 5, 512], F32) for i in range(2)]
            qt = [sb(f"qt{i}", [64, 512], BF16) for i in range(3)]
            rg = [sb(f"rg{i}", [64, 512], F32) for i in range(3)]
            pT = [sb(f"pT{i}", [128, 512], BF16) for i in range(3)]
            sq = [sb(f"sq{i}", [64, 512], F32) for i in range(2)]
            rs = [sb(f"rs{i}", [64, 512], F32) for i in range(2)]
            y1 = [sb(f"y1{i}", [64, 512], F32) for i in range(2)]
            yo = [sb(f"yo{i}", [64, 512], BF16) for i in range(2)]
            rn = self.col_vec(st, f"rn_{tag}", self.I["ret_norm"][l], n=64)
            rd = self.C["ret_decay"]
            PST = (0, 1, 5)
            items = []
            for h in range(4):
                for s in range(NS):
                    js = []
                    for j in range(4 * s + 4):
                        jj = j - 4 * s
                        if jj < 0:
                            c = math.exp(self.lg[h] * (512 * s - 128 * (j + 1)))
                            if c < 1e-30:
                                continue
                            js.append((j, jj, c))
                        else:
                            js.append((j, jj, 1.0))
                    for n_, (j, jj, c) in enumerate(js):
                        items.append((h, s, j, jj, c, n_, len(js)))

            def stage_a(idx):
                h, s, j, jj, c, n_, nn = items[idx]
                hb = h % 2
                it = h * NS + s
                qb = it % 3
                if n_ == 0:
                    if s == 0:
                        S.dma("sp", kT[hb][:], kTr[h * 64:(h + 1) * 64, :], W=[f"kT{hb}"])
                        S.dma("sp", vh[hb][:], dap(rvd, rvd.offset + h * 64, [[256, 128], [128 * 256, NT], [1, 64]]), W=[f"vh{hb}"])
                        S.dma("sp", G[hb][:], dap(rd, rd.offset + h * 5 * 128 * 512, [[512, 128], [128 * 512, 5], [1, 512]]), W=[f"G{hb}"])
                    S.dma("sp", qt[qb][:], qTr[h * 64:(h + 1) * 64, s * 512:(s + 1) * 512], W=[f"qt{qb}"])
                    S.dma("sp", rg[qb][:], rgT[h * 64:(h + 1) * 64, s * 512:(s + 1) * 512], W=[f"rg{qb}"])
                k3 = idx % 3
                pst = PST[k3]
                S.op("pe", "matmul", self.ps[pst][:, :], kT[hb][:, j * 128:(j + 1) * 128], qt[qb][:], start=True, stop=True,
                     R=[f"kT{hb}", f"qt{qb}"], W=[("ps", pst)])
                if jj < 0:
                    S.op("dve", "scalar_tensor_tensor", pT[k3][:], self.ps[pst][:, :], float(c), G[hb][:, 0, :], ALU.mult, ALU.mult,
                         R=[("ps", pst), f"G{hb}"], W=[f"pT{k3}"])
                else:
                    S.op("dve", "tensor_tensor", pT[k3][:], self.ps[pst][:, :], G[hb][:, 1 + jj, :], ALU.mult,
                         R=[("ps", pst), f"G{hb}"], W=[f"pT{k3}"])

            def stage_b(idx):
                h, s, j, jj, c, n_, nn = items[idx]
                hb = h % 2
                it = h * NS + s
                b = it % 2
                qb = it % 3
                po = 2 + b
                k3 = idx % 3
                if n_ == 0:
                    flush(po)
                S.op("pe", "matmul", self.ps[po][0:64, :], vh[hb][:, j, :], pT[k3][:], start=(n_ == 0), stop=(n_ == nn - 1),
                     R=[f"vh{hb}", f"pT{k3}"], W=[("ps", po)], inc=(n_ == nn - 1))
                if n_ == nn - 1:
                    S.op("act", "activation", sq[b][:], self.ps[po][0:64, :], AF.Square, R=[("ps", po)], W=[f"sq{b}"])
                    S.op("pool", "tensor_tensor", y1[b][:], rg[qb][:], rn[0:64, 0:1].to_broadcast([64, 512]), ALU.mult, R=[f"rg{qb}", f"rn_{tag}"], W=[f"y1{b}"])
                    pending.append([3, 0, (h, s, b, po)])

            def finalize(stage, h, s, b, po):
                if stage == 0:
                    S.op("pe", "matmul", self.ps[4][0:64, :], self.ones_f[0:64, 0:64], sq[b][:], start=True, stop=True, R=[f"sq{b}"], W=[("ps", 4)])
                    S.op("act", "activation", rs[b][:], self.ps[4][0:64, :], AF.Sqrt, bias=self.epsc[0:64, :], scale=1.0 / 64, R=[("ps", 4)], W=[f"rs{b}"])
                    pending.append([2, 1, (h, s, b, po)])
                elif 1 <= stage <= 4:
                    q4 = stage - 1
                    S.op("dve", "reciprocal", rs[b][:, q4 * 128:(q4 + 1) * 128], rs[b][:, q4 * 128:(q4 + 1) * 128], R=[f"rs{b}"], W=[f"rs{b}"])
                    pending.append([1, stage + 1, (h, s, b, po)])
                elif stage == 5:
                    S.op("pool", "tensor_tensor", y1[b][:], y1[b][:], rs[b][:], ALU.mult, R=[f"y1{b}", f"rs{b}"], W=[f"y1{b}"])
                    pending.append([3, 6, (h, s, b, po)])
                else:
                    S.op("dve", "tensor_tensor", yo[b][:], self.ps[po][0:64, :], y1[b][:], ALU.mult, R=[("ps", po), f"y1{b}"], W=[f"yo{b}"])
                    S.dma("pool", yT[h * 64:(h + 1) * 64, s * 512:(s + 1) * 512], yo[b][:], R=[f"yo{b}"], W=[("yT", h, s)])

            pending = []

            def flush(po_):
                again = True
                while again:
                    again = False
                    for pnd in list(pending):
                        if pnd[2][3] == po_:
                            pending.remove(pnd)
                            finalize(pnd[1], *pnd[2])
                            again = True

            LA = 2
            for idx in range(len(items) + LA):
                if idx < len(items):
                    stage_a(idx)
                if idx - LA >= 0:
                    stage_b(idx - LA)
                for pnd in list(pending):
                    pnd[0] -= 1
                    if pnd[0] <= 0:
                        pending.remove(pnd)
                        finalize(pnd[1], *pnd[2])
            while pending:
                pnd = pending.pop(0)
                finalize(pnd[1], *pnd[2])

    def mix_moba(self, l, tag, mqT, mkT, mvd, mmask, yT):
        nc, S = self.nc, self.S
        with contextlib.ExitStack() as st:
            sb = lambda n, shp, dt: st.enter_context(nc.sbuf_tensor(f"b{n}_{tag}", shp, dt))
            ka = [sb(f"ka{i}", [80, S_LEN], BF16) for i in range(2)]
            va = [sb(f"va{i}", [128, NT, 65], BF16) for i in range(2)]
            qa = [sb(f"qa{i}", [80, 512], BF16) for i in range(3)]
            pT = [sb(f"pT{i}", [128, 512], BF16) for i in range(3)]
            rec = [sb(f"rec{i}", [128, 512], F32) for i in range(2)]
            bc = [sb(f"bc{i}", [64, 512], F32) for i in range(2)]
            yo = [sb(f"yo{i}", [64, 512], BF16) for i in range(2)]
            PST = (0, 1, 5)
            for i in range(2):
                S.dma("pool", ka[i][64:80, :], self.C["blk_oh"], W=[f"ka_oh{i}"])
                S.op("dve", "memset", va[i][:, :, 64:65], 1.0, W=[f"va1{i}"])
            items = []
            for h in range(8):
                for s in range(NS):
                    nj = 4 * s + 4
                    for j in range(nj):
                        items.append((h, s, j, nj))
            state = {"it": -1}

            def stage_a(idx):
                h, s, j, nj = items[idx]
                hb = h % 2
                it = h * NS + s
                qb = it % 3
                if j == 0:
                    if s == 0:
                        S.dma("sp", ka[hb][0:64, :], mkT[h * 64:(h + 1) * 64, :], W=[f"ka{hb}"])
                        S.dma("sp", va[hb][:, :, 0:64], dap(mvd, mvd.offset + h * 64, [[512, 128], [128 * 512, NT], [1, 64]]), W=[f"va{hb}"])
                    S.dma("sp", qa[qb][0:64, :], mqT[h * 64:(h + 1) * 64, s * 512:(s + 1) * 512], W=[f"qa{qb}"])
                    S.dma("sp", qa[qb][64:80, :], mmask[h, :, s * 512:(s + 1) * 512], W=[f"qm{qb}"])
                k3 = idx % 3
                pst = PST[k3]
                extra = []
                for i in range(4):
                    ti = 4 * s + i
                    if ti == j:
                        extra.append((i, self.bias_t[:, h, 0, :]))
                    elif ti == j + 1:
                        extra.append((i, self.bias_t[:, h, 1, :]))
                    elif ti < j:
                        extra.append((i, self.negt[:]))
                S.op("pe", "matmul", self.ps[pst][:, :], ka[hb][:, j * 128:(j + 1) * 128], qa[qb][:], start=True, stop=(len(extra) == 0),
                     R=[f"ka{hb}", f"ka_oh{hb}", f"qa{qb}", f"qm{qb}"], W=[("ps", pst)], inc=(len(extra) == 0))
                for n_, (i, bt) in enumerate(extra):
                    last = n_ == len(extra) - 1
                    S.op("pe", "matmul", self.ps[pst][:, i * 128:(i + 1) * 128], self.ident_b[:], bt, start=False, stop=last,
                         R=[], W=[("ps", pst)], inc=last)
                S.op("act", "activation", pT[k3][:], self.ps[pst][:, :], AF.Exp, R=[("ps", pst)], W=[f"pT{k3}"])

            def stage_b(idx):
                h, s, j, nj = items[idx]
                hb = h % 2
                it = h * NS + s
                b = it % 2
                po = 2 + b
                k3 = idx % 3
                if j == 0:
                    flush(po)
                S.op("pe", "matmul", self.ps[po][0:65, :], va[hb][:, j, :], pT[k3][:], start=(j == 0), stop=(j == nj - 1),
                     R=[f"va{hb}", f"va1{hb}", f"pT{k3}"], W=[("ps", po)], inc=(j == nj - 1))
                if j == nj - 1:
                    S.op("dve", "tensor_copy", rec[b][64:65, :], self.ps[po][64:65, :], R=[("ps", po)], W=[f"rec{b}"])
                    pending.append([3, 0, (h, s, b, po)])

            def finalize(stage, h, s, b, po):
                if stage == 0:
                    S.op("pe", "matmul", self.ps[4][0:64, :], self.ones_f[64:65, 0:64], rec[b][64:65, :], start=True, stop=True, R=[f"rec{b}"], W=[("ps", 4)])
                    S.op("dve", "tensor_copy", bc[b][:], self.ps[4][0:64, :], R=[("ps", 4)], W=[f"bc{b}"])
                    S.op("dve", "reciprocal", bc[b][:], bc[b][:], R=[f"bc{b}"], W=[f"bc{b}"])
                    pending.append([5, 1, (h, s, b, po)])
                else:
                    S.op("dve", "tensor_tensor", yo[b][:], self.ps[po][0:64, :], bc[b][:], ALU.mult, R=[("ps", po), f"bc{b}"], W=[f"yo{b}"])
                    S.dma("pool", yT[256 + h * 64:256 + (h + 1) * 64, s * 512:(s + 1) * 512], yo[b][:], R=[f"yo{b}"], W=[("yT", h, s)])

            pending = []

            def flush(po_):
                again = True
                while again:
                    again = False
                    for pnd in list(pending):
                        if pnd[2][3] == po_:
                            pending.remove(pnd)
                            finalize(pnd[1], *pnd[2])
                            again = True

            LA = 2
            for idx in range(len(items) + LA):
                if idx < len(items):
                    stage_a(idx)
                if idx - LA >= 0:
                    stage_b(idx - LA)
                for pnd in list(pending):
                    pnd[0] -= 1
                    if pnd[0] <= 0:
                        pending.remove(pnd)
                        finalize(pnd[1], *pnd[2])
            while pending:
                pnd = pending.pop(0)
                finalize(pnd[1], *pnd[2])

    def gen_ret(self, st, l, tag, qTr, kTr, rgT, rvd, yT):
        nc, S = self.nc, self.S
        if True:
            sb = lambda n, shp, dt: st.enter_context(nc.sbuf_tensor(f"r{n}_{tag}", shp, dt))
            kT = [sb(f"kT{i}", [64, S_LEN], BF16) for i in range(2)]
            vh = [sb(f"vh{i}", [128, NT, 64], BF16) for i in range(2)]
            G = [sb(f"G{i}", [128, 5, 512], F32) for i in range(2)]
            qt = [sb(f"qt{i}", [64, 512], BF16) for i in range(3)]
            rg = [sb(f"rg{i}", [64, 512], F32) for i in range(3)]
            pT = [sb(f"pT{i}", [128, 512], BF16) for i in range(3)]
            sq = [sb(f"sq{i}", [64, 512], F32) for i in range(2)]
            rs = [sb(f"rs{i}", [64, 512], F32) for i in range(2)]
            y1 = [sb(f"y1{i}", [64, 512], F32) for i in range(2)]
            yo = [sb(f"yo{i}", [64, 512], BF16) for i in range(2)]
            poc = [sb(f"poc{i}", [64, 512], F32) for i in range(2)]
            rn = self.col_vec(st, f"rn_{tag}", self.I["ret_norm"][l], n=64)
            rd = self.C["ret_decay"]
            PST = (0, 1, 2)
            items = []
            for h in range(4):
                for s in range(NS):
                    js = []
                    for j in range(4 * s + 4):
                        jj = j - 4 * s
                        if jj < 0:
                            c = math.exp(self.lg[h] * (512 * s - 128 * (j + 1)))
                            if c < 1e-30:
                                continue
                            js.append((j, jj, c))
                        else:
                            js.append((j, jj, 1.0))
                    for n_, (j, jj, c) in enumerate(js):
                        items.append((h, s, j, jj, c, n_, len(js)))

            def stage_a(idx):
                h, s, j, jj, c, n_, nn = items[idx]
                hb = h % 2
                it = h * NS + s
                qb = it % 3
                if n_ == 0:
                    if s == 0:
                        S.dma("sp", kT[hb][:], kTr[h * 64:(h + 1) * 64, :], W=[f"kT{hb}"])
                        S.dma("sp", vh[hb][:], dap(rvd, rvd.offset + h * 64, [[256, 128], [128 * 256, NT], [1, 64]]), W=[f"vh{hb}"])
                        S.dma("sp", G[hb][:], dap(rd, rd.offset + h * 5 * 128 * 512, [[512, 128], [128 * 512, 5], [1, 512]]), W=[f"G{hb}"])
                    S.dma("sp", qt[qb][:], qTr[h * 64:(h + 1) * 64, s * 512:(s + 1) * 512], W=[f"qt{qb}"])
                    S.dma("sp", rg[qb][:], rgT[h * 64:(h + 1) * 64, s * 512:(s + 1) * 512], W=[f"rg{qb}"])
                k3 = idx % 3
                pst = PST[idx % len(PST)]
                S.op("pe", "matmul", self.ps[pst][:, :], kT[hb][:, j * 128:(j + 1) * 128], qt[qb][:], start=True, stop=True,
                     R=[f"kT{hb}", f"qt{qb}"], W=[("ps", pst)])
                if jj < 0:
                    S.op("dve", "scalar_tensor_tensor", pT[k3][:], self.ps[pst][:, :], float(c), G[hb][:, 0, :], ALU.mult, ALU.mult,
                         R=[("ps", pst), f"G{hb}"], W=[f"pT{k3}"])
                else:
                    S.op("dve", "tensor_tensor", pT[k3][:], self.ps[pst][:, :], G[hb][:, 1 + jj, :], ALU.mult,
                         R=[("ps", pst), f"G{hb}"], W=[f"pT{k3}"])

            def stage_b(idx):
                h, s, j, jj, c, n_, nn = items[idx]
                hb = h % 2
                it = h * NS + s
                b = it % 2
                qb = it % 3
                po = 3
                k3 = idx % 3
                S.op("pe", "matmul", self.ps[po][0:64, :], vh[hb][:, j, :], pT[k3][:], start=(n_ == 0), stop=(n_ == nn - 1),
                     R=[f"vh{hb}", f"pT{k3}"], W=[("ps", po)], inc=(n_ == nn - 1))
                if n_ == nn - 1:
                    S.op("act", "copy", poc[b][:], self.ps[po][0:64, :], R=[("ps", po)], W=[f"poc{b}"])
                    S.op("act", "activation", sq[b][:], poc[b][:], AF.Square, R=[f"poc{b}"], W=[f"sq{b}"])
                    S.op("pool", "tensor_tensor", y1[b][:], rg[qb][:], rn[0:64, 0:1].to_broadcast([64, 512]), ALU.mult, R=[f"rg{qb}", f"rn_{tag}"], W=[f"y1{b}"])
                    pending.append([3, (h, s, b, po)])

            def finalize(h, s, b, po):
                S.op("pe", "matmul", self.ps[b][0:64, :], self.ones_f[0:64, 0:64], sq[b][:], start=True, stop=True, R=[f"sq{b}"], W=[("ps", b)])
                S.op("act", "activation", rs[b][:], self.ps[b][0:64, :], AF.Sqrt, bias=self.epsc[0:64, :], scale=1.0 / 64, R=[("ps", b)], W=[f"rs{b}"])
                S.op("dve", "reciprocal", rs[b][:], rs[b][:], R=[f"rs{b}"], W=[f"rs{b}"])
                S.op("pool", "tensor_tensor", y1[b][:], y1[b][:], rs[b][:], ALU.mult, R=[f"y1{b}", f"rs{b}"], W=[f"y1{b}"])
                S.op("dve", "tensor_tensor", yo[b][:], poc[b][:], y1[b][:], ALU.mult, R=[f"poc{b}", f"y1{b}"], W=[f"yo{b}"])
                S.dma("pool", yT[h * 64:(h + 1) * 64, s * 512:(s + 1) * 512], yo[b][:], R=[f"yo{b}"], W=[("yT", h, s)])

            pending = []
            LA = 2

            def gen():
                for idx in range(len(items) + LA):
                    if idx < len(items):
                        stage_a(idx)
                    if idx - LA >= 0:
                        stage_b(idx - LA)
                    for pnd in list(pending):
                        pnd[0] -= 1
                        if pnd[0] <= 0:
                            finalize(*pnd[1])
                            pending.remove(pnd)
                    yield
                for pnd in pending:
                    finalize(*pnd[1])
                yield
            return gen()

    def gen_moba(self, st, l, tag, mqT, mkT, mvd, mmask, yT):
        nc, S = self.nc, self.S
        if True:
            sb = lambda n, shp, dt: st.enter_context(nc.sbuf_tensor(f"b{n}_{tag}", shp, dt))
            ka = [sb(f"ka{i}", [80, S_LEN], BF16) for i in range(2)]
            va = [sb(f"va{i}", [128, NT, 65], BF16) for i in range(2)]
            qa = [sb(f"qa{i}", [80, 512], BF16) for i in range(3)]
            pT = [sb(f"pT{i}", [128, 512], BF16) for i in range(3)]
            rec = [sb(f"rec{i}", [128, 512], F32) for i in range(2)]
            bc = [sb(f"bc{i}", [64, 512], F32) for i in range(2)]
            yo = [sb(f"yo{i}", [64, 512], BF16) for i in range(2)]
            poc = [sb(f"poc{i}", [65, 512], F32) for i in range(2)]
            PST = (4, 5, 6)
            for i in range(2):
                S.dma("pool", ka[i][64:80, :], self.C["blk_oh"], W=[f"ka_oh{i}"])
                S.op("dve", "memset", va[i][:, :, 64:65], 1.0, W=[f"va1{i}"])
            items = []
            for h in range(8):
                for s in range(NS):
                    nj = 4 * s + 4
                    for j in range(nj):
                        items.append((h, s, j, nj))
            state = {"it": -1}

            def stage_a(idx):
                h, s, j, nj = items[idx]
                hb = h % 2
                it = h * NS + s
                qb = it % 3
                if j == 0:
                    if s == 0:
                        S.dma("sp", ka[hb][0:64, :], mkT[h * 64:(h + 1) * 64, :], W=[f"ka{hb}"])
                        S.dma("sp", va[hb][:, :, 0:64], dap(mvd, mvd.offset + h * 64, [[512, 128], [128 * 512, NT], [1, 64]]), W=[f"va{hb}"])
                    S.dma("sp", qa[qb][0:64, :], mqT[h * 64:(h + 1) * 64, s * 512:(s + 1) * 512], W=[f"qa{qb}"])
                    S.dma("sp", qa[qb][64:80, :], mmask[h, :, s * 512:(s + 1) * 512], W=[f"qm{qb}"])
                k3 = idx % 3
                pst = PST[idx % len(PST)]
                extra = []
                for i in range(4):
                    ti = 4 * s + i
                    if ti == j:
                        extra.append((i, self.bias_t[:, h, 0, :]))
                    elif ti == j + 1:
                        extra.append((i, self.bias_t[:, h, 1, :]))
                    elif ti < j:
                        extra.append((i, self.negt[:]))
                S.op("pe", "matmul", self.ps[pst][:, :], ka[hb][:, j * 128:(j + 1) * 128], qa[qb][:], start=True, stop=(len(extra) == 0),
                     R=[f"ka{hb}", f"ka_oh{hb}", f"qa{qb}", f"qm{qb}"], W=[("ps", pst)], inc=(len(extra) == 0))
                for n_, (i, bt) in enumerate(extra):
                    last = n_ == len(extra) - 1
                    S.op("pe", "matmul", self.ps[pst][:, i * 128:(i + 1) * 128], self.ident_b[:], bt, start=False, stop=last,
                         R=[], W=[("ps", pst)], inc=last)
                S.op("act", "activation", pT[k3][:], self.ps[pst][:, :], AF.Exp, R=[("ps", pst)], W=[f"pT{k3}"])

            def stage_b(idx):
                h, s, j, nj = items[idx]
                hb = h % 2
                it = h * NS + s
                b = it % 2
                po = 7
                k3 = idx % 3
                S.op("pe", "matmul", self.ps[po][0:65, :], va[hb][:, j, :], pT[k3][:], start=(j == 0), stop=(j == nj - 1),
                     R=[f"va{hb}", f"va1{hb}", f"pT{k3}"], W=[("ps", po)], inc=(j == nj - 1))
                if j == nj - 1:
                    S.op("act", "copy", poc[b][:], self.ps[po][0:65, :], R=[("ps", po)], W=[f"poc{b}"])
                    S.op("dve", "reciprocal", rec[b][64:65, :], poc[b][64:65, :], R=[f"poc{b}"], W=[f"rec{b}"])
                    pending.append([4, (h, s, b, po)])

            def finalize(h, s, b, po):
                S.op("pe", "matmul", self.ps[4 + b][0:64, :], self.ones_f[64:65, 0:64], rec[b][64:65, :], start=True, stop=True, R=[f"rec{b}"], W=[("ps", 4 + b)])
                S.op("act", "copy", bc[b][:], self.ps[4 + b][0:64, :], R=[("ps", 4 + b)], W=[f"bc{b}"])
                S.op("dve", "tensor_tensor", yo[b][:], poc[b][0:64, :], bc[b][:], ALU.mult, R=[f"poc{b}", f"bc{b}"], W=[f"yo{b}"])
                S.dma("pool", yT[256 + h * 64:256 + (h + 1) * 64, s * 512:(s + 1) * 512], yo[b][:], R=[f"yo{b}"], W=[("yT", h, s)])

            pending = []
            LA = 2

            def gen():
                for idx in range(len(items) + LA):
                    if idx < len(items):
                        stage_a(idx)
                    if idx - LA >= 0:
                        stage_b(idx - LA)
                    for pnd in list(pending):
                        pnd[0] -= 1
                        if pnd[0] <= 0:
                            finalize(*pnd[1])
                            pending.remove(pnd)
                    yield
                for pnd in pending:
                    finalize(*pnd[1])
                yield
            return gen()

    def mix_retmoba(self, l, tag, qTr, kTr, rgT, rvd, mqT, mkT, mvd, mmask, yT):
        with contextlib.ExitStack() as st:
            g1 = self.gen_ret(st, l, tag, qTr, kTr, rgT, rvd, yT)
            g2 = self.gen_moba(st, l, tag, mqT, mkT, mvd, mmask, yT)
            live = [g1, g2]
            while live:
                for g, reps in ((g2, 2), (g1, 1)):
                    if g not in live:
                        continue
                    for _ in range(reps):
                        try:
                            next(g)
                        except StopIteration:
                            live.remove(g)
                            break

    def mix_gdn(self, l, tag, gba, gqT, gkT, gvT, gzd, yT):
        nc, S = self.nc, self.S
        I = self.I
        with contextlib.ExitStack() as st:
            sb = lambda n, shp, dt: st.enter_context(nc.sbuf_tensor(f"GD{n}_{tag}", shp, dt))
            qA = sb("qA", [64, 4, S_LEN], BF16); kA = sb("kA", [64, 4, S_LEN], BF16); vA = sb("vA", [64, 4, S_LEN], BF16)
            for (dst, src, key) in ((qA, gqT, "qA"), (kA, gkT, "kA"), (vA, gvT, "vA")):
                for hp in range(2):
                    S.dma("sp", dst[:, 2 * hp:2 * hp + 2, :], dap(src, src.offset + hp * 128 * S_LEN, [[S_LEN, 64], [64 * S_LEN, 2], [1, S_LEN]]), R=[key], W=[key])
            tri = sb("tri", [128, 3, 128], F32)
            tr = self.C["tri"]
            S.dma("sp", tri[:], dap(tr, tr.offset, [[128, 128], [128 * 128, 3], [1, 128]]), W=["tri"])
            U = tri[:, 0, :]; SM = tri[:, 1, :]; LM = tri[:, 2, :]
            onec = sb("onec", [128, 1], F32)
            S.op("dve", "memset", onec[:], 1.0, W=["onec"])
            alog = sb("alog", [128, 4], F32); dtb = sb("dtb", [128, 4], F32)
            S.dma("sp", alog[:], dap(I["gdn_a_log"][l], I["gdn_a_log"][l].offset, [[0, 128], [1, 4]]), W=["alog"])
            S.dma("sp", dtb[:], dap(I["gdn_dt_bias"][l], I["gdn_dt_bias"][l].offset, [[0, 128], [1, 4]]), W=["dtb"])
            gn = sb("gn", [128, 64], F32)
            S.dma("sp", gn[:], dap(I["gdn_norm"][l], I["gdn_norm"][l].offset, [[0, 128], [1, 64]]), W=["gn"])
            A3 = lambda nme: sb(nme, [128, NT, 4], F32)
            beta = A3("beta"); z = A3("z"); az = A3("az"); ld = A3("ld"); gcs = A3("gcs"); gts = A3("gts")
            egc = A3("egc"); ekd = A3("ekd"); egl = A3("egl"); bneg = A3("bneg"); begc = A3("begc")
            gkeys = [("gba", t) for t in range(NT)]
            S.op("act", "activation", beta[:], gba[:, :, 0:4], AF.Sigmoid, R=gkeys, W=["beta"])
            S.op("dve", "tensor_tensor", z[:], gba[:, :, 4:8], dtb[:].unsqueeze(1).to_broadcast([128, NT, 4]), ALU.add, R=gkeys + ["dtb"], W=["z"])
            S.op("act", "activation", az[:], z[:], AF.Abs, R=["z"], W=["az"])
            S.op("act", "activation", az[:], az[:], AF.Exp, scale=-1.0, R=["az"], W=["az"])
            S.op("act", "activation", az[:], az[:], AF.Ln, bias=onec[:], scale=1.0, R=["az", "onec"], W=["az"])
            S.op("dve", "scalar_tensor_tensor", z[:], z[:], 0.0, az[:], ALU.max, ALU.add, R=["z", "az"], W=["z"])
            S.op("act", "activation", alog[:], alog[:], AF.Exp, R=["alog"], W=["alog"])
            S.op("dve", "tensor_scalar", alog[:], alog[:], -1.0, None, ALU.mult, R=["alog"], W=["alog"])
            S.op("dve", "tensor_tensor", ld[:], z[:], alog[:].unsqueeze(1).to_broadcast([128, NT, 4]), ALU.mult, R=["z", "alog"], W=["ld"])
            ldf = ld[:].rearrange("p a b -> p (a b)")
            S.op("pe", "matmul", self.ps[0][:, 0:128], U, ldf, start=True, stop=True, R=["tri", "ld"], W=[("ps", 0)])
            S.op("pe", "matmul", self.ps[1][:, 0:128], self.ones_f[:], ldf, start=True, stop=True, R=["ld"], W=[("ps", 1)])
            S.op("act", "copy", gcs[:].rearrange("p a b -> p (a b)"), self.ps[0][:, 0:128], R=[("ps", 0)], W=["gcs"])
            S.op("act", "copy", gts[:].rearrange("p a b -> p (a b)"), self.ps[1][:, 0:128], R=[("ps", 1)], W=["gts"])
            S.op("act", "activation", egc[:], gcs[:], AF.Exp, R=["gcs"], W=["egc"])
            S.op("act", "activation", egl[:], gts[:], AF.Exp, R=["gts"], W=["egl"])
            S.op("dve", "tensor_tensor", ekd[:], gts[:], gcs[:], ALU.subtract, R=["gts", "gcs"], W=["ekd"])
            S.op("act", "activation", ekd[:], ekd[:], AF.Exp, R=["ekd"], W=["ekd"])
            S.op("dve", "tensor_scalar", bneg[:], beta[:], -1.0, None, ALU.mult, R=["beta"], W=["bneg"])
            S.op("dve", "tensor_tensor", begc[:], beta[:], egc[:], ALU.mult, R=["beta", "egc"], W=["begc"])
            kdec = [sb(f"kdec{i}", [128, 4, 64], BF16) for i in range(4)]
            VB = [sb(f"VB{i}", [128, 4, 64], F32) for i in range(4)]
            qkT = [sb(f"qkT{i}", [128, 4, 128], BF16) for i in range(4)]
            Nfin = [sb(f"Nfin{i}", [128, 4, 128], F32) for i in range(4)]
            oo = [sb(f"oo{i}", [128, 4, 64], F32) for i in range(4)]
            kf = sb("kf", [64, 4, 128], F32); qf = sb("qf", [64, 4, 128], F32); Rr = sb("Rr", [128, 4, 64], F32)
            TS = []
            for q_ in range(2):
                TS.append(dict(
                    gU=sb(f"gU{q_}", [128, 4, 128], F32), Dm=sb(f"Dm{q_}", [128, 4, 128], F32),
                    ES=sb(f"ES{q_}", [128, 4, 128], F32), EL=sb(f"EL{q_}", [128, 4, 128], F32), T1=sb(f"T1{q_}", [128, 4, 128], F32),
                    Bt=[sb(f"Bt{i}s{q_}", [128, 4, 128], F32) for i in range(2)],
                    Bk=[sb(f"Bk{i}s{q_}", [128, 4, 128], F32) for i in range(2)],
                    M2=sb(f"M2{q_}", [128, 4, 128], BF16), qkS=sb(f"qkS{q_}", [128, 4, 128], F32),
                    Nn=[sb(f"Nn{i}s{q_}", [128, 4, 128], F32) for i in range(2)]))
            vn = sb("vn", [128, 4, 64], BF16); o1 = sb("o1", [128, 4, 64], F32); o2 = sb("o2", [128, 4, 64], F32)
            Sf = sb("Sf", [64, 4, 64], F32); St = sb("St", [64, 4, 64], F32)
            osq = sb("osq", [128, 4, 64], F32); oss = sb("oss", [128, 4], F32)
            gzt = [sb(f"gzt{i}", [128, 256], F32) for i in range(8)]
            ytm = sb("ytm", [128, 256], BF16); yst = [sb(f"yst{i}", [128, 2, 128], BF16) for i in range(2)]
            S.op("dve", "memset", Sf[:], 0.0, W=["Sf"])
            v3 = lambda ap, a: ap.rearrange("p (a b) -> p a b", a=a)
            bc3 = lambda ap2, n: ap2.unsqueeze(2).to_broadcast([128, 4, n])
            fl = lambda t: t[:].rearrange("p a b -> p (a b)")

            def prep(n):
                ts_ = slice(n * 128, (n + 1) * 128)
                p = n % 4
                q = n % 2
                T_ = TS[q]
                gU, Dm, ES, EL, T1, Bt, Bk, M2, Nn, qkS = T_["gU"], T_["Dm"], T_["ES"], T_["EL"], T_["T1"], T_["Bt"], T_["Bk"], T_["M2"], T_["Nn"], T_["qkS"]
                S.dma("sp", gzt[n % 8][:], gzd[ts_, :], W=[f"gzt{n % 8}"])
                for h in range(4):
                    S.op("pe", "transpose", self.psb[0][:, h * 64:(h + 1) * 64], kA[:, h, ts_], self.ident_b[0:64, 0:64], R=["kA"], W=[("ps", 6)], inc=False)
                for h in range(4):
                    S.op("pe", "transpose", self.psb[0][:, 256 + h * 64:256 + (h + 1) * 64], vA[:, h, ts_], self.ident_b[0:64, 0:64], R=["vA"], W=[("ps", 6)], inc=(h == 3))
                ktm = v3(self.psb[0][:, 0:256], 4); vtm = v3(self.psb[0][:, 256:512], 4)
                S.op("dve", "tensor_tensor", kdec[p][:], ktm, bc3(ekd[:, n, :], 64), ALU.mult, R=[("ps", 6), "ekd"], W=[f"kdec{p}"])
                S.op("dve", "tensor_tensor", VB[p][:], vtm, bc3(beta[:, n, :], 64), ALU.mult, R=[("ps", 6), "beta"], W=[f"VB{p}"])
                yield
                for h in range(4):
                    S.op("pe", "matmul", self.ps[0][:, h * 128:(h + 1) * 128], kA[:, h, ts_], kA[:, h, ts_], start=True, stop=True, R=["kA"], W=[("ps", 0)], inc=(h == 3))
                for h in range(4):
                    S.op("pe", "matmul", self.ps[1][:, h * 128:(h + 1) * 128], qA[:, h, ts_], kA[:, h, ts_], start=True, stop=True, R=["kA", "qA"], W=[("ps", 1)], inc=(h == 3))
                S.op("dve", "tensor_tensor", T1[:], v3(self.ps[0][:, :], 4), bc3(bneg[:, n, :], 128), ALU.mult, R=[("ps", 0), "bneg"], W=[f"T1_{q}"])
                S.op("act", "copy", fl(qkS), self.ps[1][:, :], R=[("ps", 1)], W=[f"qkS_{q}"])
                S.op("pool", "tensor_tensor", gU[:], U.unsqueeze(1).to_broadcast([128, 4, 128]), bc3(ld[:, n, :], 128), ALU.mult, R=["tri", "ld"], W=[f"gU_{q}"])
                yield
                S.op("pe", "matmul", self.ps[2][:, :], self.ones_f[:], fl(gU), start=True, stop=True, R=[f"gU_{q}"], W=[("ps", 2)])
                S.op("dve", "scalar_tensor_tensor", Dm[:], v3(self.ps[2][:, :], 4), -1.0, bc3(gcs[:, n, :], 128), ALU.mult, ALU.add,
                     R=[("ps", 2), "gcs"], W=[f"Dm_{q}"])
                yield
                S.op("dve", "tensor_scalar", Dm[:], Dm[:], 0.0, None, ALU.min, R=[f"Dm_{q}"], W=[f"Dm_{q}"])
                S.op("act", "activation", Dm[:], Dm[:], AF.Exp, R=[f"Dm_{q}"], W=[f"Dm_{q}"])
                yield
                S.op("pool", "tensor_tensor", ES[:], Dm[:], SM.unsqueeze(1).to_broadcast([128, 4, 128]), ALU.mult, R=[f"Dm_{q}", "tri"], W=[f"ES_{q}"])
                S.op("pool", "tensor_tensor", EL[:], Dm[:], LM.unsqueeze(1).to_broadcast([128, 4, 128]), ALU.mult, R=[f"Dm_{q}", "tri"], W=[f"EL_{q}"])
                yield
                S.op("dve", "tensor_tensor", Bt[0][:], T1[:], ES[:], ALU.mult, R=[f"T1_{q}", f"ES_{q}"], W=[f"Bt0_{q}"])
                S.op("dve", "tensor_tensor", M2[:], qkS[:], EL[:], ALU.mult, R=[f"qkS_{q}", f"EL_{q}"], W=[f"M2_{q}"])
                yield
                for h in range(4):
                    S.op("pe", "transpose", self.ps[3][:, h * 128:(h + 1) * 128], Bt[0][:, h, :], self.ident_f[:], R=[f"Bt0_{q}"], W=[("ps", 3)], inc=(h == 3))
                for h in range(4):
                    S.op("pe", "transpose", self.psb[1][:, 512 + h * 128:512 + (h + 1) * 128], M2[:, h, :], self.ident_b[:], R=[f"M2_{q}"], W=[("ps", 7)], inc=(h == 3))
                S.op("act", "copy", fl(Bk[0]), self.ps[3][:, :], R=[("ps", 3)], W=[f"Bk0_{q}"])
                S.op("act", "copy", fl(qkT[p]), self.psb[1][:, 512:1024], R=[("ps", 7)], W=[f"qkT{p}"])
                yield
                S.op("dve", "tensor_tensor", Nn[0][:], Bk[0][:], self.ident_f[:].unsqueeze(1).to_broadcast([128, 4, 128]), ALU.add, R=[f"Bk0_{q}"], W=[f"Nn0_{q}"])
                cur = 0
                LB = self.gdn_lb
                for lev in range(1, 7):
                    nx = 1 - cur
                    pN, pNk = Nn[(lev - 1) % 2], f"Nn{(lev - 1) % 2}_{q}"
                    if lev < 6:
                        cN, cNk = Nn[lev % 2], f"Nn{lev % 2}_{q}"
                    else:
                        cN, cNk = Nfin[p], f"Nfin{p}"
                    lo = lev >= LB
                    nlo = (lev + 1) >= LB
                    sBk, sBt = (Bkb[cur], Btb[cur]) if lo else (Bk[cur], Bt[cur])
                    kk_, kt_ = (f"Bkb{cur}", f"Btb{cur}") if lo else (f"Bk{cur}_{q}", f"Bt{cur}_{q}")
                    for h in range(4):
                        S.op("pe", "matmul", self.ps[1][:, h * 128:(h + 1) * 128], sBk[:, h, :], sBt[:, h, :], start=True, stop=True,
                             R=[kt_, kk_], W=[("ps", 1)], inc=(h == 3))
                    if lev < 6:
                        for h in range(4):
                            S.op("pe", "matmul", self.ps[0][:, h * 128:(h + 1) * 128], sBt[:, h, :], sBk[:, h, :], start=True, stop=True,
                                 R=[kt_, kk_], W=[("ps", 0)], inc=(h == 3))
                    if lo:
                        S.op("dve", "tensor_copy", fl(Btb[nx]), self.ps[1][:, :], R=[("ps", 1)], W=[f"Btb{nx}"])
                    else:
                        S.op("dve", "tensor_copy", fl(Bt[nx]), self.ps[1][:, :], R=[("ps", 1)], W=[f"Bt{nx}_{q}"])
                        if nlo and lev < 6:
                            S.op("pool", "tensor_copy", fl(Btb[nx]), fl(Bt[nx]), R=[f"Bt{nx}_{q}"], W=[f"Btb{nx}"])
                    if lev < 6:
                        if nlo:
                            S.op("act", "copy", fl(Bkb[nx]), self.ps[0][:, :], R=[("ps", 0)], W=[f"Bkb{nx}"])
                        else:
                            S.op("act", "copy", fl(Bk[nx]), self.ps[0][:, :], R=[("ps", 0)], W=[f"Bk{nx}_{q}"])
                    if lo:
                        S.op("pool", "tensor_copy", fl(Nb), fl(pN), R=[pNk], W=["Nb"])
                    yield
                    for h in range(4):
                        if lo:
                            S.op("pe", "matmul", self.ps[2][:, h * 128:(h + 1) * 128], Btb[nx][:, h, :], Nb[:, h, :], start=True, stop=True,
                                 R=["Nb", f"Btb{nx}"], W=[("ps", 2)], inc=(h == 3))
                        else:
                            S.op("pe", "matmul", self.ps[2][:, h * 128:(h + 1) * 128], Bt[nx][:, h, :], pN[:, h, :], start=True, stop=True,
                                 R=[pNk, f"Bt{nx}_{q}"], W=[("ps", 2)], inc=(h == 3))
                    S.op("dve", "tensor_tensor", fl(cN), self.ps[2][:, :], fl(pN), ALU.add, R=[("ps", 2), pNk], W=[cNk])
                    yield
                    cur = nx

            def rec(n):
                ts_ = slice(n * 128, (n + 1) * 128)
                p = n % 4
                S.op("act", "copy", kf[:], kA[:, :, ts_], R=["kA"], W=["kf"])
                S.op("pool", "tensor_copy", qf[:], qA[:, :, ts_], R=["qA"], W=["qf"])
                for h in range(4):
                    S.op("pe", "matmul", self.ps[5][:, h * 64:(h + 1) * 64], kf[:, h, :], Sf[:, h, :], start=True, stop=True, R=["kf", "Sf"], W=[("ps", 5)], inc=(h == 3))
                for h in range(4):
                    S.op("pe", "matmul", self.ps[4][:, h * 64:(h + 1) * 64], qf[:, h, :], Sf[:, h, :], start=True, stop=True, R=["qf", "Sf"], W=[("ps", 4)], inc=(h == 3))
                yield
                S.op("dve", "tensor_tensor", o1[:], v3(self.ps[5][:, 0:256], 4), bc3(begc[:, n, :], 64), ALU.mult, R=[("ps", 5), "begc"], W=["o1"])
                S.op("dve", "tensor_tensor", Rr[:], VB[p][:], o1[:], ALU.subtract, R=[f"VB{p}", "o1"], W=["Rr"])
                yield
                for h in range(4):
                    S.op("pe", "matmul", self.ps[5][:, h * 64:(h + 1) * 64], Nfin[p][:, h, :], Rr[:, h, :], start=True, stop=True, R=[f"Nfin{p}", "Rr"], W=[("ps", 5)], inc=(h == 3))
                S.op("act", "copy", fl(vn), self.ps[5][:, 0:256], R=[("ps", 5)], W=["vn"])
                yield
                for h in range(4):
                    S.op("pe", "matmul", self.ps[5][0:64, 256 + h * 64:256 + (h + 1) * 64], kdec[p][:, h, :], vn[:, h, :], start=True, stop=True, R=[f"kdec{p}", "vn"], W=[("ps", 5)], inc=(h == 3))
                for h in range(4):
                    S.op("pe", "matmul", self.ps[4][:, 256 + h * 64:256 + (h + 1) * 64], qkT[p][:, h, :], vn[:, h, :], start=True, stop=True, R=[f"qkT{p}", "vn"], W=[("ps", 4)], inc=(h == 3))
                S.op("pool", "tensor_tensor", St[:], Sf[:], egl[0:64, n, :].unsqueeze(2).to_broadcast([64, 4, 64]), ALU.mult, R=["Sf", "egl"], W=["St"])
                yield
                S.op("dve", "tensor_tensor", Sf[:], St[:], v3(self.ps[5][0:64, 256:512], 4), ALU.add, R=["St", ("ps", 5)], W=["Sf"])
                S.op("dve", "tensor_tensor", o2[:], v3(self.ps[4][:, 0:256], 4), bc3(egc[:, n, :], 64), ALU.mult, R=[("ps", 4), "egc"], W=["o2"])
                S.op("dve", "tensor_tensor", oo[p][:], o2[:], v3(self.ps[4][:, 256:512], 4), ALU.add, R=["o2", ("ps", 4)], W=[f"oo{p}"])
                yield

            def post(n):
                p = n % 4
                S.op("pool", "tensor_tensor", osq[:], oo[p][:], oo[p][:], ALU.mult, R=[f"oo{p}"], W=["osq"])
                yield
                S.op("dve", "tensor_reduce", oss[:], osq[:], AX.X, ALU.add, R=["osq"], W=["oss"])
                S.op("act", "activation", oss[:], oss[:], AF.Sqrt, bias=self.epsc[:], scale=1.0 / 64, R=["oss"], W=["oss"])
                yield
                S.op("dve", "reciprocal", oss[:], oss[:], R=["oss"], W=["oss"])
                S.op("dve", "tensor_tensor", osq[:], oo[p][:], bc3(oss[:], 64), ALU.mult, R=[f"oo{p}", "oss"], W=["osq"])
                yield
                S.op("pool", "tensor_tensor", osq[:], osq[:], gn[:].unsqueeze(1).to_broadcast([128, 4, 64]), ALU.mult, R=["osq", "gn"], W=["osq"])
                yield
                S.op("dve", "tensor_tensor", ytm[:], fl(osq), gzt[n % 8][:], ALU.mult, R=["osq", f"gzt{n % 8}"], W=["ytm"])
                for cp in range(2):
                    S.op("pe", "transpose", self.psb[0][:, 512 + cp * 128:512 + (cp + 1) * 128], ytm[:, cp * 128:(cp + 1) * 128], self.ident_b[:], R=["ytm"], W=[("ps", 6)], inc=(cp == 1))
                yield
                S.op("act", "copy", fl(yst[n % 2]), self.psb[0][:, 512:768], R=[("ps", 6)], W=[f"yst{n % 2}"])
                S.dma("pool", dap(yT, yT.offset + 768 * S_LEN + n * 128, [[S_LEN, 128], [128 * S_LEN, 2], [1, 128]]), yst[n % 2][:], R=[f"yst{n % 2}"], W=[("yTg", n)])
                yield

            def interleave(gens):
                gens = [g for g in gens if g is not None]
                while gens:
                    for g in list(gens):
                        try:
                            next(g)
                        except StopIteration:
                            gens.remove(g)

            def chain(gens):
                for g in gens:
                    yield from g

            NP = NT // 2
            for r in range(NP + 2):
                recs = chain([rec(n) for n in (2 * r - 2, 2 * r - 1) if 0 <= n < NT])
                posts = chain([post(n) for n in (2 * r - 4, 2 * r - 3) if 0 <= n < NT])
                interleave([recs, posts,
                            prep(2 * r) if 2 * r < NT else None,
                            prep(2 * r + 1) if 2 * r + 1 < NT else None])

    def sub_ffn(self, l, X, Xout):
        nc, S = self.nc, self.S
        tag = f"f{l}"
        actT = self.scratch(f"actT{l}", [DFF, S_LEN], BF16)
        with contextlib.ExitStack() as st:
            hT = st.enter_context(nc.sbuf_tensor(f"hT_{tag}", [128, 8, S_LEN], BF16))
            with contextlib.ExitStack() as st2:
                self.norm_T(st2, X, S_LEN, self.I["norm_ffn"][l], hT, "hT", tag)
            S.barrier()
            with contextlib.ExitStack() as st2:
                sb = lambda n, s, d: st2.enter_context(nc.sbuf_tensor(f"{n}_{tag}", s, d))
                cw = sb("cw", [128, 3, 2 * NFC], F32)
                cb = sb("cb", [128, 2 * NFC], F32)
                fc = self.I["ffn_conv"][l]
                for kk in range(3):
                    S.dma("sp", cw[:, kk, :], dap(fc, fc.offset + kk * 2 * DFF, [[1, 128], [128, 2 * NFC]]), R=["cw"], W=["cw"], allow_slow_non_contiguous=True)
                fb = self.I["ffn_conv_b"][l]
                S.dma("sp", cb[:], dap(fb, fb.offset, [[1, 128], [128, 2 * NFC]]), W=["cb"], allow_slow_non_contiguous=True)
                wt = [sb(f"wu{i}", [128, 8, 256], BF16) for i in range(2)]
                pre = [sb(f"pre{i}", [128, 2 + S_LEN], F32) for i in range(2)]
                uu = [sb(f"uu{i}", [128, S_LEN], F32) for i in range(2)]
                acto = sb("acto", [128, S_LEN], BF16)
                for i in range(2):
                    S.op("pool", "memset", pre[i][:, 0:2], 0.0, W=[f"pre{i}"])
                Wu = self.I["ffn_up"][l]
                for c in range(NFC):
                    b = c % 2
                    self.S.dma("pool", wt[b][:, :, 0:128], dap(Wu, Wu.offset + c * 128, [[2 * DFF, 128], [128 * 2 * DFF, 8], [1, 128]]), W=[f"wu{b}g"])
                    self.S.dma("pool", wt[b][:, :, 128:256], dap(Wu, Wu.offset + DFF + c * 128, [[2 * DFF, 128], [128 * 2 * DFF, 8], [1, 128]]), W=[f"wu{b}v"])
                    for gv in range(2):
                        wk = f"wu{b}g" if gv == 0 else f"wu{b}v"
                        for s in range(NS):
                            pi = self.nps()
                            for k in range(8):
                                S.op("pe", "matmul", self.ps[pi][:, :], wt[b][:, k, gv * 128:(gv + 1) * 128], hT[:, k, s * 512:(s + 1) * 512],
                                     start=(k == 0), stop=(k == 7), R=[wk] + [("hT", 4 * s + i) for i in range(4)], W=[("ps", pi)], inc=(k == 7))
                            S.op("act", "copy", pre[gv][:, 2 + s * 512:2 + (s + 1) * 512], self.ps[pi][:, :], R=[("ps", pi)], W=[f"pre{gv}"])
                        ci = gv * NFC + c
                        S.op("act", "activation", uu[gv][:], pre[gv][:, 2:2 + S_LEN], AF.Identity, bias=cb[:, ci:ci + 1], scale=cw[:, 2, ci:ci + 1],
                             R=[f"pre{gv}", "cw", "cb"], W=[f"uu{gv}"])
                        eng = "dve"
                        S.op(eng, "scalar_tensor_tensor", uu[gv][:], pre[gv][:, 1:1 + S_LEN], cw[:, 1, ci:ci + 1], uu[gv][:], ALU.mult, ALU.add,
                             R=[f"pre{gv}", "cw", f"uu{gv}"], W=[f"uu{gv}"])
                        S.op(eng, "scalar_tensor_tensor", uu[gv][:], pre[gv][:, 0:S_LEN], cw[:, 0, ci:ci + 1], uu[gv][:], ALU.mult, ALU.add,
                             R=[f"pre{gv}", "cw", f"uu{gv}"], W=[f"uu{gv}"])
                    S.op("act", "activation", uu[0][:], uu[0][:], AF.Silu, R=["uu0"], W=["uu0"])
                    S.op("dve", "tensor_tensor", acto[:], uu[0][:], uu[1][:], ALU.mult, R=["uu0", "uu1"], W=["acto"])
                    S.dma("sp", actT[c * 128:(c + 1) * 128, :], acto[:], R=["acto"], W=[("actT", c)])
            S.barrier()
        with contextlib.ExitStack() as st:
            self.out_proj_residual(st, None, NFC, self.I["ffn_down"][l], X, Xout, tag, None, src=actT)


def build_program(debug=False, stages=None, nlayers=2):
    b = Builder(debug=debug, stages=stages, nlayers=nlayers)
    nc = b.build()
    return nc, b


_CACHE = {}


def kernel(**inputs):
    if "nc" not in _CACHE:
        _CACHE["nc"], _ = build_program()
        _CACHE["consts"] = make_consts()
    nc = _CACHE["nc"]
    consts = _CACHE["consts"]
    in_maps = []
    for b in range(8):
        m = {}
        for k in W_SHAPES:
            a = np.asarray(inputs[k], dtype=np.float32)
            if k in ("x", "mem"):
                a = a[b]
            m[k] = np.ascontiguousarray(a)
        for k in CONST_SHAPES:
            m["c_" + k] = np.ascontiguousarray(consts[k].astype(np.float32))
        in_maps.append(m)
    res = run_bass_kernel_spmd(nc, in_maps, core_ids=list(range(8)))
    return np.stack([np.asarray(r["y"], dtype=np.float32) for r in res.results], 0)
```

```python
import contextlib
import math
import numpy as np
import concourse.bass as bass
import concourse.mybir as mybir
from concourse.bass_utils import run_bass_kernel_spmd

F32 = mybir.dt.float32
BF16 = mybir.dt.bfloat16
AF = mybir.ActivationFunctionType
ALU = mybir.AluOpType
AX = mybir.AxisListType

S_LEN = 4096
D = 1024
NT = S_LEN // 128
NS = S_LEN // 512
MEM = 256
DFF = 2816
NFC = DFF // 128
IN_COLS = 3592
EPS = 1e-6
NEG = -30000.0
NRING = 32
EPOCH = 30000


class Sched:
    ENG = ("pe", "act", "dve", "pool", "sp")

    def __init__(self, nc, stack):
        self.nc = nc
        self.stack = stack
        self.h = {"pe": nc.tensor, "act": nc.scalar, "dve": nc.vector, "pool": nc.gpsimd, "sp": nc.sync}
        self.prog = {e: [] for e in self.ENG}
        self.cnt = {e: 0 for e in self.ENG}
        self.epoch = {e: 0 for e in self.ENG}
        self.sems = {}
        self.waited = {e: {} for e in self.ENG}
        self.lastw = {}
        self.readers = {}
        self.rq = {"sp": 0, "pool": 1, "act": 2}
        self.ring = [stack.enter_context(nc.semaphore(f"dq{i}")) for i in range(3 * NRING)]
        self.ring_cnt = [0] * (3 * NRING)
        self.dma_next = [0, 0, 0]
        self.nsem = 0
        self.ninst = 0
        for e in self.ENG:
            self._new_sem(e)

    def _new_sem(self, e):
        self.nsem += 1
        s = self.stack.enter_context(self.nc.semaphore(f"s_{e}_{self.nsem}"))
        self.sems[(e, self.epoch[e])] = s

    def _wait(self, eng, semkey, v):
        if self.waited[eng].get(semkey, 0) >= v:
            return
        self.waited[eng][semkey] = v
        sh = self.ring[semkey[1]] if semkey[0] == "d" else self.sems[(semkey[1], semkey[2])]
        self.prog[eng].append(("wait_ge", (sh, v), {}, None))

    def _deps(self, eng, reads, writes):
        toks = []
        for k in reads:
            t = self.lastw.get(k)
            if t is not None:
                toks.append(t)
        for k in writes:
            t = self.lastw.get(k)
            if t is not None:
                toks.append(t)
            toks.extend(self.readers.get(k, ()))
        for (sk, v) in toks:
            if sk[0] == "e" and sk[1] == eng and eng == "pe":
                continue
            self._wait(eng, sk, v)

    def _commit(self, tok, reads, writes):
        for k in reads:
            lst = self.readers.setdefault(k, [])
            lst[:] = [t for t in lst if t[0] != tok[0]]
            lst.append(tok)
        for k in writes:
            self.lastw[k] = tok
            self.readers[k] = []

    def op(self, eng, name, *args, R=(), W=(), inc=True, **kw):
        self._deps(eng, R, W)
        self.ninst += 1
        if inc:
            if self.cnt[eng] >= EPOCH:
                self.epoch[eng] += 1
                self.cnt[eng] = 0
                self._new_sem(eng)
            self.cnt[eng] += 1
            sh = self.sems[(eng, self.epoch[eng])]
            self.prog[eng].append((name, args, kw, sh))
            tok = (("e", eng, self.epoch[eng]), self.cnt[eng])
        else:
            self.prog[eng].append((name, args, kw, None))
            if self.cnt[eng] >= EPOCH:
                tok = (("e", eng, self.epoch[eng] + 1), 1)
            else:
                tok = (("e", eng, self.epoch[eng]), self.cnt[eng] + 1)
        self._commit(tok, R, W)
        return tok

    def dma(self, q, out, in_, R=(), W=(), **kw):
        self._deps(q, R, W)
        self.ninst += 1
        qi = self.rq[q]
        slot = qi * NRING + self.dma_next[qi] % NRING
        self.dma_next[qi] += 1
        if self.ring_cnt[slot] > 0:
            self._wait(q, ("d", slot), 16 * self.ring_cnt[slot])
        self.ring_cnt[slot] += 1
        v = 16 * self.ring_cnt[slot]
        kw = dict(kw)
        kw["out"] = out
        kw["in_"] = in_
        self.prog[q].append(("dma_start", (), kw, (self.ring[slot], 16)))
        tok = (("d", slot), v)
        self._commit(tok, R, W)
        return tok

    def barrier(self):
        for f in self.ENG:
            for e in self.ENG:
                if e != f and self.cnt[e] > 0:
                    self._wait(f, ("e", e, self.epoch[e]), self.cnt[e])
            for s in range(3 * NRING):
                if self.ring_cnt[s] > 0:
                    self._wait(f, ("d", s), 16 * self.ring_cnt[s])
        self.lastw.clear()
        self.readers.clear()

    @staticmethod
    def _emit(h, lst):
        for (name, args, kw, inc) in lst:
            ins = getattr(h, name)(*args, **kw)
            if inc is not None:
                if isinstance(inc, tuple):
                    ins.then_inc(inc[0], inc[1])
                else:
                    ins.then_inc(inc, 1)

    def finish(self):
        self.barrier()
        with self.nc.Block() as block:
            @block.tensor
            def _(e):
                self._emit(self.h["pe"], self.prog["pe"])

            @block.scalar
            def _(e):
                self._emit(self.h["act"], self.prog["act"])

            @block.vector
            def _(e):
                self._emit(self.h["dve"], self.prog["dve"])

            @block.gpsimd
            def _(e):
                self._emit(self.h["pool"], self.prog["pool"])

            @block.sync
            def _(e):
                self._emit(self.h["sp"], self.prog["sp"])


def _t5_bucket(n):
    n = np.maximum(n, 0)
    exact = 16
    nf = np.maximum(n, exact).astype(np.float32)
    large = exact + (np.log(nf / np.float32(exact)) / np.float32(math.log(128 / exact)) * np.float32(16)).astype(np.int32)
    return np.where(n < exact, n, np.minimum(large, 31))


def make_consts():
    c = {}
    c["ident"] = np.eye(128, dtype=np.float32)
    half = 32
    inv_freq = (10000.0 ** (-np.arange(half, dtype=np.float32) / half)).astype(np.float32)
    pos = np.arange(S_LEN, dtype=np.float32)
    ang = pos[None, :] * inv_freq[:, None]
    cos = np.cos(ang).astype(np.float32)
    sin = np.sin(ang).astype(np.float32)
    cos64 = np.concatenate([cos, cos], 0)
    sin64 = np.concatenate([-sin, sin], 0)
    c["rope"] = np.stack([np.concatenate([cos64, cos64], 0), np.concatenate([sin64, sin64], 0)], 0).astype(np.float32)
    lg = np.log1p(-np.exp2(-5.0 - np.arange(4, dtype=np.float64)))
    kl = np.arange(128)[:, None]
    ql = np.arange(512)[None, :]
    dec = np.zeros((4, 5, 128, 512), np.float32)
    for h in range(4):
        dec[h, 0] = np.exp(lg[h] * (ql - kl + 128))
        for jj in range(4):
            e = ql - kl - 128 * jj
            dec[h, 1 + jj] = np.where(e >= 0, np.exp(lg[h] * np.maximum(e, 0)), 0.0)
    c["ret_decay"] = dec
    c["ret_lg"] = lg
    oh = np.zeros((16, S_LEN), np.float32)
    for n in range(16):
        oh[n, n * 256:(n + 1) * 256] = 1.0
    c["blk_oh"] = oh
    t5 = np.zeros((33, 383), np.float32)
    for i in range(383):
        dl = i - 127
        if dl >= 0:
            t5[int(_t5_bucket(np.array(dl))), i] += 1.0
            t5[31, i] -= 1.0
        else:
            t5[32, i] = NEG
    c["t5oh"] = t5
    fm = np.zeros((2, 8, 2, 4, 16), np.float32)
    for s_ in range(8):
        for i4 in range(4):
            blk = (4 * s_ + i4) // 2
            fm[0, s_, :, i4, blk:] = -1e30
            fm[1, s_, :, i4, blk] = 1.0
    c["moba_fm"] = fm.reshape(2, 1024)
    i = np.arange(128)[:, None]
    j = np.arange(128)[None, :]
    c["tri"] = np.stack([(i <= j), (i > j), (i >= j)], 0).astype(np.float32)
    return c


CONST_SHAPES = {"ident": [128, 128], "rope": [2, 128, S_LEN], "ret_decay": [4, 5, 128, 512],
                "blk_oh": [16, S_LEN], "t5oh": [33, 383], "tri": [3, 128, 128], "moba_fm": [2, 1024]}

W_SHAPES = {
    "x": [S_LEN, D], "mem": [MEM, D], "norm_mix": [2, D], "w_in": [2, D, IN_COLS], "ret_norm": [2, 64],
    "moba_q_norm": [2, 64], "moba_k_norm": [2, 64], "gdn_conv": [2, 4, 768], "gdn_a_log": [2, 4],
    "gdn_dt_bias": [2, 4], "gdn_norm": [2, 64], "w_out": [2, D, D], "norm_cross": [2, D], "norm_mem": [2, D],
    "cross_wq": [2, D, 512], "cross_wkv": [2, D, D], "cross_q_norm": [2, 128], "cross_k_norm": [2, 128],
    "cross_wo": [2, 512, D], "norm_ffn": [2, D], "ffn_up": [2, D, 2 * DFF], "ffn_conv": [2, 3, 2 * DFF],
    "ffn_conv_b": [2, 2 * DFF], "ffn_down": [2, DFF, D], "rel_bias": [8, 32],
}


def dap(t, off, pat):
    return bass.AP(t.tensor if hasattr(t, "tensor") else t, off, pat)


class Builder:
    def __init__(self, debug=False, stages=None, nlayers=2):
        self.debug = debug
        self.stages = stages
        self.nlayers = nlayers
        self.nc = bass.Bass("TRN2", target_bir_lowering=False)
        nc = self.nc
        self.I = {k: nc.dram_tensor(k, s, F32, kind="ExternalInput").ap() for k, s in W_SHAPES.items()}
        self.C = {k: nc.dram_tensor("c_" + k, s, F32, kind="ExternalInput").ap() for k, s in CONST_SHAPES.items()}
        self.y = nc.dram_tensor("y", [S_LEN, D], F32, kind="ExternalOutput").ap()
        self.dbg = {}
        self.lg = make_consts()["ret_lg"]
        self.mix_parts = None
        self.gdn_lb = 7

    def scratch(self, name, shape, dt):
        kind = "ExternalOutput" if (self.debug and name in self.debug) else "Internal"
        t = self.nc.dram_tensor(name, shape, dt, kind=kind).ap()
        return t

    def build(self):
        nc = self.nc
        with contextlib.ExitStack() as st:
            self.S = S = Sched(nc, st)
            self.ps = [st.enter_context(nc.psum_tensor(f"ps{i}", [128, 512], F32)) for i in range(8)]
            self.psb = [self.ps[6 + i][:].bitcast(BF16) for i in range(2)]
            self.psi = 0
            self.ident_f = st.enter_context(nc.sbuf_tensor("ident_f", [128, 128], F32))
            self.ident_b = st.enter_context(nc.sbuf_tensor("ident_b", [128, 128], BF16))
            self.ones_f = st.enter_context(nc.sbuf_tensor("ones_f", [128, 128], F32))
            self.ones_b = st.enter_context(nc.sbuf_tensor("ones_b", [128, 128], BF16))
            self.blk64 = st.enter_context(nc.sbuf_tensor("blk64", [128, 128], F32))
            self.epsc = st.enter_context(nc.sbuf_tensor("epsc", [128, 1], F32))
            self.nhalf = st.enter_context(nc.sbuf_tensor("nhalf", [128, 512], F32))
            self.none_ = st.enter_context(nc.sbuf_tensor("none_", [128, 512], F32))
            S.dma("sp", self.ident_f[:], self.C["ident"], W=["ident_f"])
            S.op("dve", "tensor_copy", self.ident_b[:], self.ident_f[:], R=["ident_f"], W=["ident_b"])
            S.op("dve", "memset", self.ones_f[:], 1.0, W=["ones_f"])
            S.op("dve", "memset", self.ones_b[:], 1.0, W=["ones_b"])
            S.op("dve", "memset", self.blk64[:], 0.0, W=["blk64"])
            S.op("dve", "memset", self.blk64[0:64, 0:64], 1.0, R=["blk64"], W=["blk64"])
            S.op("dve", "memset", self.blk64[64:128, 64:128], 1.0, R=["blk64"], W=["blk64"])
            S.op("dve", "memset", self.epsc[:], EPS, W=["epsc"])
            S.op("dve", "memset", self.nhalf[:], -0.5, W=["nhalf"])
            S.op("dve", "memset", self.none_[:], -1.0, W=["none_"])
            self.CK = ["ident_f", "ident_b", "ones_f", "ones_b", "blk64", "epsc"]
            S.barrier()
            self.prep_t5(st)

            xs = [self.I["x"]]
            n_sub = 0
            for l in range(self.nlayers):
                for kind in ("mix", "cross", "ffn"):
                    if self.stages is not None and (l, kind) not in self.stages:
                        continue
                    n_sub += 1
            k = 0
            for l in range(self.nlayers):
                for kind in ("mix", "cross", "ffn"):
                    if self.stages is not None and (l, kind) not in self.stages:
                        continue
                    k += 1
                    xout = self.y if k == n_sub else self.scratch(f"xres{k}", [S_LEN, D], F32)
                    getattr(self, "sub_" + kind)(l, xs[-1], xout)
                    xs.append(xout)
                    S.barrier()
                    self._reset_consts()
            S.finish()
        return nc

    def _reset_consts(self):
        pass

    def nps(self):
        i = self.psi % 6
        self.psi += 1
        return i

    def norm_T(self, st, X, ntok, gain_row, hT, hkey, tag):
        nc, S = self.nc, self.S
        gT = st.enter_context(nc.sbuf_tensor(f"gT_{tag}", [128, 8], F32))
        S.dma("sp", gT[:], dap(gain_row, gain_row.offset, [[1, 128], [128, 8]]), W=[f"gT_{tag}"],
              allow_slow_non_contiguous=True)
        NB = 3
        xb = [st.enter_context(nc.sbuf_tensor(f"nx{i}_{tag}", [128, D], F32)) for i in range(NB)]
        sq = st.enter_context(nc.sbuf_tensor(f"nsq_{tag}", [128, D], BF16))
        xsb = [st.enter_context(nc.sbuf_tensor(f"nxs{i}_{tag}", [128, D], BF16)) for i in range(NB)]
        ss = [st.enter_context(nc.sbuf_tensor(f"nss{i}_{tag}", [128, 2], F32)) for i in range(NB)]
        nt = ntok // 128

        def evac(t):
            pb = self.psb[t % 2]
            S.op("dve", "tensor_tensor", hT[:, :, t * 128:(t + 1) * 128], pb[:].rearrange("p (c t) -> p c t", c=8),
                 gT[:].unsqueeze(2).to_broadcast([128, 8, 128]), ALU.mult, R=[("ps", 6 + t % 2), f"gT_{tag}"], W=[(hkey, t)])

        for t in range(nt):
            b = t % NB
            kx, ks, kxs = f"nx{b}_{tag}", f"nss{b}_{tag}", f"nxs{b}_{tag}"
            S.dma("sp", xb[b][:], X[t * 128:(t + 1) * 128, :], W=[kx])
            S.op("pool", "memset", ss[b][:], 0.0, W=[ks])
            S.op("act", "activation", sq[:], xb[b][:], AF.Square, accum_out=ss[b][:, 0:1], R=[kx, ks], W=[ks, f"nsq_{tag}"])
            S.op("act", "activation", ss[b][:, 1:2], ss[b][:, 0:1], AF.Sqrt, bias=self.epsc[:], scale=1.0 / D, R=[ks], W=[ks])
            S.op("dve", "reciprocal", ss[b][:, 1:2], ss[b][:, 1:2], R=[ks], W=[ks])
            S.op("dve", "tensor_scalar", xsb[b][:], xb[b][:], ss[b][:, 1:2], None, ALU.mult, R=[kx, ks], W=[kxs])
            pb = self.psb[t % 2]
            for c in range(8):
                S.op("pe", "transpose", pb[:, c * 128:(c + 1) * 128], xsb[b][:, c * 128:(c + 1) * 128], self.ident_b[:],
                     R=[kxs], W=[("ps", 6 + t % 2)], inc=(c == 7))
            if t >= 1:
                evac(t - 1)
        evac(nt - 1)

    def load_w(self, wt, key, Wl, col0, ncols, nk=8, row0=0):
        ncol_total = Wl.shape[-1]
        src = dap(Wl, Wl.offset + row0 * ncol_total + col0, [[ncol_total, 128], [128 * ncol_total, nk], [1, ncols]])
        return self.S.dma("pool", wt[:, 0:nk, 0:ncols], src, W=[key])

    def out_proj_residual(self, st, lhs_fn, nk, Wl, X, Xout, tag, lhs_keys, src=None):
        nc, S = self.nc, self.S
        wt = st.enter_context(nc.sbuf_tensor(f"wo_{tag}", [128, nk, D], BF16))
        step = 8
        for k0 in range(0, nk, step):
            kn = min(step, nk - k0)
            src_w = dap(Wl, Wl.offset + k0 * 128 * D, [[D, 128], [128 * D, kn], [1, D]])
            S.dma("pool", wt[:, k0:k0 + kn, :], src_w, W=[f"wo{k0}"])
        wkeys = [f"wo{k0}" for k0 in range(0, nk, step)]
        NX = 4
        xb = [st.enter_context(nc.sbuf_tensor(f"ox{i}_{tag}", [128, D], F32)) for i in range(NX)]
        lb = None
        if src is not None:
            lb = [st.enter_context(nc.sbuf_tensor(f"ol{i}_{tag}", [128, nk, 128], BF16)) for i in range(3)]
        for t in range(NT):
            b = t % NX
            S.dma("sp", xb[b][:], X[t * 128:(t + 1) * 128, :], W=[f"ox{b}"])
            if src is not None:
                l3 = t % 3
                S.dma("sp", lb[l3][:], dap(src, src.offset + t * 128, [[S_LEN, 128], [128 * S_LEN, nk], [1, 128]]), W=[f"ol{l3}"])
            for hf in range(2):
                pi = (2 * t + hf) % 6
                for k in range(nk):
                    if src is not None:
                        lhs, lk = lb[t % 3][:, k, :], [f"ol{t % 3}"]
                    else:
                        lhs, lk = lhs_fn(k, t), lhs_keys(k, t)
                    S.op("pe", "matmul", self.ps[pi][:, :], lhs, wt[:, k, hf * 512:(hf + 1) * 512],
                         start=(k == 0), stop=(k == nk - 1), R=wkeys + lk, W=[("ps", pi)], inc=(k == nk - 1))
                S.op("dve", "tensor_tensor", xb[b][:, hf * 512:(hf + 1) * 512], self.ps[pi][:, :],
                     xb[b][:, hf * 512:(hf + 1) * 512], ALU.add, R=[("ps", pi), f"ox{b}"], W=[f"ox{b}"])
            S.dma("pool", Xout[t * 128:(t + 1) * 128, :], xb[b][:], R=[f"ox{b}"], W=[("xout", tag, t)])

    def col_vec(self, st, name, src_row, n=128, scale=None):
        t = st.enter_context(self.nc.sbuf_tensor(name, [128, 1], F32))
        self.S.dma("sp", t[0:n, :], dap(src_row, src_row.offset, [[1, n], [1, 1]]), W=[name])
        if scale is not None:
            self.S.op("dve", "tensor_scalar", t[0:n, :], t[0:n, :], float(scale), None, ALU.mult, R=[name], W=[name])
        return t

    def sub_cross(self, l, X, Xout):
        nc, S = self.nc, self.S
        tag = f"c{l}"
        with contextlib.ExitStack() as st:
            hT = st.enter_context(nc.sbuf_tensor(f"hT_{tag}", [128, 8, S_LEN], BF16))
            oT = st.enter_context(nc.sbuf_tensor(f"oT_{tag}", [128, 4, S_LEN], BF16))
            with contextlib.ExitStack() as st2:
                self.norm_T(st2, X, S_LEN, self.I["norm_cross"][l], hT, "hT", tag)
            S.barrier()
            with contextlib.ExitStack() as st2:
                sb = lambda n, s, d: st2.enter_context(nc.sbuf_tensor(f"{n}_{tag}", s, d))
                memT = sb("memT", [128, 8, MEM], BF16)
                with contextlib.ExitStack() as st3:
                    self.norm_T(st3, self.I["mem"], MEM, self.I["norm_mem"][l], memT, "memT", tag + "m")
                S.barrier()
                kT = sb("kT", [128, 4, MEM], BF16)
                vtm = sb("vtm", [128, 2, 512], BF16)
                gq = self.col_vec(st2, f"gq_{tag}", self.I["cross_q_norm"][l], scale=128.0 ** -0.5)
                gk = self.col_vec(st2, f"gk_{tag}", self.I["cross_k_norm"][l])
                wk = sb("wk", [128, 8, 512], BF16)
                wv = sb("wv", [128, 8, 512], BF16)
                wq = sb("wq", [128, 8, 512], BF16)
                self.load_w(wk, "wk", self.I["cross_wkv"][l], 0, 512)
                self.load_w(wv, "wv", self.I["cross_wkv"][l], 512, 512)
                self.load_w(wq, "wq", self.I["cross_wq"][l], 0, 512)
                sq = [sb(f"sq{i}", [128, 512], F32) for i in range(2)]
                rs = [sb(f"rs{i}", [128, 512], F32) for i in range(2)]
                qn = [sb(f"qn{i}", [128, 512], BF16) for i in range(2)]
                pT = [sb(f"pT{i}", [128, 512], BF16) for i in range(4)]
                rec = [sb(f"rec{i}", [128, 512], F32) for i in range(2)]
                mk = [("memT", 0), ("memT", 1)]
                for h in range(4):
                    pi = h % 2
                    for k in range(8):
                        S.op("pe", "matmul", self.ps[pi][:, 0:MEM], wk[:, k, h * 128:(h + 1) * 128], memT[:, k, :],
                             start=(k == 0), stop=(k == 7), R=["wk"] + mk, W=[("ps", pi)], inc=(k == 7))
                    S.op("act", "activation", sq[pi][:, 0:MEM], self.ps[pi][:, 0:MEM], AF.Square, R=[("ps", pi)], W=[f"sq{pi}"])
                    pj = 2 + pi
                    S.op("pe", "matmul", self.ps[pj][:, 0:MEM], self.ones_f[:], sq[pi][:, 0:MEM], start=True, stop=True, R=[f"sq{pi}"], W=[("ps", pj)])
                    S.op("act", "activation", rs[pi][:, 0:MEM], self.ps[pj][:, 0:MEM], AF.Sqrt, bias=self.epsc[:], scale=1.0 / 128, R=[("ps", pj)], W=[f"rs{pi}"])
                    S.op("dve", "reciprocal", rs[pi][:, 0:MEM], rs[pi][:, 0:MEM], R=[f"rs{pi}"], W=[f"rs{pi}"])
                    S.op("dve", "scalar_tensor_tensor", kT[:, h, :], self.ps[pi][:, 0:MEM], gk[:, 0:1], rs[pi][:, 0:MEM], ALU.mult, ALU.mult,
                         R=[("ps", pi), f"rs{pi}", f"gk_{tag}"], W=["kT"])
                for mt in range(2):
                    pi = 4 + mt
                    for k in range(8):
                        S.op("pe", "matmul", self.ps[pi][:, :], memT[:, k, mt * 128:(mt + 1) * 128], wv[:, k, :],
                             start=(k == 0), stop=(k == 7), R=["wv"] + mk, W=[("ps", pi)], inc=(k == 7))
                    S.op("act", "copy", vtm[:, mt, :], self.ps[pi][:, :], R=[("ps", pi)], W=["vtm"])
                its = [(h, s) for h in range(4) for s in range(NS)]

                def ca(i):
                    h, s = its[i]
                    b2 = i % 2
                    pq = b2
                    for k in range(8):
                        S.op("pe", "matmul", self.ps[pq][:, :], wq[:, k, h * 128:(h + 1) * 128], hT[:, k, s * 512:(s + 1) * 512],
                             start=(k == 0), stop=(k == 7), R=["wq"] + [("hT", 4 * s + j) for j in range(4)], W=[("ps", pq)], inc=(k == 7))
                    S.op("act", "activation", sq[b2][:], self.ps[pq][:, :], AF.Square, R=[("ps", pq)], W=[f"sq{b2}"])

                def cb_(i):
                    h, s = its[i]
                    b2 = i % 2
                    pq = b2
                    S.op("pe", "matmul", self.ps[2][:, :], self.ones_f[:], sq[b2][:], start=True, stop=True, R=[f"sq{b2}"], W=[("ps", 2)])
                    S.op("act", "activation", rs[b2][:], self.ps[2][:, :], AF.Ln, bias=self.epsc[:], scale=1.0 / 128, R=[("ps", 2)], W=[f"rs{b2}"])
                    S.op("act", "activation", rs[b2][:], rs[b2][:], AF.Exp, scale=-0.5, R=[f"rs{b2}"], W=[f"rs{b2}"])
                    S.op("dve", "scalar_tensor_tensor", qn[b2][:], self.ps[pq][:, :], gq[:, 0:1], rs[b2][:], ALU.mult, ALU.mult,
                         R=[("ps", pq), f"rs{b2}", f"gq_{tag}"], W=[f"qn{b2}"])

                def cc(i):
                    h, s = its[i]
                    b2 = i % 2
                    po, pz = 4, 5
                    for mt in range(2):
                        S.op("pe", "matmul", self.ps[3][:, :], kT[:, h, mt * 128:(mt + 1) * 128], qn[b2][:], start=True, stop=True,
                             R=["kT", f"qn{b2}"], W=[("ps", 3)])
                        pt = pT[(2 * i + mt) % 4]
                        ptk = f"pT{(2 * i + mt) % 4}"
                        S.op("act", "activation", pt[:], self.ps[3][:, :], AF.Exp, R=[("ps", 3)], W=[ptk])
                    for mt in range(2):
                        pt = pT[(2 * i + mt) % 4]
                        ptk = f"pT{(2 * i + mt) % 4}"
                        S.op("pe", "matmul", self.ps[po][:, :], vtm[:, mt, h * 128:(h + 1) * 128], pt[:], start=(mt == 0), stop=(mt == 1),
                             R=["vtm", ptk], W=[("ps", po)], inc=(mt == 1))
                    for mt in range(2):
                        pt = pT[(2 * i + mt) % 4]
                        ptk = f"pT{(2 * i + mt) % 4}"
                        S.op("pe", "matmul", self.ps[pz][:, :], self.ones_b[:], pt[:], start=(mt == 0), stop=(mt == 1),
                             R=[ptk], W=[("ps", pz)], inc=(mt == 1))
                    S.op("act", "activation", rec[b2][:], self.ps[pz][:, :], AF.Ln, R=[("ps", pz)], W=[f"rec{b2}"])
                    S.op("act", "activation", rec[b2][:], rec[b2][:], AF.Exp, scale=-1.0, R=[f"rec{b2}"], W=[f"rec{b2}"])
                    S.op("dve", "tensor_tensor", oT[:, h, s * 512:(s + 1) * 512], self.ps[po][:, :], rec[b2][:], ALU.mult,
                         R=[("ps", po), f"rec{b2}"], W=[("oT", h, s)])

                n_it = len(its)
                for i in range(n_it + 2):
                    if i < n_it:
                        ca(i)
                    if 0 <= i - 1 < n_it:
                        cb_(i - 1)
                    if 0 <= i - 2 < n_it:
                        cc(i - 2)
            S.barrier()
            with contextlib.ExitStack() as st2:
                self.out_proj_residual(st2, lambda k, t: oT[:, k, t * 128:(t + 1) * 128], 4, self.I["cross_wo"][l], X, Xout, tag,
                                       lambda k, t: [("oT", k, t // 4)])

    def fm_proj(self, wt, wkey, hT, s, pi, col0=0):
        S = self.S
        for k in range(8):
            S.op("pe", "matmul", self.ps[pi][:, :], wt[:, k, col0:col0 + 128], hT[:, k, s * 512:(s + 1) * 512],
                 start=(k == 0), stop=(k == 7), R=[wkey] + [("hT", 4 * s + i) for i in range(4)], W=[("ps", pi)], inc=(k == 7))

    def prep_t5(self, st):
        nc, S = self.nc, self.S
        self.t5R = self.scratch("t5R", [8, 128, 383], F32)
        self.bias_t = st.enter_context(nc.sbuf_tensor("bias_t", [128, 8, 2, 128], BF16))
        self.negt = st.enter_context(nc.sbuf_tensor("negt", [128, 128], BF16))
        S.op("dve", "memset", self.negt[:], NEG, W=["negt"])
        with contextlib.ExitStack() as st2:
            relbT = st2.enter_context(nc.sbuf_tensor("relbT", [32, 8], F32))
            rb = self.I["rel_bias"]
            S.dma("sp", relbT[:], dap(rb, rb.offset, [[1, 32], [32, 8]]), W=["relbT"], allow_slow_non_contiguous=True)
            t5 = st2.enter_context(nc.sbuf_tensor("t5oh", [33, 383], F32))
            S.dma("sp", t5[:], self.C["t5oh"], W=["t5oh"])
            L = st2.enter_context(nc.sbuf_tensor("t5L", [33, 128], F32))
            Rs = st2.enter_context(nc.sbuf_tensor("t5Rs", [128, 383], F32))
            for h in range(8):
                S.op("dve", "tensor_copy", L[0:32, :], relbT[0:32, h:h + 1].to_broadcast([32, 128]), R=["relbT"], W=["t5L"])
                S.op("dve", "memset", L[32:33, :], 1.0, R=["t5L"], W=["t5L"])
                S.op("pe", "matmul", self.ps[4][:, 0:383], L[:], t5[:], start=True, stop=True, R=["t5L", "t5oh"], W=[("ps", 4)])
                S.op("act", "copy", Rs[:], self.ps[4][:, 0:383], R=[("ps", 4)], W=["t5Rs"])
                S.dma("sp", self.t5R[h], Rs[:], R=["t5Rs"], W=[("t5R", h)])
                base = self.t5R.offset + h * 128 * 383
                S.dma("pool", self.bias_t[:, h, 0, :], dap(self.t5R, base + 127, [[382, 128], [1, 128]]), R=[("t5R", h)], W=["bias_t"])
                S.dma("pool", self.bias_t[:, h, 1, :], dap(self.t5R, base + 255, [[382, 128], [1, 128]]), R=[("t5R", h)], W=["bias_t"])
        S.barrier()

    def sub_mix(self, l, X, Xout):
        nc, S = self.nc, self.S
        tag = f"m{l}"
        I = self.I
        Win = I["w_in"][l]
        sc = lambda n, shp, dt: self.scratch(f"{n}{l}", shp, dt)
        qTr = sc("qTr", [256, S_LEN], BF16); kTr = sc("kTr", [256, S_LEN], BF16); rgT = sc("rgT", [256, S_LEN], F32)
        rvd = sc("rvd", [S_LEN, 256], BF16); mqT = sc("mqT", [512, S_LEN], BF16); mkT = sc("mkT", [512, S_LEN], BF16)
        mvd = sc("mvd", [S_LEN, 512], BF16); mmask = sc("mmask", [8, 16, S_LEN], BF16)
        gqT = sc("gqT", [256, S_LEN], BF16); gkT = sc("gkT", [256, S_LEN], BF16); gvT = sc("gvT", [256, S_LEN], BF16)
        gzd = sc("gzd", [S_LEN, 256], F32); yT = sc("yT", [D, S_LEN], BF16)
        with contextlib.ExitStack() as st:
            gba = st.enter_context(nc.sbuf_tensor(f"gba_{tag}", [128, NT, 8], F32))
            with contextlib.ExitStack() as sp_:
                hT = sp_.enter_context(nc.sbuf_tensor(f"hT_{tag}", [128, 8, S_LEN], BF16))
                with contextlib.ExitStack() as st2:
                    self.norm_T(st2, X, S_LEN, I["norm_mix"][l], hT, "hT", tag)
                S.barrier()
                cur = [sp_]
                sb = lambda n, shp, dt: cur[0].enter_context(nc.sbuf_tensor(f"{n}_{tag}", shp, dt))
                hkeys = lambda s: [("hT", 4 * s + i) for i in range(4)]
                wtm = sb("wtm", [128, 8, 512], BF16)
                stg_b = [sb(f"stgb{i}", [128, 512], BF16) for i in range(2)]
                stg_f = [sb(f"stgf{i}", [128, 256], F32) for i in range(2)]
                for (nm, col0, ncols) in (("rv", 512, 256), ("mv", 2048, 512), ("gz", 3328, 256), ("gba", 3584, 8)):
                    self.load_w(wtm, "wtm", Win, col0, ncols)
                    for t in range(NT):
                        pi = 2 + t % 2
                        for k in range(8):
                            S.op("pe", "matmul", self.ps[pi][:, 0:ncols], hT[:, k, t * 128:(t + 1) * 128], wtm[:, k, 0:ncols],
                                 start=(k == 0), stop=(k == 7), R=["wtm", ("hT", t)], W=[("ps", pi)], inc=(k == 7))
                        b = t % 2
                        if nm == "rv":
                            S.op("act", "copy", stg_b[b][:, 0:256], self.ps[pi][:, 0:256], R=[("ps", pi)], W=[f"stgb{b}"])
                            S.dma("sp", rvd[t * 128:(t + 1) * 128, :], stg_b[b][:, 0:256], R=[f"stgb{b}"], W=[("rvd", t)])
                        elif nm == "mv":
                            S.op("act", "copy", stg_b[b][:, :], self.ps[pi][:, :], R=[("ps", pi)], W=[f"stgb{b}"])
                            S.dma("sp", mvd[t * 128:(t + 1) * 128, :], stg_b[b][:, :], R=[f"stgb{b}"], W=[("mvd", t)])
                        elif nm == "gz":
                            S.op("act", "activation", stg_f[b][:, :], self.ps[pi][:, 0:256], AF.Silu, R=[("ps", pi)], W=[f"stgf{b}"])
                            S.dma("sp", gzd[t * 128:(t + 1) * 128, :], stg_f[b][:, :], R=[f"stgf{b}"], W=[("gzd", t)])
                        else:
                            S.op("act", "copy", gba[:, t, :], self.ps[pi][:, 0:8], R=[("ps", pi)], W=[("gba", t)])
                wa = [sb(f"wa{i}", [128, 8, 128], BF16) for i in range(2)]
                wb = [sb(f"wb{i}", [128, 8, 128], BF16) for i in range(2)]
                ob = [sb(f"ob{i}", [128, 512], BF16) for i in range(3)]
                sq = [sb(f"sq{i}", [128, 512], F32) for i in range(2)]
                rs = [sb(f"rs{i}", [128, 512], F32) for i in range(2)]
                sub_r = contextlib.ExitStack()
                cur[0] = sub_r
                rope = [sb(f"rope{i}", [128, 2, 512], F32) for i in range(3)]
                t1 = [sb(f"t1{i}", [128, 512], F32) for i in range(2)]
                t2 = [sb(f"t2{i}", [128, 512], F32) for i in range(2)]
                of = [sb(f"of{i}", [128, 512], F32) for i in range(2)]
                nload = [0, 0]

                def ldw(col0):
                    b = nload[0] % 2
                    nload[0] += 1
                    self.load_w(wa[b], f"wa{b}", Win, col0, 128)
                    return wa[b], f"wa{b}"

                def ldw_perm(col0):
                    b = nload[1] % 2
                    nload[1] += 1
                    for (d0, s0) in ((0, 32), (32, 0), (64, 96), (96, 64)):
                        src = dap(Win, Win.offset + col0 + s0, [[IN_COLS, 128], [128 * IN_COLS, 8], [1, 32]])
                        S.dma("pool", wb[b][:, :, d0:d0 + 32], src, R=[f"wb{b}"], W=[f"wb{b}"])
                    return wb[b], f"wb{b}"

                def run_pipe(n, stage_a, stage_b, la=1):
                    for i in range(n + la):
                        if i < n:
                            stage_a(i)
                        if i - la >= 0:
                            stage_b(i - la)

                rp = self.C["rope"]
                its = []
                for (col_base, scale, dst) in ((0, 1.0, qTr), (256, 0.125, kTr)):
                    for ch in range(2):
                        for s in range(NS):
                            its.append((col_base, scale, dst, ch, s))
                wcur = {}

                def ra(i):
                    col_base, scale, dst, ch, s = its[i]
                    if s == 0:
                        wcur["B"] = ldw_perm(col_base + ch * 128)
                        wcur["A"] = ldw(col_base + ch * 128)
                    b3 = i % 3
                    S.dma("sp", rope[b3][:], dap(rp, rp.offset + s * 512, [[S_LEN, 128], [128 * S_LEN, 2], [1, 512]]), W=[f"rope{b3}"])
                    pa = (2 * i) % 4
                    self.fm_proj(wcur["A"][0], wcur["A"][1], hT, s, pa)
                    self.fm_proj(wcur["B"][0], wcur["B"][1], hT, s, pa + 1)

                def rb(i):
                    col_base, scale, dst, ch, s = its[i]
                    b3, b2 = i % 3, i % 2
                    pa = (2 * i) % 4
                    S.op("dve", "scalar_tensor_tensor", t1[b2][:], self.ps[pa][:, :], float(scale), rope[b3][:, 0, :], ALU.mult, ALU.mult,
                         R=[("ps", pa), f"rope{b3}"], W=[f"t1{b2}"])
                    S.op("dve", "scalar_tensor_tensor", t2[b2][:], self.ps[pa + 1][:, :], float(scale), rope[b3][:, 1, :], ALU.mult, ALU.mult,
                         R=[("ps", pa + 1), f"rope{b3}"], W=[f"t2{b2}"])
                    S.op("pool", "tensor_tensor", ob[b3][:], t1[b2][:], t2[b2][:], ALU.add, R=[f"t1{b2}", f"t2{b2}"], W=[f"ob{b3}"])
                    S.dma("pool", dst[ch * 128:(ch + 1) * 128, s * 512:(s + 1) * 512], ob[b3][:], R=[f"ob{b3}"], W=[("fmout", i)])
                run_pipe(len(its), ra, rb)

                its_g = [(ch, s) for ch in range(2) for s in range(NS)]

                def ga(i):
                    ch, s = its_g[i]
                    if s == 0:
                        wcur["A"] = ldw(768 + ch * 128)
                    self.fm_proj(wcur["A"][0], wcur["A"][1], hT, s, i % 4)

                def gb(i):
                    ch, s = its_g[i]
                    b2 = i % 2
                    S.op("act", "activation", of[b2][:], self.ps[i % 4][:, :], AF.Silu, R=[("ps", i % 4)], W=[f"of{b2}"])
                    S.dma("pool", rgT[ch * 128:(ch + 1) * 128, s * 512:(s + 1) * 512], of[b2][:], R=[f"of{b2}"], W=[("rgT", ch, s)])
                run_pipe(len(its_g), ga, gb)

                S.barrier()
                sub_r.close()
                sub_m = contextlib.ExitStack()
                cur[0] = sub_m
                gk2 = sb("gk2", [128, 1], F32); gq2 = sb("gq2", [128, 1], F32)
                for hh in range(2):
                    S.dma("sp", gk2[hh * 64:(hh + 1) * 64, :], dap(I["moba_k_norm"][l], I["moba_k_norm"][l].offset, [[1, 64], [1, 1]]), R=["gk2"], W=["gk2"])
                    S.dma("sp", gq2[hh * 64:(hh + 1) * 64, :], dap(I["moba_q_norm"][l], I["moba_q_norm"][l].offset, [[1, 64], [1, 1]]), R=["gq2"], W=["gq2"])
                S.op("dve", "tensor_scalar", gq2[:], gq2[:], 0.125, None, ALU.mult, R=["gq2"], W=["gq2"])
                kms = sb("kms", [128, 4, 16], F32)
                n32 = [sb(f"n32{i}", [128, 512], F32) for i in range(2)]
                gate = sb("gate", [128, 64, 16], F32)
                m8 = sb("m8", [128, 64, 8], F32)
                thr = sb("thr", [128, 64], F32)
                alw = sb("alw", [128, 64, 16], F32)
                mTs = sb("mTs", [16, 2, S_LEN], BF16)
                fmk = sb("fmk", [128, 2, 1024], F32)
                S.dma("sp", fmk[:], dap(self.C["moba_fm"], self.C["moba_fm"].offset, [[0, 128], [1024, 2], [1, 1024]]), W=["fmk"])
                its_m = []
                for (isq, col_base, gcol, gkey, dst) in ((0, 1536, gk2, "gk2", mkT), (1, 1024, gq2, "gq2", mqT)):
                    for ch in range(4):
                        for s in range(NS):
                            its_m.append((isq, col_base, gcol, gkey, dst, ch, s))

                def ma(i):
                    isq, col_base, gcol, gkey, dst, ch, s = its_m[i]
                    if s == 0:
                        wcur["A"] = ldw(col_base + ch * 128)
                    pa = i % 3
                    self.fm_proj(wcur["A"][0], wcur["A"][1], hT, s, pa)
                    S.op("act", "activation", sq[i % 2][:], self.ps[pa][:, :], AF.Square, R=[("ps", pa)], W=[f"sq{i % 2}"])

                def mb(i):
                    isq, col_base, gcol, gkey, dst, ch, s = its_m[i]
                    pa = i % 3
                    b2, b3 = i % 2, i % 3
                    S.op("pe", "matmul", self.ps[4][:, :], self.blk64[:], sq[b2][:], start=True, stop=True, R=[f"sq{b2}"], W=[("ps", 4)])
                    S.op("act", "activation", rs[b2][:], self.ps[4][:, :], AF.Sqrt, bias=self.epsc[:], scale=1.0 / 64, R=[("ps", 4)], W=[f"rs{b2}"])
                    S.op("dve", "reciprocal", rs[b2][:], rs[b2][:], R=[f"rs{b2}"], W=[f"rs{b2}"])
                    S.op("dve", "scalar_tensor_tensor", n32[b2][:], self.ps[pa][:, :], gcol[:, 0:1], rs[b2][:], ALU.mult, ALU.mult,
                         R=[("ps", pa), f"rs{b2}", gkey], W=[f"n32{b2}"])
                    S.op("act", "copy", ob[b3][:], n32[b2][:], R=[f"n32{b2}"], W=[f"ob{b3}"])
                    S.dma("pool", dst[ch * 128:(ch + 1) * 128, s * 512:(s + 1) * 512], ob[b3][:], R=[f"ob{b3}"], W=[("mT", isq, ch, s)])
                    if isq == 0:
                        S.op("dve", "tensor_reduce", kms[:, ch, 2 * s:2 * s + 2], n32[b2][:].rearrange("p (a b) -> p a b", a=2), AX.X, ALU.add,
                             R=[f"n32{b2}"], W=["kms"])
                        return
                    gbank = 3 if s < 4 else 5
                    for hh in range(2):
                        for i4 in range(4):
                            g = hh * 4 + i4
                            c0 = (s % 4) * 128 + g * 16
                            S.op("pe", "matmul", self.ps[gbank][:, c0:c0 + 16], n32[b2][hh * 64:(hh + 1) * 64, i4 * 128:(i4 + 1) * 128],
                                 kms[hh * 64:(hh + 1) * 64, ch, :], start=True, stop=True, R=[f"n32{b2}", "kms"], W=[("ps", gbank)], inc=(g == 7))
                    if s != NS - 1:
                        return
                    gfl = gate[:].rearrange("p a b -> p (a b)")
                    afl = alw[:].rearrange("p a b -> p (a b)")
                    S.op("act", "copy", gfl[:, 0:512], self.ps[3][:, :], R=[("ps", 3)], W=["gate"])
                    S.op("act", "copy", gfl[:, 512:1024], self.ps[5][:, :], R=[("ps", 5), "gate"], W=["gate"])
                    S.op("dve", "tensor_tensor", gfl, gfl, fmk[:, 0, :], ALU.add, R=["gate", "fmk"], W=["gate"])
                    for g in range(64):
                        S.op("dve", "max", m8[:, g, :], gate[:, g, :], R=["gate"], W=[("m8", g)])
                    S.op("dve", "tensor_scalar", thr[:], m8[:, :, 2], -1e29, None, ALU.max, R=[("m8", g) for g in range(64)], W=["thr"])
                    S.op("dve", "tensor_tensor", alw[:], gate[:], thr[:].unsqueeze(2).to_broadcast([128, 64, 16]), ALU.is_ge,
                         R=["gate", "thr"], W=["alw"])
                    S.op("dve", "tensor_tensor", afl, afl, fmk[:, 1, :], ALU.max, R=["alw", "fmk"], W=["alw"])
                    S.op("dve", "tensor_scalar", afl, afl, 1.0, -NEG, ALU.subtract, ALU.mult, R=["alw"], W=["alw"])
                    for s2 in range(NS):
                        for hh in range(2):
                            tb = 3 if (2 * s2 + hh) % 2 == 0 else 5
                            for i4 in range(4):
                                S.op("pe", "transpose", self.ps[tb][0:16, i4 * 128:(i4 + 1) * 128], alw[:, s2 * 8 + hh * 4 + i4, :], self.ident_f[:],
                                     R=["alw"], W=[("ps", tb)], inc=(i4 == 3))
                            S.op("act", "copy", mTs[:, hh, s2 * 512:(s2 + 1) * 512], self.ps[tb][0:16, :], R=[("ps", tb)], W=[("mTs", hh, s2)])
                    for hh in range(2):
                        S.dma("pool", mmask[2 * ch + hh, :, :], mTs[:, hh, :], R=[("mTs", hh, s2) for s2 in range(NS)], W=[("mmask", ch, hh)])
                run_pipe(len(its_m), ma, mb)
                S.barrier()
                sub_m.close()
                sub_g = contextlib.ExitStack()
                cur[0] = sub_g

                cwg = sb("cwg", [128, 4, 6], F32)
                gc = I["gdn_conv"][l]
                for kk in range(4):
                    S.dma("sp", cwg[:, kk, :], dap(gc, gc.offset + kk * 768, [[1, 128], [128, 6]]), R=["cwg"], W=["cwg"], allow_slow_non_contiguous=True)
                gpre = [sb(f"gpre{i}", [128, 3 + S_LEN], F32) for i in range(2)]
                gu = [sb(f"gu{i}", [128, S_LEN], F32) for i in range(2)]
                gvb = sb("gvb", [128, S_LEN], BF16)
                for i in range(2):
                    S.op("pool", "memset", gpre[i][:, 0:3], 0.0, W=[f"gpre{i}"])

                def g_proj(ch):
                    cb = ch % 2
                    wA, wAk = ldw(2560 + ch * 128)
                    for s in range(NS):
                        pa = s % 4
                        self.fm_proj(wA, wAk, hT, s, pa)
                        S.op("act", "copy", gpre[cb][:, 3 + s * 512:3 + (s + 1) * 512], self.ps[pa][:, :], R=[("ps", pa)], W=[f"gpre{cb}"])

                def g_post(ch):
                    cb = ch % 2
                    S.op("act", "activation", gu[cb][:], gpre[cb][:, 3:3 + S_LEN], AF.Copy, scale=cwg[:, 3, ch:ch + 1], R=[f"gpre{cb}", "cwg"], W=[f"gu{cb}"])
                    for kk in range(3):
                        S.op("dve", "scalar_tensor_tensor", gu[cb][:], gpre[cb][:, kk:kk + S_LEN], cwg[:, kk, ch:ch + 1], gu[cb][:], ALU.mult, ALU.add,
                             R=[f"gpre{cb}", "cwg", f"gu{cb}"], W=[f"gu{cb}"])
                    S.op("act", "activation", gu[cb][:], gu[cb][:], AF.Silu, R=[f"gu{cb}"], W=[f"gu{cb}"])
                    if ch >= 4:
                        S.op("act", "copy", gvb[:], gu[cb][:], R=[f"gu{cb}"], W=["gvb"])
                        S.dma("pool", gvT[(ch - 4) * 128:(ch - 3) * 128, :], gvb[:], R=["gvb"], W=[("gvT", ch)])
                        return
                    dst = gqT if ch < 2 else gkT
                    qs = 0.125 if ch < 2 else 1.0

                    def la_(s):
                        S.op("act", "activation", sq[s % 2][:], gu[cb][:, s * 512:(s + 1) * 512], AF.Square, R=[f"gu{cb}"], W=[f"sq{s % 2}"])

                    def lb_(s):
                        b2, b3 = s % 2, s % 3
                        S.op("pe", "matmul", self.ps[4 + b2][:, :], self.blk64[:], sq[b2][:], start=True, stop=True, R=[f"sq{b2}"], W=[("ps", 4 + b2)])
                        S.op("act", "activation", rs[b2][:], self.ps[4 + b2][:, :], AF.Sqrt, bias=self.epsc[:], scale=1.0, R=[("ps", 4 + b2)], W=[f"rs{b2}"])
                        S.op("dve", "reciprocal", rs[b2][:], rs[b2][:], R=[f"rs{b2}"], W=[f"rs{b2}"])
                        S.op("dve", "scalar_tensor_tensor", ob[b3][:], gu[cb][:, s * 512:(s + 1) * 512], float(qs), rs[b2][:], ALU.mult, ALU.mult,
                             R=[f"gu{cb}", f"rs{b2}"], W=[f"ob{b3}"])
                        S.dma("pool", dst[(ch % 2) * 128:(ch % 2 + 1) * 128, s * 512:(s + 1) * 512], ob[b3][:], R=[f"ob{b3}"], W=[("gT", ch, s)])
                    run_pipe(NS, la_, lb_)

                for ch in range(7):
                    if ch < 6:
                        g_proj(ch)
                    if ch >= 1:
                        g_post(ch - 1)
                S.barrier()
                sub_g.close()
                cur[0] = sp_
            S.barrier()
            if self.mix_parts is None or "ret" in self.mix_parts:
                self.mix_ret(l, tag, qTr, kTr, rgT, rvd, yT)
                S.barrier()
            if self.mix_parts is None or "moba" in self.mix_parts:
                self.mix_moba(l, tag, mqT, mkT, mvd, mmask, yT)
                S.barrier()
            if self.mix_parts is None or "gdn" in self.mix_parts:
                self.mix_gdn(l, tag, gba, gqT, gkT, gvT, gzd, yT)
                S.barrier()
        with contextlib.ExitStack() as st:
            self.out_proj_residual(st, None, 8, I["w_out"][l], X, Xout, tag, None, src=yT)

    def mix_ret(self, l, tag, qTr, kTr, rgT, rvd, yT):
        nc, S = self.nc, self.S
        with contextlib.ExitStack() as st:
            sb = lambda n, shp, dt: st.enter_context(nc.sbuf_tensor(f"r{n}_{tag}", shp, dt))
            kT = [sb(f"kT{i}", [64, S_LEN], BF16) for i in range(2)]
            vh = [sb(f"vh{i}", [128, NT, 64], BF16) for i in range(2)]
            G = [sb(f"G{i}", [128, 5, 512], F32) for i in range(2)]
            qt = [sb(f"qt{i}", [64, 512], BF16) for i in range(3)]
            rg = [sb(f"rg{i}", [64, 512], F32) for i in range(3)]
            pT = [sb(f"pT{i}", [128, 512], BF16) for i in range(3)]
            sq = [sb(f"sq{i}", [64, 512], F32) for i in range(2)]
            rs = [sb(f"rs{i}", [64, 512], F32) for i in range(2)]
            y1 = [sb(f"y1{i}", [64, 512], F32) for i in range(2)]
            yo = [sb(f"yo{i}", [64, 512], BF16) for i in range(2)]
            rn = self.col_vec(st, f"rn_{tag}", self.I["ret_norm"][l], n=64)
            rd = self.C["ret_decay"]
            PST = (0, 1, 5)
            items = []
            for h in range(4):
                for s in range(NS):
                    js = []
                    for j in range(4 * s + 4):
                        jj = j - 4 * s
                        if jj < 0:
                            c = math.exp(self.lg[h] * (512 * s - 128 * (j + 1)))
                            if c < 1e-30:
                                continue
                            js.append((j, jj, c))
                        else:
                            js.append((j, jj, 1.0))
                    for n_, (j, jj, c) in enumerate(js):
                        items.append((h, s, j, jj, c, n_, len(js)))

            def stage_a(idx):
                h, s, j, jj, c, n_, nn = items[idx]
                hb = h % 2
                it = h * NS + s
                qb = it % 3
                if n_ == 0:
                    if s == 0:
                        S.dma("sp", kT[hb][:], kTr[h * 64:(h + 1) * 64, :], W=[f"kT{hb}"])
                        S.dma("sp", vh[hb][:], dap(rvd, rvd.offset + h * 64, [[256, 128], [128 * 256, NT], [1, 64]]), W=[f"vh{hb}"])
                        S.dma("sp", G[hb][:], dap(rd, rd.offset + h * 5 * 128 * 512, [[512, 128], [128 * 512, 5], [1, 512]]), W=[f"G{hb}"])
                    S.dma("sp", qt[qb][:], qTr[h * 64:(h + 1) * 64, s * 512:(s + 1) * 512], W=[f"qt{qb}"])
                    S.dma("sp", rg[qb][:], rgT[h * 64:(h + 1) * 64, s * 512:(s + 1) * 512], W=[f"rg{qb}"])
                k3 = idx % 3
                pst = PST[k3]
                S.op("pe", "matmul", self.ps[pst][:, :], kT[hb][:, j * 128:(j + 1) * 128], qt[qb][:], start=True, stop=True,
                     R=[f"kT{hb}", f"qt{qb}"], W=[("ps", pst)])
                if jj < 0:
                    S.op("dve", "scalar_tensor_tensor", pT[k3][:], self.ps[pst][:, :], float(c), G[hb][:, 0, :], ALU.mult, ALU.mult,
                         R=[("ps", pst), f"G{hb}"], W=[f"pT{k3}"])
                else:
                    S.op("dve", "tensor_tensor", pT[k3][:], self.ps[pst][:, :], G[hb][:, 1 + jj, :], ALU.mult,
                         R=[("ps", pst), f"G{hb}"], W=[f"pT{k3}"])

            def stage_b(idx):
                h, s, j, jj, c, n_, nn = items[idx]
                hb = h % 2
                it = h * NS + s
                b = it % 2
                qb = it % 3
                po = 2 + b
                k3 = idx % 3
                if n_ == 0:
                    flush(po)
                S.op("pe", "matmul", self.ps[po][0:64, :], vh[hb][:, j, :], pT[k3][:], start=(n_ == 0), stop=(n_ == nn - 1),
                     R=[f"vh{hb}", f"pT{k3}"], W=[("ps", po)], inc=(n_ == nn - 1))
                if n_ == nn - 1:
                    S.op("act", "activation", sq[b][:], self.ps[po][0:64, :], AF.Square, R=[("ps", po)], W=[f"sq{b}"])
                    S.op("pool", "tensor_tensor", y1[b][:], rg[qb][:], rn[0:64, 0:1].to_broadcast([64, 512]), ALU.mult, R=[f"rg{qb}", f"rn_{tag}"], W=[f"y1{b}"])
                    pending.append([3, 0, (h, s, b, po)])

            def finalize(stage, h, s, b, po):
                if stage == 0:
                    S.op("pe", "matmul", self.ps[4][0:64, :], self.ones_f[0:64, 0:64], sq[b][:], start=True, stop=True, R=[f"sq{b}"], W=[("ps", 4)])
                    S.op("act", "activation", rs[b][:], self.ps[4][0:64, :], AF.Sqrt, bias=self.epsc[0:64, :], scale=1.0 / 64, R=[("ps", 4)], W=[f"rs{b}"])
                    pending.append([2, 1, (h, s, b, po)])
                elif 1 <= stage <= 4:
                    q4 = stage - 1
                    S.op("dve", "reciprocal", rs[b][:, q4 * 128:(q4 + 1) * 128], rs[b][:, q4 * 128:(q4 + 1) * 128], R=[f"rs{b}"], W=[f"rs{b}"])
                    pending.append([1, stage + 1, (h, s, b, po)])
                elif stage == 5:
                    S.op("pool", "tensor_tensor", y1[b][:], y1[b][:], rs[b][:], ALU.mult, R=[f"y1{b}", f"rs{b}"], W=[f"y1{b}"])
                    pending.append([3, 6, (h, s, b, po)])
                else:
                    S.op("dve", "tensor_tensor", yo[b][:], self.ps[po][0:64, :], y1[b][:], ALU.mult, R=[("ps", po), f"y1{b}"], W=[f"yo{b}"])
                    S.dma("pool", yT[h * 64:(h + 1) * 64, s * 512:(s + 1) * 512], yo[b][:], R=[f"yo{b}"], W=[("yT", h, s)])

            pending = []

            def flush(po_):
                again = True
                while again:
                    again = False
                    for pnd in list(pending):
                        if pnd[2][3] == po_:
                            pending.remove(pnd)
                            finalize(pnd[1], *pnd[2])
                            again = True

            LA = 2
            for idx in range(len(items) + LA):
                if idx < len(items):
                    stage_a(idx)
                if idx - LA >= 0:
                    stage_b(idx - LA)
                for pnd in list(pending):
                    pnd[0] -= 1
                    if pnd[0] <= 0:
                        pending.remove(pnd)
                        finalize(pnd[1], *pnd[2])
            while pending:
                pnd = pending.pop(0)
                finalize(pnd[1], *pnd[2])

    def mix_moba(self, l, tag, mqT, mkT, mvd, mmask, yT):
        nc, S = self.nc, self.S
        with contextlib.ExitStack() as st:
            sb = lambda n, shp, dt: st.enter_context(nc.sbuf_tensor(f"b{n}_{tag}", shp, dt))
            ka = [sb(f"ka{i}", [80, S_LEN], BF16) for i in range(2)]
            va = [sb(f"va{i}", [128, NT, 65], BF16) for i in range(2)]
            qa = [sb(f"qa{i}", [80, 512], BF16) for i in range(3)]
            pT = [sb(f"pT{i}", [128, 512], BF16) for i in range(3)]
            rec = [sb(f"rec{i}", [128, 512], F32) for i in range(2)]
            bc = [sb(f"bc{i}", [64, 512], F32) for i in range(2)]
            yo = [sb(f"yo{i}", [64, 512], BF16) for i in range(2)]
            PST = (0, 1, 5)
            for i in range(2):
                S.dma("pool", ka[i][64:80, :], self.C["blk_oh"], W=[f"ka_oh{i}"])
                S.op("dve", "memset", va[i][:, :, 64:65], 1.0, W=[f"va1{i}"])
            items = []
            for h in range(8):
                for s in range(NS):
                    nj = 4 * s + 4
                    for j in range(nj):
                        items.append((h, s, j, nj))
            state = {"it": -1}

            def stage_a(idx):
                h, s, j, nj = items[idx]
                hb = h % 2
                it = h * NS + s
                qb = it % 3
                if j == 0:
                    if s == 0:
                        S.dma("sp", ka[hb][0:64, :], mkT[h * 64:(h + 1) * 64, :], W=[f"ka{hb}"])
                        S.dma("sp", va[hb][:, :, 0:64], dap(mvd, mvd.offset + h * 64, [[512, 128], [128 * 512, NT], [1, 64]]), W=[f"va{hb}"])
                    S.dma("sp", qa[qb][0:64, :], mqT[h * 64:(h + 1) * 64, s * 512:(s + 1) * 512], W=[f"qa{qb}"])
                    S.dma("sp", qa[qb][64:80, :], mmask[h, :, s * 512:(s + 1) * 512], W=[f"qm{qb}"])
                k3 = idx % 3
                pst = PST[k3]
                extra = []
                for i in range(4):
                    ti = 4 * s + i
                    if ti == j:
                        extra.append((i, self.bias_t[:, h, 0, :]))
                    elif ti == j + 1:
                        extra.append((i, self.bias_t[:, h, 1, :]))
                    elif ti < j:
                        extra.append((i, self.negt[:]))
                S.op("pe", "matmul", self.ps[pst][:, :], ka[hb][:, j * 128:(j + 1) * 128], qa[qb][:], start=True, stop=(len(extra) == 0),
                     R=[f"ka{hb}", f"ka_oh{hb}", f"qa{qb}", f"qm{qb}"], W=[("ps", pst)], inc=(len(extra) == 0))
                for n_, (i, bt) in enumerate(extra):
                    last = n_ == len(extra) - 1
                    S.op("pe", "matmul", self.ps[pst][:, i * 128:(i + 1) * 128], self.ident_b[:], bt, start=False, stop=last,
                         R=[], W=[("ps", pst)], inc=last)
                S.op("act", "activation", pT[k3][:], self.ps[pst][:, :], AF.Exp, R=[("ps", pst)], W=[f"pT{k3}"])

            def stage_b(idx):
                h, s, j, nj = items[idx]
                hb = h % 2
                it = h * NS + s
                b = it % 2
                po = 2 + b
                k3 = idx % 3
                if j == 0:
                    flush(po)
                S.op("pe", "matmul", self.ps[po][0:65, :], va[hb][:, j, :], pT[k3][:], start=(j == 0), stop=(j == nj - 1),
                     R=[f"va{hb}", f"va1{hb}", f"pT{k3}"], W=[("ps", po)], inc=(j == nj - 1))
                if j == nj - 1:
                    S.op("dve", "tensor_copy", rec[b][64:65, :], self.ps[po][64:65, :], R=[("ps", po)], W=[f"rec{b}"])
                    pending.append([3, 0, (h, s, b, po)])

            def finalize(stage, h, s, b, po):
                if stage == 0:
                    S.op("pe", "matmul", self.ps[4][0:64, :], self.ones_f[64:65, 0:64], rec[b][64:65, :], start=True, stop=True, R=[f"rec{b}"], W=[("ps", 4)])
                    S.op("dve", "tensor_copy", bc[b][:], self.ps[4][0:64, :], R=[("ps", 4)], W=[f"bc{b}"])
                    S.op("dve", "reciprocal", bc[b][:], bc[b][:], R=[f"bc{b}"], W=[f"bc{b}"])
                    pending.append([5, 1, (h, s, b, po)])
                else:
                    S.op("dve", "tensor_tensor", yo[b][:], self.ps[po][0:64, :], bc[b][:], ALU.mult, R=[("ps", po), f"bc{b}"], W=[f"yo{b}"])
                    S.dma("pool", yT[256 + h * 64:256 + (h + 1) * 64, s * 512:(s + 1) * 512], yo[b][:], R=[f"yo{b}"], W=[("yT", h, s)])

            pending = []

            def flush(po_):
                again = True
                while again:
                    again = False
                    for pnd in list(pending):
                        if pnd[2][3] == po_:
                            pending.remove(pnd)
                            finalize(pnd[1], *pnd[2])
                            again = True

            LA = 2
            for idx in range(len(items) + LA):
                if idx < len(items):
                    stage_a(idx)
                if idx - LA >= 0:
                    stage_b(idx - LA)
                for pnd in list(pending):
                    pnd[0] -= 1
                    if pnd[0] <= 0:
                        pending.remove(pnd)
                        finalize(pnd[1], *pnd[2])
            while pending:
                pnd = pending.pop(0)
                finalize(pnd[1], *pnd[2])

    def gen_ret(self, st, l, tag, qTr, kTr, rgT, rvd, yT):
        nc, S = self.nc, self.S
        if True:
            sb = lambda n, shp, dt: st.enter_context(nc.sbuf_tensor(f"r{n}_{tag}", shp, dt))
            kT = [sb(f"kT{i}", [64, S_LEN], BF16) for i in range(2)]
            vh = [sb(f"vh{i}", [128, NT, 64], BF16) for i in range(2)]
            G = [sb(f"G{i}", [128, 5, 512], F32) for i in range(2)]
            qt = [sb(f"qt{i}", [64, 512], BF16) for i in range(3)]
            rg = [sb(f"rg{i}", [64, 512], F32) for i in range(3)]
            pT = [sb(f"pT{i}", [128, 512], BF16) for i in range(3)]
            sq = [sb(f"sq{i}", [64, 512], F32) for i in range(2)]
            rs = [sb(f"rs{i}", [64, 512], F32) for i in range(2)]
            y1 = [sb(f"y1{i}", [64, 512], F32) for i in range(2)]
            yo = [sb(f"yo{i}", [64, 512], BF16) for i in range(2)]
            poc = [sb(f"poc{i}", [64, 512], F32) for i in range(2)]
            rn = self.col_vec(st, f"rn_{tag}", self.I["ret_norm"][l], n=64)
            rd = self.C["ret_decay"]
            PST = (0, 1, 2)
            items = []
            for h in range(4):
                for s in range(NS):
                    js = []
                    for j in range(4 * s + 4):
                        jj = j - 4 * s
                        if jj < 0:
                            c = math.exp(self.lg[h] * (512 * s - 128 * (j + 1)))
                            if c < 1e-30:
                                continue
                            js.append((j, jj, c))
                        else:
                            js.append((j, jj, 1.0))
                    for n_, (j, jj, c) in enumerate(js):
                        items.append((h, s, j, jj, c, n_, len(js)))

            def stage_a(idx):
                h, s, j, jj, c, n_, nn = items[idx]
                hb = h % 2
                it = h * NS + s
                qb = it % 3
                if n_ == 0:
                    if s == 0:
                        S.dma("sp", kT[hb][:], kTr[h * 64:(h + 1) * 64, :], W=[f"kT{hb}"])
                        S.dma("sp", vh[hb][:], dap(rvd, rvd.offset + h * 64, [[256, 128], [128 * 256, NT], [1, 64]]), W=[f"vh{hb}"])
                        S.dma("sp", G[hb][:], dap(rd, rd.offset + h * 5 * 128 * 512, [[512, 128], [128 * 512, 5], [1, 512]]), W=[f"G{hb}"])
                    S.dma("sp", qt[qb][:], qTr[h * 64:(h + 1) * 64, s * 512:(s + 1) * 512], W=[f"qt{qb}"])
                    S.dma("sp", rg[qb][:], rgT[h * 64:(h + 1) * 64, s * 512:(s + 1) * 512], W=[f"rg{qb}"])
                k3 = idx % 3
                pst = PST[idx % len(PST)]
                S.op("pe", "matmul", self.ps[pst][:, :], kT[hb][:, j * 128:(j + 1) * 128], qt[qb][:], start=True, stop=True,
                     R=[f"kT{hb}", f"qt{qb}"], W=[("ps", pst)])
                if jj < 0:
                    S.op("dve", "scalar_tensor_tensor", pT[k3][:], self.ps[pst][:, :], float(c), G[hb][:, 0, :], ALU.mult, ALU.mult,
                         R=[("ps", pst), f"G{hb}"], W=[f"pT{k3}"])
                else:
                    S.op("dve", "tensor_tensor", pT[k3][:], self.ps[pst][:, :], G[hb][:, 1 + jj, :], ALU.mult,
                         R=[("ps", pst), f"G{hb}"], W=[f"pT{k3}"])

            def stage_b(idx):
                h, s, j, jj, c, n_, nn = items[idx]
                hb = h % 2
                it = h * NS + s
                b = it % 2
                qb = it % 3
                po = 3
                k3 = idx % 3
                S.op("pe", "matmul", self.ps[po][0:64, :], vh[hb][:, j, :], pT[k3][:], start=(n_ == 0), stop=(n_ == nn - 1),
                     R=[f"vh{hb}", f"pT{k3}"], W=[("ps", po)], inc=(n_ == nn - 1))
                if n_ == nn - 1:
                    S.op("act", "copy", poc[b][:], self.ps[po][0:64, :], R=[("ps", po)], W=[f"poc{b}"])
                    S.op("act", "activation", sq[b][:], poc[b][:], AF.Square, R=[f"poc{b}"], W=[f"sq{b}"])
                    S.op("pool", "tensor_tensor", y1[b][:], rg[qb][:], rn[0:64, 0:1].to_broadcast([64, 512]), ALU.mult, R=[f"rg{qb}", f"rn_{tag}"], W=[f"y1{b}"])
                    pending.append([3, (h, s, b, po)])

            def finalize(h, s, b, po):
                S.op("pe", "matmul", self.ps[b][0:64, :], self.ones_f[0:64, 0:64], sq[b][:], start=True, stop=True, R=[f"sq{b}"], W=[("ps", b)])
                S.op("act", "activation", rs[b][:], self.ps[b][0:64, :], AF.Sqrt, bias=self.epsc[0:64, :], scale=1.0 / 64, R=[("ps", b)], W=[f"rs{b}"])
                S.op("dve", "reciprocal", rs[b][:], rs[b][:], R=[f"rs{b}"], W=[f"rs{b}"])
                S.op("pool", "tensor_tensor", y1[b][:], y1[b][:], rs[b][:], ALU.mult, R=[f"y1{b}", f"rs{b}"], W=[f"y1{b}"])
                S.op("dve", "tensor_tensor", yo[b][:], poc[b][:], y1[b][:], ALU.mult, R=[f"poc{b}", f"y1{b}"], W=[f"yo{b}"])
                S.dma("pool", yT[h * 64:(h + 1) * 64, s * 512:(s + 1) * 512], yo[b][:], R=[f"yo{b}"], W=[("yT", h, s)])

            pending = []
            LA = 2

            def gen():
                for idx in range(len(items) + LA):
                    if idx < len(items):
                        stage_a(idx)
                    if idx - LA >= 0:
                        stage_b(idx - LA)
                    for pnd in list(pending):
                        pnd[0] -= 1
                        if pnd[0] <= 0:
                            finalize(*pnd[1])
                            pending.remove(pnd)
                    yield
                for pnd in pending:
                    finalize(*pnd[1])
                yield
            return gen()

    def gen_moba(self, st, l, tag, mqT, mkT, mvd, mmask, yT):
        nc, S = self.nc, self.S
        if True:
            sb = lambda n, shp, dt: st.enter_context(nc.sbuf_tensor(f"b{n}_{tag}", shp, dt))
            ka = [sb(f"ka{i}", [80, S_LEN], BF16) for i in range(2)]
            va = [sb(f"va{i}", [128, NT, 65], BF16) for i in range(2)]
            qa = [sb(f"qa{i}", [80, 512], BF16) for i in range(3)]
            pT = [sb(f"pT{i}", [128, 512], BF16) for i in range(3)]
            rec = [sb(f"rec{i}", [128, 512], F32) for i in range(2)]
            bc = [sb(f"bc{i}", [64, 512], F32) for i in range(2)]
            yo = [sb(f"yo{i}", [64, 512], BF16) for i in range(2)]
            poc = [sb(f"poc{i}", [65, 512], F32) for i in range(2)]
            PST = (4, 5, 6)
            for i in range(2):
                S.dma("pool", ka[i][64:80, :], self.C["blk_oh"], W=[f"ka_oh{i}"])
                S.op("dve", "memset", va[i][:, :, 64:65], 1.0, W=[f"va1{i}"])
            items = []
            for h in range(8):
                for s in range(NS):
                    nj = 4 * s + 4
                    for j in range(nj):
                        items.append((h, s, j, nj))
            state = {"it": -1}

            def stage_a(idx):
                h, s, j, nj = items[idx]
                hb = h % 2
                it = h * NS + s
                qb = it % 3
                if j == 0:
                    if s == 0:
                        S.dma("sp", ka[hb][0:64, :], mkT[h * 64:(h + 1) * 64, :], W=[f"ka{hb}"])
                        S.dma("sp", va[hb][:, :, 0:64], dap(mvd, mvd.offset + h * 64, [[512, 128], [128 * 512, NT], [1, 64]]), W=[f"va{hb}"])
                    S.dma("sp", qa[qb][0:64, :], mqT[h * 64:(h + 1) * 64, s * 512:(s + 1) * 512], W=[f"qa{qb}"])
                    S.dma("sp", qa[qb][64:80, :], mmask[h, :, s * 512:(s + 1) * 512], W=[f"qm{qb}"])
                k3 = idx % 3
                pst = PST[idx % len(PST)]
                extra = []
                for i in range(4):
                    ti = 4 * s + i
                    if ti == j:
                        extra.append((i, self.bias_t[:, h, 0, :]))
                    elif ti == j + 1:
                        extra.append((i, self.bias_t[:, h, 1, :]))
                    elif ti < j:
                        extra.append((i, self.negt[:]))
                S.op("pe", "matmul", self.ps[pst][:, :], ka[hb][:, j * 128:(j + 1) * 128], qa[qb][:], start=True, stop=(len(extra) == 0),
                     R=[f"ka{hb}", f"ka_oh{hb}", f"qa{qb}", f"qm{qb}"], W=[("ps", pst)], inc=(len(extra) == 0))
                for n_, (i, bt) in enumerate(extra):
                    last = n_ == len(extra) - 1
                    S.op("pe", "matmul", self.ps[pst][:, i * 128:(i + 1) * 128], self.ident_b[:], bt, start=False, stop=last,
                         R=[], W=[("ps", pst)], inc=last)
                S.op("act", "activation", pT[k3][:], self.ps[pst][:, :], AF.Exp, R=[("ps", pst)], W=[f"pT{k3}"])

            def stage_b(idx):
                h, s, j, nj = items[idx]
                hb = h % 2
                it = h * NS + s
                b = it % 2
                po = 7
                k3 = idx % 3
                S.op("pe", "matmul", self.ps[po][0:65, :], va[hb][:, j, :], pT[k3][:], start=(j == 0), stop=(j == nj - 1),
                     R=[f"va{hb}", f"va1{hb}", f"pT{k3}"], W=[("ps", po)], inc=(j == nj - 1))
                if j == nj - 1:
                    S.op("act", "copy", poc[b][:], self.ps[po][0:65, :], R=[("ps", po)], W=[f"poc{b}"])
                    S.op("dve", "reciprocal", rec[b][64:65, :], poc[b][64:65, :], R=[f"poc{b}"], W=[f"rec{b}"])
                    pending.append([4, (h, s, b, po)])

            def finalize(h, s, b, po):
                S.op("pe", "matmul", self.ps[4 + b][0:64, :], self.ones_f[64:65, 0:64], rec[b][64:65, :], start=True, stop=True, R=[f"rec{b}"], W=[("ps", 4 + b)])
                S.op("act", "copy", bc[b][:], self.ps[4 + b][0:64, :], R=[("ps", 4 + b)], W=[f"bc{b}"])
                S.op("dve", "tensor_tensor", yo[b][:], poc[b][0:64, :], bc[b][:], ALU.mult, R=[f"poc{b}", f"bc{b}"], W=[f"yo{b}"])
                S.dma("pool", yT[256 + h * 64:256 + (h + 1) * 64, s * 512:(s + 1) * 512], yo[b][:], R=[f"yo{b}"], W=[("yT", h, s)])

            pending = []
            LA = 2

            def gen():
                for idx in range(len(items) + LA):
                    if idx < len(items):
                        stage_a(idx)
                    if idx - LA >= 0:
                        stage_b(idx - LA)
                    for pnd in list(pending):
                        pnd[0] -= 1
                        if pnd[0] <= 0:
                            finalize(*pnd[1])
                            pending.remove(pnd)
                    yield
                for pnd in pending:
                    finalize(*pnd[1])
                yield
            return gen()

    def mix_retmoba(self, l, tag, qTr, kTr, rgT, rvd, mqT, mkT, mvd, mmask, yT):
        with contextlib.ExitStack() as st:
            g1 = self.gen_ret(st, l, tag, qTr, kTr, rgT, rvd, yT)
            g2 = self.gen_moba(st, l, tag, mqT, mkT, mvd, mmask, yT)
            live = [g1, g2]
            while live:
                for g, reps in ((g2, 2), (g1, 1)):
                    if g not in live:
                        continue
                    for _ in range(reps):
                        try:
                            next(g)
                        except StopIteration:
                            live.remove(g)
                            break

    def mix_gdn(self, l, tag, gba, gqT, gkT, gvT, gzd, yT):
        nc, S = self.nc, self.S
        I = self.I
        with contextlib.ExitStack() as st:
            sb = lambda n, shp, dt: st.enter_context(nc.sbuf_tensor(f"GD{n}_{tag}", shp, dt))
            qA = sb("qA", [64, 4, S_LEN], BF16); kA = sb("kA", [64, 4, S_LEN], BF16); vA = sb("vA", [64, 4, S_LEN], BF16)
            for (dst, src, key) in ((qA, gqT, "qA"), (kA, gkT, "kA"), (vA, gvT, "vA")):
                for hp in range(2):
                    S.dma("sp", dst[:, 2 * hp:2 * hp + 2, :], dap(src, src.offset + hp * 128 * S_LEN, [[S_LEN, 64], [64 * S_LEN, 2], [1, S_LEN]]), R=[key], W=[key])
            tri = sb("tri", [128, 3, 128], F32)
            tr = self.C["tri"]
            S.dma("sp", tri[:], dap(tr, tr.offset, [[128, 128], [128 * 128, 3], [1, 128]]), W=["tri"])
            U = tri[:, 0, :]; SM = tri[:, 1, :]; LM = tri[:, 2, :]
            onec = sb("onec", [128, 1], F32)
            S.op("dve", "memset", onec[:], 1.0, W=["onec"])
            alog = sb("alog", [128, 4], F32); dtb = sb("dtb", [128, 4], F32)
            S.dma("sp", alog[:], dap(I["gdn_a_log"][l], I["gdn_a_log"][l].offset, [[0, 128], [1, 4]]), W=["alog"])
            S.dma("sp", dtb[:], dap(I["gdn_dt_bias"][l], I["gdn_dt_bias"][l].offset, [[0, 128], [1, 4]]), W=["dtb"])
            gn = sb("gn", [128, 64], F32)
            S.dma("sp", gn[:], dap(I["gdn_norm"][l], I["gdn_norm"][l].offset, [[0, 128], [1, 64]]), W=["gn"])
            A3 = lambda nme: sb(nme, [128, NT, 4], F32)
            beta = A3("beta"); z = A3("z"); az = A3("az"); ld = A3("ld"); gcs = A3("gcs"); gts = A3("gts")
            egc = A3("egc"); ekd = A3("ekd"); egl = A3("egl"); bneg = A3("bneg"); begc = A3("begc")
            gkeys = [("gba", t) for t in range(NT)]
            S.op("act", "activation", beta[:], gba[:, :, 0:4], AF.Sigmoid, R=gkeys, W=["beta"])
            S.op("dve", "tensor_tensor", z[:], gba[:, :, 4:8], dtb[:].unsqueeze(1).to_broadcast([128, NT, 4]), ALU.add, R=gkeys + ["dtb"], W=["z"])
            S.op("act", "activation", az[:], z[:], AF.Abs, R=["z"], W=["az"])
            S.op("act", "activation", az[:], az[:], AF.Exp, scale=-1.0, R=["az"], W=["az"])
            S.op("act", "activation", az[:], az[:], AF.Ln, bias=onec[:], scale=1.0, R=["az", "onec"], W=["az"])
            S.op("dve", "scalar_tensor_tensor", z[:], z[:], 0.0, az[:], ALU.max, ALU.add, R=["z", "az"], W=["z"])
            S.op("act", "activation", alog[:], alog[:], AF.Exp, R=["alog"], W=["alog"])
            S.op("dve", "tensor_scalar", alog[:], alog[:], -1.0, None, ALU.mult, R=["alog"], W=["alog"])
            S.op("dve", "tensor_tensor", ld[:], z[:], alog[:].unsqueeze(1).to_broadcast([128, NT, 4]), ALU.mult, R=["z", "alog"], W=["ld"])
            ldf = ld[:].rearrange("p a b -> p (a b)")
            S.op("pe", "matmul", self.ps[0][:, 0:128], U, ldf, start=True, stop=True, R=["tri", "ld"], W=[("ps", 0)])
            S.op("pe", "matmul", self.ps[1][:, 0:128], self.ones_f[:], ldf, start=True, stop=True, R=["ld"], W=[("ps", 1)])
            S.op("act", "copy", gcs[:].rearrange("p a b -> p (a b)"), self.ps[0][:, 0:128], R=[("ps", 0)], W=["gcs"])
            S.op("act", "copy", gts[:].rearrange("p a b -> p (a b)"), self.ps[1][:, 0:128], R=[("ps", 1)], W=["gts"])
            S.op("act", "activation", egc[:], gcs[:], AF.Exp, R=["gcs"], W=["egc"])
            S.op("act", "activation", egl[:], gts[:], AF.Exp, R=["gts"], W=["egl"])
            S.op("dve", "tensor_tensor", ekd[:], gts[:], gcs[:], ALU.subtract, R=["gts", "gcs"], W=["ekd"])
            S.op("act", "activation", ekd[:], ekd[:], AF.Exp, R=["ekd"], W=["ekd"])
            S.op("dve", "tensor_scalar", bneg[:], beta[:], -1.0, None, ALU.mult, R=["beta"], W=["bneg"])
            S.op("dve", "tensor_tensor", begc[:], beta[:], egc[:], ALU.mult, R=["beta", "egc"], W=["begc"])
            kdec = [sb(f"kdec{i}", [128, 4, 64], BF16) for i in range(2)]
            VB = [sb(f"VB{i}", [128, 4, 64], F32) for i in range(2)]
            qkT = [sb(f"qkT{i}", [128, 4, 128], BF16) for i in range(2)]
            Nfin = [sb(f"Nfin{i}", [128, 4, 128], F32) for i in range(2)]
            oo = [sb(f"oo{i}", [128, 4, 64], F32) for i in range(2)]
            kf = sb("kf", [64, 4, 128], F32); qf = sb("qf", [64, 4, 128], F32); Rr = sb("Rr", [128, 4, 64], F32)
            gU = sb("gU", [128, 4, 128], F32); Dm = sb("Dm", [128, 4, 128], F32)
            ES = sb("ES", [128, 4, 128], F32); EL = sb("EL", [128, 4, 128], F32); T1 = sb("T1", [128, 4, 128], F32)
            Bt = [sb(f"Bt{i}", [128, 4, 128], F32) for i in range(2)]
            Bk = [sb(f"Bk{i}", [128, 4, 128], F32) for i in range(2)]
            M2 = sb("M2", [128, 4, 128], BF16)
            Btb = [sb(f"Btb{i}", [128, 4, 128], BF16) for i in range(2)]
            Bkb = [sb(f"Bkb{i}", [128, 4, 128], BF16) for i in range(2)]
            Nb = sb("Nb", [128, 4, 128], BF16)
            Nn = [sb(f"Nn{i}", [128, 4, 128], F32) for i in range(2)]
            vn = sb("vn", [128, 4, 64], BF16); o1 = sb("o1", [128, 4, 64], F32); o2 = sb("o2", [128, 4, 64], F32)
            Sf = sb("Sf", [64, 4, 64], F32); St = sb("St", [64, 4, 64], F32)
            osq = sb("osq", [128, 4, 64], F32); oss = sb("oss", [128, 4], F32)
            gzt = [sb(f"gzt{i}", [128, 256], F32) for i in range(4)]
            ytm = sb("ytm", [128, 256], BF16); yst = [sb(f"yst{i}", [128, 2, 128], BF16) for i in range(2)]
            S.op("dve", "memset", Sf[:], 0.0, W=["Sf"])
            v3 = lambda ap, a: ap.rearrange("p (a b) -> p a b", a=a)
            bc3 = lambda ap2, n: ap2.unsqueeze(2).to_broadcast([128, 4, n])
            fl = lambda t: t[:].rearrange("p a b -> p (a b)")

            def prep(n):
                ts_ = slice(n * 128, (n + 1) * 128)
                p = n % 2
                S.dma("sp", gzt[n % 4][:], gzd[ts_, :], W=[f"gzt{n % 4}"])
                for h in range(4):
                    S.op("pe", "transpose", self.psb[0][:, h * 64:(h + 1) * 64], kA[:, h, ts_], self.ident_b[0:64, 0:64], R=["kA"], W=[("ps", 6)], inc=False)
                for h in range(4):
                    S.op("pe", "transpose", self.psb[0][:, 256 + h * 64:256 + (h + 1) * 64], vA[:, h, ts_], self.ident_b[0:64, 0:64], R=["vA"], W=[("ps", 6)], inc=(h == 3))
                ktm = v3(self.psb[0][:, 0:256], 4); vtm = v3(self.psb[0][:, 256:512], 4)
                S.op("dve", "tensor_tensor", kdec[p][:], ktm, bc3(ekd[:, n, :], 64), ALU.mult, R=[("ps", 6), "ekd"], W=[f"kdec{p}"])
                S.op("dve", "tensor_tensor", VB[p][:], vtm, bc3(beta[:, n, :], 64), ALU.mult, R=[("ps", 6), "beta"], W=[f"VB{p}"])
                yield
                for h in range(4):
                    S.op("pe", "matmul", self.ps[0][:, h * 128:(h + 1) * 128], kA[:, h, ts_], kA[:, h, ts_], start=True, stop=True, R=["kA"], W=[("ps", 0)], inc=(h == 3))
                for h in range(4):
                    S.op("pe", "matmul", self.ps[1][:, h * 128:(h + 1) * 128], qA[:, h, ts_], kA[:, h, ts_], start=True, stop=True, R=["kA", "qA"], W=[("ps", 1)], inc=(h == 3))
                S.op("pool", "tensor_tensor", gU[:], U.unsqueeze(1).to_broadcast([128, 4, 128]), bc3(ld[:, n, :], 128), ALU.mult, R=["tri", "ld"], W=["gU"])
                yield
                S.op("pe", "matmul", self.ps[2][:, :], self.ones_f[:], fl(gU), start=True, stop=True, R=["gU"], W=[("ps", 2)])
                S.op("dve", "scalar_tensor_tensor", Dm[:], v3(self.ps[2][:, :], 4), -1.0, bc3(gcs[:, n, :], 128), ALU.mult, ALU.add,
                     R=[("ps", 2), "gcs"], W=["Dm"])
                yield
                S.op("dve", "tensor_scalar", Dm[:], Dm[:], 0.0, None, ALU.min, R=["Dm"], W=["Dm"])
                S.op("act", "activation", Dm[:], Dm[:], AF.Exp, R=["Dm"], W=["Dm"])
                yield
                S.op("pool", "tensor_tensor", ES[:], Dm[:], SM.unsqueeze(1).to_broadcast([128, 4, 128]), ALU.mult, R=["Dm", "tri"], W=["ES"])
                S.op("pool", "tensor_tensor", EL[:], Dm[:], LM.unsqueeze(1).to_broadcast([128, 4, 128]), ALU.mult, R=["Dm", "tri"], W=["EL"])
                S.op("dve", "tensor_tensor", T1[:], v3(self.ps[0][:, :], 4), bc3(bneg[:, n, :], 128), ALU.mult, R=[("ps", 0), "bneg"], W=["T1"])
                yield
                S.op("dve", "tensor_tensor", Bt[0][:], T1[:], ES[:], ALU.mult, R=["T1", "ES"], W=["Bt0"])
                S.op("dve", "tensor_tensor", M2[:], v3(self.ps[1][:, :], 4), EL[:], ALU.mult, R=[("ps", 1), "EL"], W=["M2"])
                yield
                for h in range(4):
                    S.op("pe", "transpose", self.ps[3][:, h * 128:(h + 1) * 128], Bt[0][:, h, :], self.ident_f[:], R=["Bt0"], W=[("ps", 3)], inc=(h == 3))
                for h in range(4):
                    S.op("pe", "transpose", self.psb[1][:, 512 + h * 128:512 + (h + 1) * 128], M2[:, h, :], self.ident_b[:], R=["M2"], W=[("ps", 7)], inc=(h == 3))
                S.op("act", "copy", fl(Bk[0]), self.ps[3][:, :], R=[("ps", 3)], W=["Bk0"])
                S.op("act", "copy", fl(qkT[p]), self.psb[1][:, 512:1024], R=[("ps", 7)], W=[f"qkT{p}"])
                yield
                S.op("dve", "tensor_tensor", Nn[0][:], Bk[0][:], self.ident_f[:].unsqueeze(1).to_broadcast([128, 4, 128]), ALU.add, R=["Bk0"], W=["Nn0"])
                cur = 0
                LB = self.gdn_lb
                for lev in range(1, 7):
                    nx = 1 - cur
                    pN, pNk = Nn[(lev - 1) % 2], f"Nn{(lev - 1) % 2}"
                    if lev < 6:
                        cN, cNk = Nn[lev % 2], f"Nn{lev % 2}"
                    else:
                        cN, cNk = Nfin[p], f"Nfin{p}"
                    lo = lev >= LB
                    nlo = (lev + 1) >= LB
                    sBk, sBt = (Bkb[cur], Btb[cur]) if lo else (Bk[cur], Bt[cur])
                    kk_, kt_ = (f"Bkb{cur}", f"Btb{cur}") if lo else (f"Bk{cur}", f"Bt{cur}")
                    for h in range(4):
                        S.op("pe", "matmul", self.ps[1][:, h * 128:(h + 1) * 128], sBk[:, h, :], sBt[:, h, :], start=True, stop=True,
                             R=[kt_, kk_], W=[("ps", 1)], inc=(h == 3))
                    if lev < 6:
                        for h in range(4):
                            S.op("pe", "matmul", self.ps[0][:, h * 128:(h + 1) * 128], sBt[:, h, :], sBk[:, h, :], start=True, stop=True,
                                 R=[kt_, kk_], W=[("ps", 0)], inc=(h == 3))
                    if lo:
                        S.op("dve", "tensor_copy", fl(Btb[nx]), self.ps[1][:, :], R=[("ps", 1)], W=[f"Btb{nx}"])
                    else:
                        S.op("dve", "tensor_copy", fl(Bt[nx]), self.ps[1][:, :], R=[("ps", 1)], W=[f"Bt{nx}"])
                        if nlo and lev < 6:
                            S.op("pool", "tensor_copy", fl(Btb[nx]), fl(Bt[nx]), R=[f"Bt{nx}"], W=[f"Btb{nx}"])
                    if lev < 6:
                        if nlo:
                            S.op("act", "copy", fl(Bkb[nx]), self.ps[0][:, :], R=[("ps", 0)], W=[f"Bkb{nx}"])
                        else:
                            S.op("act", "copy", fl(Bk[nx]), self.ps[0][:, :], R=[("ps", 0)], W=[f"Bk{nx}"])
                    if lo:
                        S.op("pool", "tensor_copy", fl(Nb), fl(pN), R=[pNk], W=["Nb"])
                    yield
                    for h in range(4):
                        if lo:
                            S.op("pe", "matmul", self.ps[2][:, h * 128:(h + 1) * 128], Btb[nx][:, h, :], Nb[:, h, :], start=True, stop=True,
                                 R=["Nb", f"Btb{nx}"], W=[("ps", 2)], inc=(h == 3))
                        else:
                            S.op("pe", "matmul", self.ps[2][:, h * 128:(h + 1) * 128], Bt[nx][:, h, :], pN[:, h, :], start=True, stop=True,
                                 R=[pNk, f"Bt{nx}"], W=[("ps", 2)], inc=(h == 3))
                    S.op("dve", "tensor_tensor", fl(cN), self.ps[2][:, :], fl(pN), ALU.add, R=[("ps", 2), pNk], W=[cNk])
                    yield
                    cur = nx

            def rec(n):
                ts_ = slice(n * 128, (n + 1) * 128)
                p = n % 2
                S.op("act", "copy", kf[:], kA[:, :, ts_], R=["kA"], W=["kf"])
                S.op("pool", "tensor_copy", qf[:], qA[:, :, ts_], R=["qA"], W=["qf"])
                for h in range(4):
                    S.op("pe", "matmul", self.ps[5][:, h * 64:(h + 1) * 64], kf[:, h, :], Sf[:, h, :], start=True, stop=True, R=["kf", "Sf"], W=[("ps", 5)], inc=(h == 3))
                for h in range(4):
                    S.op("pe", "matmul", self.ps[4][:, h * 64:(h + 1) * 64], qf[:, h, :], Sf[:, h, :], start=True, stop=True, R=["qf", "Sf"], W=[("ps", 4)], inc=(h == 3))
                yield
                S.op("dve", "tensor_tensor", o1[:], v3(self.ps[5][:, 0:256], 4), bc3(begc[:, n, :], 64), ALU.mult, R=[("ps", 5), "begc"], W=["o1"])
                S.op("dve", "tensor_tensor", Rr[:], VB[p][:], o1[:], ALU.subtract, R=[f"VB{p}", "o1"], W=["Rr"])
                yield
                for h in range(4):
                    S.op("pe", "matmul", self.ps[5][:, h * 64:(h + 1) * 64], Nfin[p][:, h, :], Rr[:, h, :], start=True, stop=True, R=[f"Nfin{p}", "Rr"], W=[("ps", 5)], inc=(h == 3))
                S.op("act", "copy", fl(vn), self.ps[5][:, 0:256], R=[("ps", 5)], W=["vn"])
                yield
                for h in range(4):
                    S.op("pe", "matmul", self.ps[5][0:64, 256 + h * 64:256 + (h + 1) * 64], kdec[p][:, h, :], vn[:, h, :], start=True, stop=True, R=[f"kdec{p}", "vn"], W=[("ps", 5)], inc=(h == 3))
                for h in range(4):
                    S.op("pe", "matmul", self.ps[4][:, 256 + h * 64:256 + (h + 1) * 64], qkT[p][:, h, :], vn[:, h, :], start=True, stop=True, R=[f"qkT{p}", "vn"], W=[("ps", 4)], inc=(h == 3))
                S.op("pool", "tensor_tensor", St[:], Sf[:], egl[0:64, n, :].unsqueeze(2).to_broadcast([64, 4, 64]), ALU.mult, R=["Sf", "egl"], W=["St"])
                yield
                S.op("dve", "tensor_tensor", Sf[:], St[:], v3(self.ps[5][0:64, 256:512], 4), ALU.add, R=["St", ("ps", 5)], W=["Sf"])
                S.op("dve", "tensor_tensor", o2[:], v3(self.ps[4][:, 0:256], 4), bc3(egc[:, n, :], 64), ALU.mult, R=[("ps", 4), "egc"], W=["o2"])
                S.op("dve", "tensor_tensor", oo[p][:], o2[:], v3(self.ps[4][:, 256:512], 4), ALU.add, R=["o2", ("ps", 4)], W=[f"oo{p}"])
                yield

            def post(n):
                p = n % 2
                S.op("pool", "tensor_tensor", osq[:], oo[p][:], oo[p][:], ALU.mult, R=[f"oo{p}"], W=["osq"])
                yield
                S.op("dve", "tensor_reduce", oss[:], osq[:], AX.X, ALU.add, R=["osq"], W=["oss"])
                S.op("act", "activation", oss[:], oss[:], AF.Sqrt, bias=self.epsc[:], scale=1.0 / 64, R=["oss"], W=["oss"])
                yield
                S.op("dve", "reciprocal", oss[:], oss[:], R=["oss"], W=["oss"])
                S.op("dve", "tensor_tensor", osq[:], oo[p][:], bc3(oss[:], 64), ALU.mult, R=[f"oo{p}", "oss"], W=["osq"])
                yield
                S.op("pool", "tensor_tensor", osq[:], osq[:], gn[:].unsqueeze(1).to_broadcast([128, 4, 64]), ALU.mult, R=["osq", "gn"], W=["osq"])
                yield
                S.op("dve", "tensor_tensor", ytm[:], fl(osq), gzt[n % 4][:], ALU.mult, R=["osq", f"gzt{n % 4}"], W=["ytm"])
                for cp in range(2):
                    S.op("pe", "transpose", self.psb[0][:, 512 + cp * 128:512 + (cp + 1) * 128], ytm[:, cp * 128:(cp + 1) * 128], self.ident_b[:], R=["ytm"], W=[("ps", 6)], inc=(cp == 1))
                yield
                S.op("act", "copy", fl(yst[p]), self.psb[0][:, 512:768], R=[("ps", 6)], W=[f"yst{p}"])
                S.dma("pool", dap(yT, yT.offset + 768 * S_LEN + n * 128, [[S_LEN, 128], [128 * S_LEN, 2], [1, 128]]), yst[p][:], R=[f"yst{p}"], W=[("yTg", n)])
                yield

            def interleave(gens):
                gens = [g for g in gens if g is not None]
                while gens:
                    for g in list(gens):
                        try:
                            next(g)
                        except StopIteration:
                            gens.remove(g)

            for n in range(NT + 2):
                interleave([rec(n - 1) if 0 <= n - 1 < NT else None,
                            post(n - 2) if 0 <= n - 2 < NT else None,
                            prep(n) if n < NT else None])

    def sub_ffn(self, l, X, Xout):
        nc, S = self.nc, self.S
        tag = f"f{l}"
        actT = self.scratch(f"actT{l}", [DFF, S_LEN], BF16)
        with contextlib.ExitStack() as st:
            hT = st.enter_context(nc.sbuf_tensor(f"hT_{tag}", [128, 8, S_LEN], BF16))
            with contextlib.ExitStack() as st2:
                self.norm_T(st2, X, S_LEN, self.I["norm_ffn"][l], hT, "hT", tag)
            S.barrier()
            with contextlib.ExitStack() as st2:
                sb = lambda n, s, d: st2.enter_context(nc.sbuf_tensor(f"{n}_{tag}", s, d))
                cw = sb("cw", [128, 3, 2 * NFC], F32)
                cb = sb("cb", [128, 2 * NFC], F32)
                fc = self.I["ffn_conv"][l]
                for kk in range(3):
                    S.dma("sp", cw[:, kk, :], dap(fc, fc.offset + kk * 2 * DFF, [[1, 128], [128, 2 * NFC]]), R=["cw"], W=["cw"], allow_slow_non_contiguous=True)
                fb = self.I["ffn_conv_b"][l]
                S.dma("sp", cb[:], dap(fb, fb.offset, [[1, 128], [128, 2 * NFC]]), W=["cb"], allow_slow_non_contiguous=True)
                wt = [sb(f"wu{i}", [128, 8, 256], BF16) for i in range(2)]
                pre = [sb(f"pre{i}", [128, 2 + S_LEN], F32) for i in range(2)]
                uu = [sb(f"uu{i}", [128, S_LEN], F32) for i in range(2)]
                acto = sb("acto", [128, S_LEN], BF16)
                for i in range(2):
                    S.op("pool", "memset", pre[i][:, 0:2], 0.0, W=[f"pre{i}"])
                Wu = self.I["ffn_up"][l]
                for c in range(NFC):
                    b = c % 2
                    self.S.dma("pool", wt[b][:, :, 0:128], dap(Wu, Wu.offset + c * 128, [[2 * DFF, 128], [128 * 2 * DFF, 8], [1, 128]]), W=[f"wu{b}g"])
                    self.S.dma("pool", wt[b][:, :, 128:256], dap(Wu, Wu.offset + DFF + c * 128, [[2 * DFF, 128], [128 * 2 * DFF, 8], [1, 128]]), W=[f"wu{b}v"])
                    for gv in range(2):
                        wk = f"wu{b}g" if gv == 0 else f"wu{b}v"
                        for s in range(NS):
                            pi = self.nps()
                            for k in range(8):
                                S.op("pe", "matmul", self.ps[pi][:, :], wt[b][:, k, gv * 128:(gv + 1) * 128], hT[:, k, s * 512:(s + 1) * 512],
                                     start=(k == 0), stop=(k == 7), R=[wk] + [("hT", 4 * s + i) for i in range(4)], W=[("ps", pi)], inc=(k == 7))
                            S.op("act", "copy", pre[gv][:, 2 + s * 512:2 + (s + 1) * 512], self.ps[pi][:, :], R=[("ps", pi)], W=[f"pre{gv}"])
                        ci = gv * NFC + c
                        S.op("act", "activation", uu[gv][:], pre[gv][:, 2:2 + S_LEN], AF.Identity, bias=cb[:, ci:ci + 1], scale=cw[:, 2, ci:ci + 1],
                             R=[f"pre{gv}", "cw", "cb"], W=[f"uu{gv}"])
                        eng = "dve"
                        S.op(eng, "scalar_tensor_tensor", uu[gv][:], pre[gv][:, 1:1 + S_LEN], cw[:, 1, ci:ci + 1], uu[gv][:], ALU.mult, ALU.add,
                             R=[f"pre{gv}", "cw", f"uu{gv}"], W=[f"uu{gv}"])
                        S.op(eng, "scalar_tensor_tensor", uu[gv][:], pre[gv][:, 0:S_LEN], cw[:, 0, ci:ci + 1], uu[gv][:], ALU.mult, ALU.add,
                             R=[f"pre{gv}", "cw", f"uu{gv}"], W=[f"uu{gv}"])
                    S.op("act", "activation", uu[0][:], uu[0][:], AF.Silu, R=["uu0"], W=["uu0"])
                    S.op("dve", "tensor_tensor", acto[:], uu[0][:], uu[1][:], ALU.mult, R=["uu0", "uu1"], W=["acto"])
                    S.dma("sp", actT[c * 128:(c + 1) * 128, :], acto[:], R=["acto"], W=[("actT", c)])
            S.barrier()
        with contextlib.ExitStack() as st:
            self.out_proj_residual(st, None, NFC, self.I["ffn_down"][l], X, Xout, tag, None, src=actT)


def build_program(debug=False, stages=None, nlayers=2):
    b = Builder(debug=debug, stages=stages, nlayers=nlayers)
    nc = b.build()
    return nc, b


_CACHE = {}


def kernel(**inputs):
    if "nc" not in _CACHE:
        _CACHE["nc"], _ = build_program()
        _CACHE["consts"] = make_consts()
    nc = _CACHE["nc"]
    consts = _CACHE["consts"]
    in_maps = []
    for b in range(8):
        m = {}
        for k in W_SHAPES:
            a = np.asarray(inputs[k], dtype=np.float32)
            if k in ("x", "mem"):
                a = a[b]
            m[k] = np.ascontiguousarray(a)
        for k in CONST_SHAPES:
            m["c_" + k] = np.ascontiguousarray(consts[k].astype(np.float32))
        in_maps.append(m)
    res = run_bass_kernel_spmd(nc, in_maps, core_ids=list(range(8)))
    return np.stack([np.asarray(r["y"], dtype=np.float32) for r in res.results], 0)
```
